# Optimizing a Trainium2 kernel written in Bass

```python
import math
import jax, jax.numpy as jnp
from jax import lax
import numpy as np

D_MODEL = 1024
BATCH = 4
SEQ = 4096
DEPTH = 4

GRID_W = 64
CTX_LEN = 256
N_MIXERS = 2
N_CONV_LAYERS = (DEPTH + N_MIXERS - 1) // N_MIXERS
N_DELTA_LAYERS = DEPTH // N_MIXERS
CONV_WIDTH = 31
DN_HEADS = 8
DN_HEAD_DIM = D_MODEL // DN_HEADS
DN_SHORT_CONV = 5
DN_CHUNK = 64
N_DIRS = 2
DN_QKV = 3 * D_MODEL
DN_IN = 4 * D_MODEL + 2 * N_DIRS * DN_HEADS
FFN_HIDDEN = 4 * D_MODEL
EPS = 1e-6

kernel_name = "hybrid_conformer_gdn_dit_block"


def rms_norm(x, g):
    xf = x.astype(jnp.float32)
    y = xf * lax.rsqrt(jnp.mean(xf * xf, axis=-1, keepdims=True) + EPS)
    return (y * g.astype(jnp.float32)).astype(x.dtype)


def layer_norm(x, g, b):
    xf = x.astype(jnp.float32)
    mu = jnp.mean(xf, axis=-1, keepdims=True)
    xc = xf - mu
    y = xc * lax.rsqrt(jnp.mean(xc * xc, axis=-1, keepdims=True) + EPS)
    return (y * g.astype(jnp.float32) + b.astype(jnp.float32)).astype(x.dtype)


def l2_norm(x):
    xf = x.astype(jnp.float32)
    return xf * lax.rsqrt(jnp.sum(xf * xf, axis=-1, keepdims=True) + EPS)


def modulate(h, shift, scale):
    return h * (1.0 + scale) + shift


def dwconv(x, w):
    K, C = w.shape
    return lax.conv_general_dilated(
        x, w[:, None, :].astype(x.dtype), window_strides=(1,),
        padding=[(K // 2, K // 2)], dimension_numbers=('NWC', 'WIO', 'NWC'),
        feature_group_count=C)


def conformer_conv(h, n_seg, w_pw1, b_pw1, w_dw, b_dw, ln_g, ln_b, w_pw2, b_pw2):
    B, L, D = h.shape
    u = h @ w_pw1 + b_pw1
    a, gate = jnp.split(u, 2, axis=-1)
    u = a * jax.nn.sigmoid(gate)
    u = dwconv(u.reshape(B * n_seg, L // n_seg, D), w_dw).reshape(B, L, D) + b_dw
    u = jax.nn.silu(layer_norm(u, ln_g, ln_b))
    return u @ w_pw2 + b_pw2


def gated_delta_chunked(q, k, v, g, beta, state):
    B, H, L, DK = q.shape
    DV = v.shape[-1]
    C = DN_CHUNK
    NC = L // C
    q = q.reshape(B, H, NC, C, DK)
    k = k.reshape(B, H, NC, C, DK)
    v = v.reshape(B, H, NC, C, DV)
    g = g.reshape(B, H, NC, C)
    beta = beta.reshape(B, H, NC, C)
    G = jnp.cumsum(g, axis=-1)
    lower = jnp.tril(jnp.ones((C, C), dtype=bool))
    strict = jnp.tril(jnp.ones((C, C), dtype=bool), -1)
    decay = jnp.exp(jnp.where(lower, G[..., :, None] - G[..., None, :], -jnp.inf))
    kk = jnp.einsum('bhncd,bhnsd->bhncs', k, k)
    a_mat = jnp.where(strict, beta[..., :, None] * kk * decay, 0.0)
    eye = jnp.broadcast_to(jnp.eye(C, dtype=q.dtype), a_mat.shape)
    rhs = jnp.concatenate([beta[..., None] * v, (beta * jnp.exp(G))[..., None] * k], axis=-1)
    sol = lax.linalg.triangular_solve(a_mat + eye, rhs, left_side=True, lower=True,
                                      unit_diagonal=True)
    u, w = sol[..., :DV], sol[..., DV:]
    qk = jnp.einsum('bhncd,bhnsd->bhncs', q, k) * decay
    qg = q * jnp.exp(G)[..., None]
    kd = k * jnp.exp(G[..., -1:] - G)[..., None]
    chunk_decay = jnp.exp(G[..., -1])
    xs = tuple(jnp.moveaxis(t, 2, 0) for t in (u, w, qg, qk, kd, chunk_decay))

    def step(S, inp):
        u_c, w_c, qg_c, qk_c, kd_c, cd_c = inp
        v_new = u_c - jnp.einsum('bhcd,bhde->bhce', w_c, S)
        o_c = (jnp.einsum('bhcd,bhde->bhce', qg_c, S)
               + jnp.einsum('bhcs,bhse->bhce', qk_c, v_new))
        S = S * cd_c[..., None, None] + jnp.einsum('bhcd,bhce->bhde', kd_c, v_new)
        return S, o_c

    state, o = lax.scan(step, state, xs)
    o = jnp.moveaxis(o, 0, 2).reshape(B, H, L, DV)
    return o, state


def gdn_project(h, w_in, conv_w, a_log, dt_bias):
    B, L, _ = h.shape
    p = h @ w_in
    qkv = jax.nn.silu(dwconv(p[..., :DN_QKV], conv_w))
    z = p[..., DN_QKV:4 * D_MODEL].reshape(B, L, DN_HEADS, DN_HEAD_DIM)
    ab = p[..., 4 * D_MODEL:].astype(jnp.float32).reshape(B, L, 2, N_DIRS, DN_HEADS)
    to_heads = lambda t: t.reshape(B, L, DN_HEADS, DN_HEAD_DIM).transpose(0, 2, 1, 3)
    q, k, v = jnp.split(qkv, 3, axis=-1)
    q = l2_norm(to_heads(q)) * (DN_HEAD_DIM ** -0.5)
    k = l2_norm(to_heads(k))
    v = to_heads(v).astype(jnp.float32)
    g = -jnp.exp(a_log.astype(jnp.float32)) * jax.nn.softplus(ab[:, :, 0] + dt_bias.astype(jnp.float32))
    beta = jax.nn.sigmoid(ab[:, :, 1])
    return q, k, v, z, jnp.transpose(g, (2, 0, 3, 1)), jnp.transpose(beta, (2, 0, 3, 1))


def gdn_output(o, z, norm_g, w_o, dtype):
    B, H, L, DV = o.shape
    o = rms_norm(o.transpose(0, 2, 1, 3), norm_g) * jax.nn.silu(z.astype(jnp.float32))
    return o.reshape(B, L, H * DV).astype(dtype) @ w_o


def gated_deltanet_bidir(hc, hl, w_in, conv_w, a_log, dt_bias, norm_g, w_o):
    qc, kc, vc, zc, gc, bc = gdn_project(hc, w_in, conv_w, a_log, dt_bias)
    ql, kl, vl, zl, gl, bl = gdn_project(hl, w_in, conv_w, a_log, dt_bias)
    B = hl.shape[0]
    s0 = jnp.zeros((B, DN_HEADS, DN_HEAD_DIM, DN_HEAD_DIM), jnp.float32)
    fl = lambda t: jnp.flip(t, axis=2)
    oc_f, sc_f = gated_delta_chunked(qc, kc, vc, gc[0], bc[0], s0)
    ol_f, _ = gated_delta_chunked(ql, kl, vl, gl[0], bl[0], sc_f)
    oc_b, sc_b = gated_delta_chunked(fl(qc), fl(kc), fl(vc), fl(gc[1]), fl(bc[1]), s0)
    ol_b, _ = gated_delta_chunked(fl(ql), fl(kl), fl(vl), fl(gl[1]), fl(bl[1]), sc_b)
    yc = gdn_output(oc_f + fl(oc_b), zc, norm_g, w_o, hc.dtype)
    yl = gdn_output(ol_f + fl(ol_b), zl, norm_g, w_o, hl.dtype)
    return yc, yl


def sq_relu_mlp(h, w1, w2):
    return jnp.square(jax.nn.relu(h @ w1)) @ w2


def setup_inputs(seed: int = 0) -> dict:
    key = jax.random.key(seed)
    ks = jax.random.split(key, 26)
    D = D_MODEL
    NA, NB = N_CONV_LAYERS, N_DELTA_LAYERS
    nrm = lambda k, shape, s: jax.random.normal(k, shape, jnp.float32) * s
    dt = jnp.exp(jax.random.uniform(ks[20], (NB, N_DIRS, DN_HEADS), jnp.float32,
                                    minval=math.log(1e-3), maxval=math.log(1e-1)))
    return {
        "x": nrm(ks[0], (BATCH, SEQ, D), 1.0),
        "c": nrm(ks[1], (BATCH, D), 1.0),
        "ctx": nrm(ks[2], (BATCH, CTX_LEN, D), 1.0),
        "c_ctx": nrm(ks[3], (D,), 1.0),
        "w_mod": nrm(ks[4], (DEPTH, D, 6 * D), 0.5 * D ** -0.5),
        "b_mod": nrm(ks[5], (DEPTH, 6 * D), 0.02),
        "norm1_g": 1.0 + nrm(ks[6], (DEPTH, D), 0.02),
        "norm2_g": 1.0 + nrm(ks[7], (DEPTH, D), 0.02),
        "final_g": 1.0 + nrm(ks[8], (D,), 0.02),
        "conv_w_pw1": nrm(ks[9], (NA, D, 2 * D), D ** -0.5),
        "conv_b_pw1": nrm(ks[10], (NA, 2 * D), 0.02),
        "conv_w_dw": nrm(ks[11], (NA, CONV_WIDTH, D), CONV_WIDTH ** -0.5),
        "conv_b_dw": nrm(ks[12], (NA, D), 0.02),
        "conv_ln_g": 1.0 + nrm(ks[13], (NA, D), 0.02),
        "conv_ln_b": nrm(ks[14], (NA, D), 0.02),
        "conv_w_pw2": nrm(ks[15], (NA, D, D), D ** -0.5),
        "conv_b_pw2": nrm(ks[16], (NA, D), 0.02),
        "dn_w_in": nrm(ks[17], (NB, D, DN_IN), D ** -0.5),
        "dn_conv_w": nrm(ks[18], (NB, DN_SHORT_CONV, DN_QKV), DN_SHORT_CONV ** -0.5),
        "dn_a_log": jnp.log(jax.random.uniform(ks[19], (NB, N_DIRS, DN_HEADS), jnp.float32,
                                               minval=1.0, maxval=16.0)),
        "dn_dt_bias": dt + jnp.log(-jnp.expm1(-dt)),
        "dn_norm_g": 1.0 + nrm(ks[21], (NB, DN_HEAD_DIM), 0.02),
        "dn_w_o": nrm(ks[22], (NB, D, D), D ** -0.5),
        "mlp_w1": nrm(ks[23], (DEPTH, D, FFN_HIDDEN), D ** -0.5),
        "mlp_w2": nrm(ks[24], (DEPTH, FFN_HIDDEN, D), FFN_HIDDEN ** -0.5),
    }


def reference(x, c, ctx, c_ctx, w_mod, b_mod, norm1_g, norm2_g, final_g,
              conv_w_pw1, conv_b_pw1, conv_w_dw, conv_b_dw, conv_ln_g, conv_ln_b,
              conv_w_pw2, conv_b_pw2, dn_w_in, dn_conv_w, dn_a_log, dn_dt_bias,
              dn_norm_g, dn_w_o, mlp_w1, mlp_w2):
    rows = x.shape[1] // GRID_W
    xl, xc = x, ctx
    sc_l = jax.nn.silu(c)
    sc_c = jax.nn.silu(c_ctx)
    for i in range(DEPTH):
        last = i == DEPTH - 1
        j = i // N_MIXERS
        mod_l = (sc_l @ w_mod[i] + b_mod[i])[:, None, :]
        mod_c = (sc_c @ w_mod[i] + b_mod[i])[None, None, :]
        sh1_l, s1_l, g1_l, sh2_l, s2_l, g2_l = jnp.split(mod_l, 6, axis=-1)
        sh1_c, s1_c, g1_c, sh2_c, s2_c, g2_c = jnp.split(mod_c, 6, axis=-1)
        hl = modulate(rms_norm(xl, norm1_g[i]), sh1_l, s1_l)
        hc = modulate(rms_norm(xc, norm1_g[i]), sh1_c, s1_c)
        if i % N_MIXERS == 0:
            conv_p = (conv_w_pw1[j], conv_b_pw1[j], conv_w_dw[j], conv_b_dw[j],
                      conv_ln_g[j], conv_ln_b[j], conv_w_pw2[j], conv_b_pw2[j])
            yl = conformer_conv(hl, rows, *conv_p)
            if not last:
                yc = conformer_conv(hc, 1, *conv_p)
        else:
            yc, yl = gated_deltanet_bidir(hc, hl, dn_w_in[j], dn_conv_w[j], dn_a_log[j],
                                          dn_dt_bias[j], dn_norm_g[j], dn_w_o[j])
        xl = xl + g1_l * yl
        xl = xl + g2_l * sq_relu_mlp(modulate(rms_norm(xl, norm2_g[i]), sh2_l, s2_l),
                                     mlp_w1[i], mlp_w2[i])
        if not last:
            xc = xc + g1_c * yc
            xc = xc + g2_c * sq_relu_mlp(modulate(rms_norm(xc, norm2_g[i]), sh2_c, s2_c),
                                         mlp_w1[i], mlp_w2[i])
    return rms_norm(xl, final_g)
```

```python
import numpy as np
import concourse.bass as bass
import concourse.mybir as mybir
from concourse.bass_utils import run_bass_kernel_spmd

F32 = mybir.dt.float32
BF16 = mybir.dt.bfloat16
F32R = mybir.dt.float32r
FP32R = False
ALU = mybir.AluOpType
AF = mybir.ActivationFunctionType

D = 1024
NCH = 8
NCTX = 256
NLAT = 2048
T = NCTX + NLAT
SEQ = 4096
DEPTH = 4
EPS = 1e-6
KW = 31
PADW = 15
NCORES = 8
DEBUG_TAGS = False
TAGMAP = {}


class Buf:
    __slots__ = ("name", "last_w", "readers")

    def __init__(self, name):
        self.name = name
        self.last_w = None
        self.readers = []


class Op:
    __slots__ = ("eng", "fn", "deps", "needs_inc", "idx", "dma_key", "dma_val", "tag")

    def __init__(self, eng, fn):
        self.eng = eng
        self.fn = fn
        self.deps = []
        self.needs_inc = False
        self.idx = 0
        self.dma_key = None
        self.dma_val = 0


class Prog:
    ENGS = ("pe", "act", "dve", "pool", "sp")

    def __init__(self):
        self.ops = {e: [] for e in self.ENGS}
        self.dma_count = {}
        self.dma_total_keys = set()
        self.last_dma = {}
        self.dma_inc = {}
        self.fence_deps = []

    def fence(self):
        f = []
        for e in self.ENGS:
            for o in reversed(self.ops[e]):
                if o.dma_key is None:
                    o.needs_inc = True
                    f.append(o)
                    break
        f.extend(self.last_dma.values())
        self.fence_deps = f

    def _add_dep(self, op, dep, war=False):
        if dep is None or dep is op:
            return
        if dep.dma_key is None and dep.eng == op.eng:
            if op.eng == "pe":
                return
        if dep.dma_key is None:
            dep.needs_inc = True
        op.deps.append(dep)

    def op(self, eng, fn, reads=(), writes=()):
        o = Op(eng, fn)
        if DEBUG_TAGS:
            import sys as _sys
            f = _sys._getframe(1)
            tg = []
            while f is not None and len(tg) < 4:
                tg.append(f.f_lineno)
                f = f.f_back
            o.tag = tg
        for dep in self.fence_deps:
            if dep.dma_key is not None or dep.eng != eng or eng != "pe":
                o.deps.append(dep)
        for b in reads:
            self._add_dep(o, b.last_w)
        for b in writes:
            self._add_dep(o, b.last_w)
            for r in b.readers:
                self._add_dep(o, r, war=True)
        for b in reads:
            b.readers.append(o)
        for b in writes:
            b.last_w = o
            b.readers = []
        self.ops[eng].append(o)
        return o

    def dma(self, queue, key, fn, reads=(), writes=(), wait_total=False, indep=False, inc=16):
        o = self.op(queue, fn, reads, writes)
        self.dma_inc[key] = inc
        if wait_total or indep:
            o.deps = [d for d in o.deps if d.dma_key != key]
        n = self.dma_count.get(key, 0) + 1
        self.dma_count[key] = n
        o.dma_key = key
        o.dma_val = inc * n
        self.last_dma[key] = o
        if wait_total:
            self.dma_total_keys.add(key)
        return o

    def emit(self, nc, block, sems, final_waits=()):
        for e in self.ENGS:
            c = 0
            for o in self.ops[e]:
                if o.dma_key is None and o.needs_inc:
                    c += 1
                    o.idx = c

        def token(dep):
            if dep.dma_key is not None:
                if dep.dma_key in self.dma_total_keys:
                    return dep.dma_key, self.dma_inc[dep.dma_key] * self.dma_count[dep.dma_key]
                return dep.dma_key, dep.dma_val
            return dep.eng, dep.idx

        def run(e, eng):
            known = {}
            for o in self.ops[e]:
                for dep in o.deps:
                    k, v = token(dep)
                    if known.get(k, 0) >= v:
                        continue
                    eng.wait_ge(sems[k], v)
                    known[k] = v
                ins = o.fn(eng)
                if DEBUG_TAGS:
                    try:
                        TAGMAP[ins.ins.name] = o.tag
                    except Exception:
                        pass
                if o.dma_key is not None:
                    ins.then_inc(sems[o.dma_key], self.dma_inc[o.dma_key])
                elif o.needs_inc:
                    ins.then_inc(sems[e], 1)
            for k in final_waits.get(e, ()) if isinstance(final_waits, dict) else ():
                eng.wait_ge(sems[k], 16 * self.dma_count[k])

        @block.tensor
        def _(eng):
            run("pe", eng)

        @block.scalar
        def _(eng):
            run("act", eng)

        @block.vector
        def _(eng):
            run("dve", eng)

        @block.gpsimd
        def _(eng):
            run("pool", eng)

        @block.sync
        def _(eng):
            run("sp", eng)


def fm(vec):
    v = np.asarray(vec, np.float32).reshape(-1, 128)
    return np.ascontiguousarray(v.T)


class ConstPack:
    def __init__(self):
        self.cols = {}
        self.n = 0
        self.parts = []

    def add(self, name, arr):
        arr = np.asarray(arr, np.float32)
        assert arr.shape[0] == 128
        arr = arr.reshape(128, -1)
        self.cols[name] = (self.n, arr.shape[1])
        self.n += arr.shape[1]
        self.parts.append(arr)

    def array(self):
        return np.ascontiguousarray(np.concatenate(self.parts, axis=1))


def pack_consts(inp, b, hf):
    cp = ConstPack()
    cc = np.stack([fm(inp["c"][b]), fm(inp["c_ctx"])], axis=2)
    cp.add("c", cc)
    for i in range(DEPTH):
        cp.add(f"bmod{i}", fm(inp["b_mod"][i]))
        cp.add(f"n1g{i}", fm(inp["norm1_g"][i]))
        cp.add(f"n2g{i}", fm(inp["norm2_g"][i]))
    cp.add("fg", fm(inp["final_g"]))
    for j in range(2):
        cp.add(f"b1{j}", fm(inp["conv_b_pw1"][j]))
        wdw = np.asarray(inp["conv_w_dw"][j], np.float32)
        if hf == 1:
            wdw = wdw[::-1]
        w = wdw.reshape(KW, NCH, 128).transpose(2, 1, 0)
        cp.add(f"wdw{j}", w)
        cp.add(f"bdw{j}", fm(inp["conv_b_dw"][j]))
        cp.add(f"lng{j}", fm(inp["conv_ln_g"][j]))
        cp.add(f"lnb{j}", fm(inp["conv_ln_b"][j]))
        cp.add(f"b2{j}", fm(inp["conv_b_pw2"][j]))
    dirmap = (hf, 1 - hf)
    for j in range(2):
        cw = np.asarray(inp["dn_conv_w"][j], np.float32)
        if hf == 1:
            cw = cw[::-1]
        w = cw.reshape(5, 24, 128).transpose(2, 1, 0)
        cp.add(f"cw{j}", w)
        cp.add(f"ng{j}", np.asarray(inp["dn_norm_g"][j], np.float32).reshape(128, 1))
        al = np.zeros((128, 1), np.float32)
        db = np.zeros((128, 1), np.float32)
        for d in range(2):
            al[32 * d:32 * d + 8, 0] = inp["dn_a_log"][j][dirmap[d]]
            db[32 * d:32 * d + 8, 0] = inp["dn_dt_bias"][j][dirmap[d]]
        cp.add(f"alog{j}", al)
        cp.add(f"dtb{j}", db)
    pm = np.zeros((128, 2), np.float32)
    pm[:, 1 - hf] = 1.0
    cp.add("pm", pm)
    return cp


def make_wab(inp, hf):
    dirmap = (hf, 1 - hf)
    out = np.zeros((2, D, 128), np.float32)
    for j in range(2):
        w = np.asarray(inp["dn_w_in"][j], np.float32)
        for d in range(2):
            out[j, :, 32 * d:32 * d + 8] = w[:, 4096 + dirmap[d] * 8:4096 + dirmap[d] * 8 + 8]
            out[j, :, 64 + 32 * d:64 + 32 * d + 8] = w[:, 4096 + 16 + dirmap[d] * 8:4096 + 16 + dirmap[d] * 8 + 8]
    return out


SLOT_ELEMS = 4096
ARENA_BYTES = 107 * 1024


class StopBuild(Exception):
    pass


class Builder:
    dbg_stop = None

    def __init__(self, nlayers, cols, ncols, ncores=NCORES):
        self.ncores = ncores
        self.nlayers = nlayers
        self.cols = cols
        self.ncols = ncols
        self.P = Prog()
        self.nc = bass.Bass("TRN2", target_bir_lowering=False)
        self.psum_rr = 0
        self.slot_rr = 0
        self.ar_off = 0

    def phase(self, nslots):
        self.P.fence()
        self.ar_off = 0
        self.SLOT = []
        self.SLOTB = []
        for i in range(nslots):
            v, b = self.aalloc(f"SLOT{i}", [SLOT_ELEMS], BF16)
            self.SLOT.append(v)
            self.SLOTB.append(b)
        self.slot_rr = 0

    def aalloc(self, name, shape, dtype):
        n = 1
        for d_ in shape:
            n *= d_
        esz = 4 if dtype == F32 else 2
        nbytes = (n * esz + 63) // 64 * 64
        off = self.ar_off
        self.ar_off += nbytes
        assert self.ar_off <= ARENA_BYTES, (name, self.ar_off)
        if not hasattr(self, "amap"):
            self.amap = {}
        self.amap[name] = (off, tuple(shape), esz)
        v = self.ARENA[:, off // 2: off // 2 + n * esz // 2]
        if dtype == F32:
            v = v.bitcast(F32)
        if len(shape) > 1:
            names = [f"d{k}" for k in range(len(shape))]
            kw = {nm: sz for nm, sz in zip(names[:-1], shape[:-1])}
            v = v.rearrange(f"p ({' '.join(names)}) -> p {' '.join(names)}", **kw)
        return v, Buf(name)

    def cst(self, name, c0=0, n=None):
        o, w = self.cols[name]
        if n is None:
            n = w - c0
        return self.CONST[:, o + c0:o + c0 + n]

    def psum(self):
        i = self.psum_rr % len(self.PS)
        self.psum_rr += 1
        return self.PS[i], self.PSB[i]

    def tmp(self):
        i = self.tmp_rr % len(self.TMP)
        self.tmp_rr += 1
        return self.TMP[i], self.TMPB[i]

    def load_slot(self, dram_ap):
        i = self.slot_rr % len(self.SLOT)
        self.slot_rr += 1
        st, sb = self.SLOT[i], self.SLOTB[i]
        shp = dram_ap.shape
        view = st[:, 0:shp[1] * shp[2]].rearrange("p (k n) -> p k n", k=shp[1])
        self.P.dma("pool", f"slot{i}", lambda e, o=view, a=dram_ap: e.dma_start(out=o, in_=a),
                   reads=(), writes=(sb,))
        return view, sb

    def mm(self, out, lhsT, rhs, start, stop, reads, writes):
        if FP32R and lhsT.dtype == F32:
            lhsT = lhsT.bitcast(F32R)
            rhs = rhs.bitcast(F32R)
        self.P.op("pe", lambda e: e.matmul(out, lhsT, rhs, start=start, stop=stop),
                  reads=reads, writes=writes)

    def act(self, out, in_, func, reads, writes, bias=None, scale=None):
        kw = {}
        if bias is not None:
            kw["bias"] = bias
        if scale is not None:
            kw["scale"] = scale
        self.P.op("act", lambda e: e.activation(out, in_, func, **kw), reads=reads, writes=writes)

    def dve(self, fn, reads, writes):
        self.P.op("dve", fn, reads=reads, writes=writes)

    def rstd_from(self, src, srcb, n, out, outb):
        self.act(self.LNT[:, 0:n], src, AF.Ln, reads=(srcb, self.EPSB), writes=(self.LNTB,),
                 bias=self.EPST[:, 0:1])
        self.act(out[:, 0:n], self.LNT[:, 0:n], AF.Exp, reads=(self.LNTB,), writes=(outb,), scale=-0.5)

    def norm_mod(self, t0, n, Acol, Bcol, out3, outb, dst_dram=None):
        X, XB = self.X, self.XB
        self.act(self.SQ[:, :, 0:n], X[:, :, t0:t0 + n], AF.Square, reads=(XB,), writes=(self.SQB,))
        ps, psb = self.psum()
        for c in range(NCH):
            self.mm(ps[:, 0:n], self.ONES[:, :], self.SQ[:, c, 0:n], c == 0, c == NCH - 1,
                    reads=(self.SQB, self.ONESB), writes=(psb,))
        self.rstd_from(ps[:, 0:n], psb, n, self.RSTD, self.RSTDB)
        for c in range(NCH):
            tm, tmb = self.tmp()
            self.dve(lambda e, c=c, tm=tm: e.scalar_tensor_tensor(
                tm[:, 0:n], X[:, c, t0:t0 + n], Acol(c), self.RSTD[:, 0:n], ALU.mult, ALU.mult),
                reads=(XB, self.RSTDB, self.MODB, self.CONSTB), writes=(tmb,))
            if dst_dram is not None:
                self.P.dma("sp", "out_" + tmb.name, lambda e, c=c, tm=tm: e.dma_start(out=dst_dram(c), in_=tm[:, 0:n]),
                           reads=(tmb,), writes=())
            else:
                self.act(out3[:, c, 0:n], tm[:, 0:n], AF.Identity, reads=(tmb, self.MODB),
                         writes=(outb,), bias=Bcol(c))

    def compute_mod(self, i, w_mod):
        wv = w_mod[i].rearrange("(k p) n -> p k n", p=128)
        ps, psb = self.psum()
        psv = ps[:, 0:96].rearrange("p (n s) -> p n s", s=2)
        for s in range(12):
            sv, sb = self.load_slot(wv[:, :, 512 * s:512 * (s + 1)])
            for q in range(4):
                n = 4 * s + q
                for k in range(NCH):
                    self.mm(psv[:, n, :], sv[:, k, 128 * q:128 * (q + 1)], self.SC[:, k, :], k == 0, k == NCH - 1,
                            reads=(sb, self.SCB), writes=(psb,))
        o, _ = self.cols[f"bmod{i}"]
        bm = self.CONST[:, o:o + 48]
        M = self.MOD[:, i]
        for s in range(2):
            self.dve(lambda e, s=s: e.tensor_tensor(
                M[:, :, :, s], psv[:, :, s].rearrange("p (m c) -> p m c", m=6),
                bm.rearrange("p (m c) -> p m c", m=6), ALU.add),
                reads=(psb, self.CONSTB), writes=(self.MODB,))
        for m, gname in ((1, f"n1g{i}"), (4, f"n2g{i}")):
            for s in range(2):
                self.dve(lambda e, m=m, s=s, gname=gname: e.scalar_tensor_tensor(
                    M[:, m, :, s], M[:, m, :, s], 1.0, self.cst(gname), ALU.add, ALU.mult),
                    reads=(self.MODB, self.CONSTB), writes=(self.MODB,))

    def modcol(self, i, m, s):
        return lambda c: self.MOD[:, i, m, c, s:s + 1]

    def tiles(self, include_ctx=True):
        r = []
        if include_ctx:
            r.append((0, NCTX, 1))
        for k in range(NLAT // 512):
            r.append((NCTX + 512 * k, 512, 0))
        return r

    def conv_module(self, i, j, last, W):
        self.phase(6)
        HNT, HNTB = self.aalloc("HNT", [NCH, 512], BF16)
        UL, ULB = self.aalloc("UL", [NCH, 8 * (64 + 2 * PADW)], BF16)
        UC, UCB = self.aalloc("UC", [NCH, NCTX + 2 * PADW], BF16)
        DG, DGB = self.aalloc("DIAG", [KW, 128], BF16)
        CB, CBB = self.aalloc("CB", [NCH, 512], BF16)
        VT, VTB = self.aalloc("VT", [NCH, 512], BF16)
        self.MEAN, self.MEANB = self.aalloc("MEAN", [512], F32)
        self.VAR, self.VARB = self.aalloc("VAR", [512], F32)
        self.SIG, self.SIGB = self.aalloc("SIG", [512], F32)
        self.P.op("pool", lambda e: e.memset(UL, 0.0), writes=(ULB,))
        self.P.op("pool", lambda e: e.memset(UC, 0.0), writes=(UCB,))
        w1v = W["conv_w_pw1"][j].rearrange("(k p) n -> p k n", p=128)
        w2v = W["conv_w_pw2"][j].rearrange("(k p) n -> p k n", p=128)
        s1 = [self.load_slot(w1v[:, :, 512 * s:512 * (s + 1)]) for s in range(4)]
        s2 = [self.load_slot(w2v[:, :, 512 * s:512 * (s + 1)]) for s in range(2)]
        b1 = lambda c: self.cst(f"b1{j}", c, 1)
        for tile_ in self.tiles(include_ctx=not last):
            self.conv_tile(i, j, tile_, s1, s2, b1, HNT, HNTB, UL, ULB, UC, UCB, DG, DGB, CB, CBB, VT, VTB)

    def conv_tile(self, i, j, tile_, s1, s2, b1, HNT, HNTB, UL, ULB, UC, UCB, DG, DGB, CB, CBB, VT, VTB):
        if True:
            t0, n, s = tile_
            nrow = 1 if s == 1 else n // 64
            rl = n // nrow
            rs = rl + 2 * PADW
            self.norm_mod(t0, n, self.modcol(i, 1, s), self.modcol(i, 0, s), HNT, HNTB)
            U = UC if s == 1 else UL
            UB = UCB if s == 1 else ULB
            Uv = U.rearrange("p c (r w) -> p c r w", w=rs)
            for c in range(NCH):
                pa, pab = self.psum()
                sv, sb = s1[c // 4]
                for k in range(NCH):
                    self.mm(pa[:, 0:n], sv[:, k, 128 * (c % 4):128 * (c % 4 + 1)], HNT[:, k, 0:n],
                            k == 0, k == NCH - 1, reads=(sb, HNTB), writes=(pab,))
                pg, pgb = self.psum()
                sv, sb = s1[2 + c // 4]
                for k in range(NCH):
                    self.mm(pg[:, 0:n], sv[:, k, 128 * (c % 4):128 * (c % 4 + 1)], HNT[:, k, 0:n],
                            k == 0, k == NCH - 1, reads=(sb, HNTB), writes=(pgb,))
                self.act(self.SIG[:, 0:n], pg[:, 0:n], AF.Sigmoid, reads=(pgb, self.CONSTB),
                         writes=(self.SIGB,), bias=b1(NCH + c))
                self.dve(lambda e, c=c, pa=pa: e.scalar_tensor_tensor(
                    Uv[:, c, :, PADW:PADW + rl], pa[:, 0:n].rearrange("p (r w) -> p r w", w=rl), b1(c),
                    self.SIG[:, 0:n].rearrange("p (r w) -> p r w", w=rl), ALU.add, ALU.mult),
                    reads=(pab, self.SIGB, self.CONSTB), writes=(UB,))
            wo, _ = self.cols[f"wdw{j}"]
            for c in range(NCH):
                wk = self.CONST[:, wo + c * KW:wo + (c + 1) * KW]
                self.dve(lambda e, wk=wk: e.tensor_tensor(
                    DG, self.IDENT[:, None, :].to_broadcast([128, KW, 128]),
                    wk[:, :, None].to_broadcast([128, KW, 128]), ALU.mult),
                    reads=(self.IDENTB, self.CONSTB), writes=(DGB,))
                pc, pcb = self.psum()
                pcv = pc[:, 0:n].rearrange("p (r w) -> p r w", w=rl)
                for k in range(KW):
                    self.mm(pcv, DG[:, k, :], Uv[:, c, :, k:k + rl], k == 0, k == KW - 1,
                            reads=(DGB, UB), writes=(pcb,))
                bd = self.cst(f"bdw{j}", c, 1)
                self.act(CB[:, c, 0:n], pc[:, 0:n], AF.Identity, reads=(pcb, self.CONSTB),
                         writes=(CBB,), bias=bd)
                self.act(self.SQ[:, c, 0:n], pc[:, 0:n], AF.Square, reads=(pcb, self.CONSTB),
                         writes=(self.SQB,), bias=bd)
            pm, pmb = self.psum()
            for c in range(NCH):
                self.mm(pm[:, 0:n], self.ONES[:, :], CB[:, c, 0:n], c == 0, c == NCH - 1,
                        reads=(CBB, self.ONESB), writes=(pmb,))
            pq, pqb = self.psum()
            for c in range(NCH):
                self.mm(pq[:, 0:n], self.ONES[:, :], self.SQ[:, c, 0:n], c == 0, c == NCH - 1,
                        reads=(self.SQB, self.ONESB), writes=(pqb,))
            self.dve(lambda e, pm=pm: e.tensor_copy(self.MEAN[:, 0:n], pm[:, 0:n]),
                     reads=(pmb,), writes=(self.MEANB,))
            self.dve(lambda e: e.tensor_tensor(self.VAR[:, 0:n], self.MEAN[:, 0:n], self.MEAN[:, 0:n], ALU.mult),
                     reads=(self.MEANB,), writes=(self.VARB,))
            self.dve(lambda e, pq=pq: e.tensor_tensor(self.VAR[:, 0:n], pq[:, 0:n], self.VAR[:, 0:n], ALU.subtract),
                     reads=(pqb, self.VARB), writes=(self.VARB,))
            self.rstd_from(self.VAR[:, 0:n], self.VARB, n, self.RSTD, self.RSTDB)
            for c in range(NCH):
                tm, tmb = self.tmp()
                self.dve(lambda e, c=c, tm=tm: e.tensor_tensor(tm[:, 0:n], CB[:, c, 0:n], self.MEAN[:, 0:n], ALU.subtract),
                         reads=(CBB, self.MEANB), writes=(tmb,))
                self.dve(lambda e, tm=tm: e.tensor_tensor(tm[:, 0:n], tm[:, 0:n], self.RSTD[:, 0:n], ALU.mult),
                         reads=(tmb, self.RSTDB), writes=(tmb,))
                self.act(VT[:, c, 0:n], tm[:, 0:n], AF.Silu, reads=(tmb, self.CONSTB), writes=(VTB,),
                         bias=self.cst(f"lnb{j}", c, 1), scale=self.cst(f"lng{j}", c, 1))
            for fc in range(NCH):
                po, pob = self.psum()
                sv, sb = s2[fc // 4]
                for k in range(NCH):
                    self.mm(po[:, 0:n], sv[:, k, 128 * (fc % 4):128 * (fc % 4 + 1)], VT[:, k, 0:n],
                            k == 0, k == NCH - 1, reads=(sb, VTB), writes=(pob,))
                tm, tmb = self.tmp()
                self.dve(lambda e, fc=fc, po=po, tm=tm: e.tensor_scalar(
                    tm[:, 0:n], po[:, 0:n], self.cst(f"b2{j}", fc, 1), self.MOD[:, i, 2, fc, s:s + 1], ALU.add, ALU.mult),
                    reads=(pob, self.CONSTB, self.MODB), writes=(tmb,))
                self.dve(lambda e, fc=fc, tm=tm: e.tensor_tensor(
                    self.X[:, fc, t0:t0 + n], self.X[:, fc, t0:t0 + n], tm[:, 0:n], ALU.add),
                    reads=(tmb, self.XB), writes=(self.XB,))

    def mlp(self, i, last, W):
        if self.dbg_stop == ("M", i):
            raise StopBuild()
        self.phase(6)
        HN, HNB = self.aalloc("HN", [NCH, T], BF16)
        HQ, HQB = self.aalloc("HQ", [NCH, 512], BF16)
        w1v = W["mlp_w1"][i].rearrange("(k p) n -> p k n", p=128)
        w2v = W["mlp_w2"][i].rearrange("(k p) n -> p k n", p=128)
        tiles = self.tiles(include_ctx=not last)
        for (t0, n, s) in tiles:
            self.norm_mod(t0, n, self.modcol(i, 4, s), self.modcol(i, 3, s), HN[:, :, t0:t0 + n], HNB)
        for q in range(4):
            s1 = [self.load_slot(w1v[:, :, 1024 * q + 512 * s:1024 * q + 512 * (s + 1)]) for s in range(2)]
            s2 = [self.load_slot(w2v[:, 8 * q + 4 * s:8 * q + 4 * (s + 1), :]) for s in range(2)]
            for tile_ in tiles:
                self.mlp_tile(i, tile_, s1, s2, HN, HNB, HQ, HQB)

    def mlp_tile(self, i, tile_, s1, s2, HN, HNB, HQ, HQB):
        if True:
            if True:
                t0, n, s = tile_
                for hc in range(8):
                    ph, phb = self.psum()
                    sv, sb = s1[hc // 4]
                    for k in range(NCH):
                        self.mm(ph[:, 0:n], sv[:, k, 128 * (hc % 4):128 * (hc % 4 + 1)], HN[:, k, t0:t0 + n],
                                k == 0, k == NCH - 1, reads=(sb, HNB), writes=(phb,))
                    tm, tmb = self.tmp()
                    self.act(tm[:, 0:n], ph[:, 0:n], AF.Relu, reads=(phb,), writes=(tmb,))
                    self.dve(lambda e, hc=hc, tm=tm: e.tensor_tensor(HQ[:, hc, 0:n], tm[:, 0:n], tm[:, 0:n], ALU.mult),
                             reads=(tmb,), writes=(HQB,))
                for fc in range(NCH):
                    py, pyb = self.psum()
                    for hc in range(8):
                        sv, sb = s2[hc // 4]
                        self.mm(py[:, 0:n], sv[:, hc % 4, 128 * fc:128 * (fc + 1)], HQ[:, hc, 0:n],
                                hc == 0, hc == 7, reads=(sb, HQB), writes=(pyb,))
                    self.dve(lambda e, fc=fc, py=py: e.scalar_tensor_tensor(
                        self.X[:, fc, t0:t0 + n], py[:, 0:n], self.MOD[:, i, 5, fc, s:s + 1],
                        self.X[:, fc, t0:t0 + n], ALU.mult, ALU.add),
                        reads=(pyb, self.MODB, self.XB), writes=(self.XB,))

    def build(self):
        nc = self.nc
        P = self.P
        dt = nc.dram_tensor
        xT = dt("xT", [D, T], F32, kind="ExternalInput").ap()
        consts = dt("consts", [128, self.ncols], F32, kind="ExternalInput").ap()
        W = {}
        for name, shp in (("w_mod", [DEPTH, D, 6 * D]), ("conv_w_pw1", [2, D, 2 * D]), ("conv_w_pw2", [2, D, D]),
                          ("dn_w_in", [2, D, 4128]), ("dn_w_o", [2, D, D]),
                          ("mlp_w1", [DEPTH, D, 4 * D]), ("mlp_w2", [DEPTH, 4 * D, D])):
            if name in needed_weights(self.nlayers):
                W[name] = dt(name, shp, F32, kind="ExternalInput").ap()
        outT = dt("outT", [D, NLAT], F32, kind="ExternalOutput").ap()
        if self.nlayers >= 2:
            W["wab"] = dt("wab", [2, D, 128], F32, kind="ExternalInput").ap()
            self.hx_in = dt("hx_in", [128, 16], BF16)
            self.hx_out = dt("hx_out", [256, 16], BF16)
            self.st_in = dt("st_in", [128, 128], F32)
            self.st_out = dt("st_out", [256, 128], F32)
            self.HXI = Buf("hx_in"); self.HXO = Buf("hx_out"); self.STI = Buf("st_in"); self.STO = Buf("st_out")
        self.W = W

        from contextlib import ExitStack
        with ExitStack() as es:
            def sb(name, shape, dtype):
                return es.enter_context(nc.sbuf_tensor(name, shape, dtype))

            self.X = sb("X", [128, NCH, T], F32); self.XB = Buf("X")
            self.ARENA = sb("ARENA", [128, ARENA_BYTES // 2], BF16)
            self.CONST = sb("CONST", [128, self.ncols], F32); self.CONSTB = Buf("CONST")
            self.MOD = sb("MOD", [128, DEPTH, 6, NCH, 2], F32); self.MODB = Buf("MOD")
            self.SC = sb("SC", [128, NCH, 2], BF16); self.SCB = Buf("SC")
            self.ONES = sb("ONES", [128, 128], BF16); self.ONESB = Buf("ONES")
            self.IDENT = sb("IDENT", [128, 128], BF16); self.IDENTB = Buf("IDENT")
            self.IDF = sb("IDF", [128, 128], F32); self.IDFB = Buf("IDF")
            self.EPST = sb("EPST", [128, 1], F32); self.EPSB = Buf("EPS")
            for nm_ in ("ONES", "IDENT", "IDF"):
                setattr(self, nm_, getattr(self, nm_)[:, :])
            self.SQ = sb("SQ", [128, NCH, 512], BF16); self.SQB = Buf("SQ")
            self.LNT = sb("LNT", [128, 512], F32); self.LNTB = Buf("LNT")
            self.RSTD = sb("RSTD", [128, 512], F32); self.RSTDB = Buf("RSTD")
            self.ONE1 = sb("ONE1", [128, 1], F32); self.ONEB = Buf("ONE1")
            self.ONEF = sb("ONEF", [128, 128], F32); self.ONEFB = Buf("ONEF")
            self.ONESS = sb("ONESS", [128, 128], BF16); self.ONESSB = Buf("ONESS")
            self.ONES128 = sb("ONES128", [128, 128], BF16); self.ONES128B = Buf("ONES128")
            self.ML = sb("ML", [128, 128], F32); self.MLS = sb("MLS", [128, 128], F32)
            self.MU = sb("MU", [128, 128], F32); self.MUS = sb("MUS", [128, 128], F32)
            self.MASKB = Buf("MASK")
            for nm_ in ("ONEF", "ONESS", "ONES128", "ML", "MLS", "MU", "MUS", "ONE1"):
                setattr(self, nm_, getattr(self, nm_)[:, :])
            self.TMP = [sb(f"TMP{i}", [128, 512], F32) for i in range(2)]
            self.TMPB = [Buf(f"TMP{i}") for i in range(2)]
            self.tmp_rr = 0
            self.OUTB = Buf("OUT")
            self.PS = [es.enter_context(nc.psum_tensor(f"PS{i}", [128, 512], F32)) for i in range(8)]
            self.PSB = [Buf(f"PS{i}") for i in range(8)]

            P.dma("sp", "const", lambda e: e.dma_start(out=self.CONST[:, :], in_=consts[:, :]),
                  writes=(self.CONSTB,), wait_total=True)
            xv = xT.rearrange("(c p) t -> p c t", p=128)
            for c in range(NCH):
                P.dma("sp", "xin", lambda e, c=c: e.dma_start(out=self.X[:, c, :], in_=xv[:, c, :]),
                      writes=(self.XB,), wait_total=True)
            P.op("pool", lambda e: e.memset(self.ONES[:, :], 1.0 / D), writes=(self.ONESB,))
            P.op("pool", lambda e: e.memset(self.EPST[:, :], EPS), writes=(self.EPSB,))
            P.op("pool", lambda e: e.memset(self.IDF[:, :], 0.0), writes=(self.IDFB,))
            P.op("pool", lambda e: e.affine_select(self.IDF[:, :], self.IDF[:, :], pattern=[[-1, 128]],
                                                    compare_op=ALU.not_equal, fill=1.0, base=0, channel_multiplier=1),
                 reads=(self.IDFB,), writes=(self.IDFB,))
            P.op("pool", lambda e: e.tensor_copy(self.IDENT[:, :], self.IDF[:, :]), reads=(self.IDFB,), writes=(self.IDENTB,))
            P.op("pool", lambda e: e.memset(self.ONE1[:, :], 1.0), writes=(self.ONEB,))
            P.op("pool", lambda e: e.memset(self.ONEF[:, :], 1.0), writes=(self.ONEFB,))
            P.op("pool", lambda e: e.memset(self.ONESS[:, :], 1.0), writes=(self.ONESSB,))
            P.op("pool", lambda e: e.memset(self.ONES128[:, :], 1.0 / 128), writes=(self.ONES128B,))
            for mt, base, cm, pat in ((self.ML, 0, 1, -1), (self.MLS, -1, 1, -1), (self.MU, 0, -1, 1), (self.MUS, -1, -1, 1)):
                P.op("pool", lambda e, mt=mt, base=base, cm=cm, pat=pat: e.affine_select(
                    mt[:, :], self.ONEF[:, :], pattern=[[pat, 128]], compare_op=ALU.is_ge, fill=0.0,
                    base=base, channel_multiplier=cm), reads=(self.ONEFB,), writes=(self.MASKB,))
            self.BD = sb("BD", [128, 128], F32)[:, :]
            P.op("pool", lambda e: e.memset(self.BD, 0.0), writes=(self.MASKB,))
            P.op("pool", lambda e: e.memset(self.BD[0:64, 0:64], 1.0), writes=(self.MASKB,))
            P.op("pool", lambda e: e.memset(self.BD[64:128, 64:128], 1.0), writes=(self.MASKB,))
            for mt in (self.ML, self.MLS, self.MU, self.MUS):
                P.op("pool", lambda e, mt=mt: e.tensor_tensor(mt, mt, self.BD, ALU.mult),
                     reads=(self.MASKB,), writes=(self.MASKB,))
            co, _ = self.cols["c"]
            self.act(self.SC[:, :, :].rearrange("p k s -> p (k s)"), self.CONST[:, co:co + 16], AF.Silu,
                     reads=(self.CONSTB,), writes=(self.SCB,))
            self.phase(6)
            for i in range(self.nlayers):
                self.compute_mod(i, W["w_mod"])

            try:
                for i in range(self.nlayers):
                    last = i == self.nlayers - 1
                    j = i // 2
                    if i % 2 == 0:
                        self.conv_module(i, j, last, W)
                    else:
                        self.delta_module(i, j, last, W)
                    self.mlp(i, last, W)
            except StopBuild:
                pass

            ov = outT.rearrange("(c p) t -> p c t", p=128)
            for (t0, n, s) in self.tiles(include_ctx=False):
                fgc = lambda c: self.cst("fg", c, 1)
                self.norm_mod(t0, n, fgc, None, None, None,
                              dst_dram=lambda c, t0=t0, n=n: ov[:, c, t0 - NCTX:t0 - NCTX + n])

            keys = list(self.P.dma_count.keys())
            sems = {}
            for k in list(Prog.ENGS) + keys:
                sems[k] = es.enter_context(nc.semaphore(f"s_{k}"))
            block = es.enter_context(nc.Block())
            self.P.emit(nc, block, sems, final_waits={"sp": tuple(k for k in keys if k.startswith("out_"))})
        return nc

    def exchange(self, key, src_ap, src_b, din, din_b, dout, dout_b, dst_ap, dst_b, din_view=None):
        P = self.P
        dv = din[:, :] if din_view is None else din_view
        P.dma("sp", key + "_a", lambda e: e.dma_start(out=dv, in_=src_ap), reads=(src_b,), writes=(din_b,))
        groups = [[2 * k, 2 * k + 1] for k in range(self.ncores // 2)]
        P.dma("pool", key + "_c", lambda e: e.collective_compute(
            "AllGather", ALU.bypass, replica_groups=groups, ins=[din.ap().opt()], outs=[dout.ap().opt()]),
            reads=(din_b,), writes=(dout_b,), inc=1)
        P.dma("sp", key + "_b", lambda e: e.dma_start(
            out=dst_ap, in_=dout.ap().rearrange("(r p) n -> p r n", p=128)), reads=(dout_b,), writes=(dst_b,))

    def delta_module(self, i, j, last, W):
        P = self.P
        self.phase(0)
        A = self.aalloc
        NCK = T // 128
        PT_ = 2312
        pcol = lambda t: t + 2 if t < NCTX else t + 6
        WIN, WINB = A("WIN", [8, 512], BF16)
        WO, WOB = A("WO", [1024], BF16)
        WAB, WABB = WIN[:, :, 0:128], WINB
        HNP, HNPB = A("HNP", [8, PT_], BF16)
        G, GB = A("G", [T], F32)
        O, OB = A("O", [T], F32)
        COLS, COLSB = A("COLS", [NCK, 32], F32)
        HLB, HLBB = A("HLB", [NCK, 16], F32)
        KDF, KDFB = A("KDF", [NCK, 16], F32)
        CDA, CDAB = A("CDA", [NCK, 16], F32)
        CDB_, CDBB = A("CDB", [NCK, 16], F32)
        NBE, NBEB = A("NBE", [NCK, 16], F32)
        CD = (CDA, CDB_)
        CDB = (CDAB, CDBB)
        NB, NBB = HLB, HLBB
        TOT, TOTB = A("TOT", [2 * NCK], F32)
        EAL, EALB = A("EAL", [1], F32)
        SELC, SELCB = A("SELC", [32], F32)
        QT, QTB = A("QT", [T], BF16)
        KF, KFB = A("KF", [T], F32)
        VTM, VTMB = A("VTM", [NCK, 128], BF16)
        PRET, PRETB = A("PRET", [260], BF16)
        VTT, VTTB = A("VTT", [256], BF16)
        DG5, DG5B = A("DG5", [5, 128], BF16)
        SELR, SELRB = A("SELR", [4, 128], F32)
        HXG, HXGB = A("HXG", [2, 16], BF16)
        HXS, HXSB = A("HXS", [16], F32)
        SG, SGB = A("SG", [2, 128], F32)
        S, SB_ = A("S", [128], F32)
        SBF, SBFB = A("SBF", [128], BF16)
        OG, OGB = A("OG", [512], BF16)
        f32t = {}
        for nm in ("E1", "E2", "EROW", "M2I", "P", "PT", "Z"):
            nb_ = 1 if nm in ("E1", "E2") else 2
            bufs_ = [A(nm + str(k), [128], F32) for k in range(nb_)]
            f32t[nm] = bufs_ * (2 // nb_)
        b16t = {}
        for nm in ("KB", "QKTM", "BV", "KD", "QG", "VN", "YB", "ZB"):
            nb_ = 2 if nm in ("QKTM",) else 1
            bufs_ = [A(nm + str(k), [128], BF16) for k in range(nb_)]
            b16t[nm] = bufs_ * (2 // nb_)
        rr = {}

        def nxt(d, nm):
            k = rr.get(nm, 0)
            rr[nm] = k + 1
            return d[nm][k % 2]

        def dve(fn, reads, writes):
            P.op("dve", fn, reads=reads, writes=writes)

        P.op("pool", lambda e: e.memset(HNP[:, :, 0:2], 0.0), writes=(HNPB,))
        P.op("pool", lambda e: e.memset(HNP[:, :, 258:262], 0.0), writes=(HNPB,))
        for (t0, n, s) in self.tiles():
            self.norm_mod(t0, n, self.modcol(i, 1, s), self.modcol(i, 0, s), HNP[:, :, pcol(t0):pcol(t0) + n], HNPB)
        self.exchange(f"hx", HNP[:, :, 2308:2310], HNPB, self.hx_in, self.HXI, self.hx_out, self.HXO,
                      HXG, HXGB, din_view=self.hx_in.ap().rearrange("p (k t) -> p k t", t=2))
        pm = lambda r: self.cst("pm", r, 1)
        dve(lambda e: e.tensor_scalar(HXS, HXG[:, 0, :], pm(0), None, ALU.mult), (HXGB, self.CONSTB), (HXSB,))
        dve(lambda e: e.scalar_tensor_tensor(HXS, HXG[:, 1, :], pm(1), HXS, ALU.mult, ALU.add),
            (HXGB, HXSB, self.CONSTB), (HXSB,))
        hv = HXS.rearrange("p (k t) -> p k t", t=2)
        dve(lambda e: e.tensor_copy(HNP[:, :, 2310:2311], hv[:, :, 1:2]), (HXSB,), (HNPB,))
        dve(lambda e: e.tensor_copy(HNP[:, :, 2311:2312], hv[:, :, 0:1]), (HXSB,), (HNPB,))

        P.dma("pool", "wab", lambda e: e.dma_start(out=WAB, in_=W["wab"][j].rearrange("(k p) n -> p k n", p=128)),
              writes=(WABB,))
        for q in range(4):
            dve(lambda e, q=q: e.tensor_copy(SELC[:, 8 * q:8 * q + 8], self.IDF[:, 32 * q:32 * q + 8]),
                (self.IDFB,), (SELCB,))
        self.act(EAL, self.cst(f"alog{j}"), AF.Exp, reads=(self.CONSTB,), writes=(EALB,))
        for (t0, n, s) in self.tiles():
            ps, psb = self.psum()
            for k in range(NCH):
                self.mm(ps[:, 0:n], WAB[:, k, :], HNP[:, k, pcol(t0):pcol(t0) + n], k == 0, k == NCH - 1,
                        reads=(WABB, HNPB), writes=(psb,))
            self.act(O[0:64, t0:t0 + n], ps[0:64, 0:n], AF.Exp, reads=(psb, self.CONSTB), writes=(OB,),
                     bias=self.cst(f"dtb{j}")[0:64, :])
            self.act(O[0:64, t0:t0 + n], O[0:64, t0:t0 + n], AF.Ln, reads=(OB, self.ONEB), writes=(OB,),
                     bias=self.ONE1[0:64, :])
            dve(lambda e, t0=t0, n=n: e.tensor_scalar(O[0:64, t0:t0 + n], O[0:64, t0:t0 + n], EAL[0:64, :], None, ALU.mult),
                (OB, EALB), (OB,))
            self.act(G[64:128, t0:t0 + n], ps[64:128, 0:n], AF.Sigmoid, reads=(psb,), writes=(GB,))
        for c in range(2 * NCK):
            dve(lambda e, c=c: e.tensor_tensor_scan(G[0:64, 64 * c:64 * (c + 1)], self.ONEF[0:64, 0:64],
                                                    O[0:64, 64 * c:64 * (c + 1)], 0.0, ALU.mult, ALU.add),
                (OB, self.ONEFB), (GB,))
        Gv = G[32:64, :].rearrange("p (c w) -> p c w", w=64)
        dve(lambda e: e.tensor_copy(TOT[32:64, :], Gv[:, :, 63]), (GB,), (TOTB,))
        dve(lambda e: e.tensor_tensor(Gv, TOT[32:64, :, None].to_broadcast([32, 2 * NCK, 64]), Gv, ALU.subtract),
            (GB, TOTB), (GB,))
        dve(lambda e: e.tensor_tensor(G[32:64, :], G[32:64, :], O[32:64, :], ALU.add), (GB, OB), (GB,))
        for c in range(NCK):
            ps, psb = self.psum()
            self.mm(ps[:, 0:32], G[:, 128 * c:128 * (c + 1)], SELC, True, True, reads=(GB, SELCB), writes=(psb,))
            self.act(COLS[:, c, :], ps[:, 0:32], AF.Identity, reads=(psb,), writes=(COLSB,))
        G3 = lambda lo, hi: G[lo:hi, :].rearrange("p (c w) -> p c w", w=128)
        for sub, (RH, RHB, HL, HLB_) in enumerate(((KDF, KDFB, CDA, CDAB), (NBE, NBEB, CDB_, CDBB))):
            P.op("pool", lambda e, RH=RH: e.memset(RH, 0.0), writes=(RHB,))
            tf_ = 63 + 64 * sub
            tb_ = 64 * sub
            dve(lambda e, RH=RH, tf_=tf_: e.tensor_tensor(
                RH[0:32], SELC[0:32, None, 0:16].to_broadcast([32, NCK, 16]),
                G3(0, 32)[:, :, tf_:tf_ + 1].to_broadcast([32, NCK, 16]), ALU.mult), (SELCB, GB, RHB), (RHB,))
            dve(lambda e, RH=RH, tb_=tb_: e.tensor_tensor(
                RH[32:64], SELC[32:64, None, 0:16].to_broadcast([32, NCK, 16]),
                G3(32, 64)[:, :, tb_:tb_ + 1].to_broadcast([32, NCK, 16]), ALU.mult), (SELCB, GB, RHB), (RHB,))
            ps, psb = self.psum()
            self.mm(ps[:, 0:NCK * 16], self.ONEF, RH.rearrange("p c n -> p (c n)"), True, True,
                    reads=(RHB, self.ONEFB), writes=(psb,))
            self.act(HL.rearrange("p c n -> p (c n)"), ps[:, 0:NCK * 16], AF.Identity, reads=(psb,), writes=(HLB_,))
            hs = slice(64 * sub, 64 * sub + 64)
            dve(lambda e, HL=HL, hs=hs: e.tensor_copy(HLB[hs], HL[hs]), (HLB_,), (HLBB,))
        self.act(CDA, CDA, AF.Exp, reads=(CDAB, HLBB), writes=(CDAB,), scale=-1.0)
        self.act(CDB_, CDB_, AF.Exp, reads=(CDBB, HLBB), writes=(CDBB,), scale=-1.0)
        dve(lambda e: e.tensor_tensor(KDF, COLS[:, :, 0:16], HLB, ALU.subtract), (COLSB, HLBB), (KDFB,))
        self.act(KDF, KDF, AF.Exp, reads=(KDFB,), writes=(KDFB,))
        self.act(NBE, COLS[:, :, 0:16], AF.Exp, reads=(COLSB,), writes=(NBEB,), scale=-1.0)
        dve(lambda e: e.scalar_tensor_tensor(NBE, COLS[:, :, 16:32], -1.0, NBE, ALU.mult, ALU.mult),
            (COLSB, NBEB), (NBEB,))
        dve(lambda e: e.tensor_scalar(NB, COLS[:, :, 16:32], -1.0, None, ALU.mult), (COLSB,), (NBB,))

        w_in = W["dn_w_in"][j]
        wq = w_in[:, 0:4096].rearrange("(k p) (g hh d) -> p k g hh d", p=128, g=4, hh=8)
        cwo, _ = self.cols[f"cw{j}"]
        QSC = float(128 ** -0.5)

        for h in range(8):
            self.delta_head(i, j, h, last, locals())

    def delta_head(self, i, j, h, last, L):
        P = self.P
        g_ = lambda k: L[k]
        (WIN, WINB, WO, WOB, HNP, HNPB, G, GB, O, OB, COLS, COLSB, KDF, KDFB, CD, CDB, NBE, NBEB, NB, NBB,
         QT, QTB, KF, KFB, VTM, VTMB, PRET, PRETB, VTT, VTTB, DG5, DG5B, SELR, SELRB, SG, SGB, S, SB_,
         SBF, SBFB, OG, OGB) = [g_(k) for k in (
            "WIN", "WINB", "WO", "WOB", "HNP", "HNPB", "G", "GB", "O", "OB", "COLS", "COLSB", "KDF", "KDFB", "CD", "CDB",
            "NBE", "NBEB", "NB", "NBB", "QT", "QTB", "KF", "KFB", "VTM", "VTMB", "PRET", "PRETB", "VTT", "VTTB",
            "DG5", "DG5B", "SELR", "SELRB", "SG", "SGB", "S", "SB_", "SBF", "SBFB", "OG", "OGB")]
        f32t, b16t, nxt, dve, pcol, wq, cwo, QSC, NCK, W = (g_(k) for k in (
            "f32t", "b16t", "nxt", "dve", "pcol", "wq", "cwo", "QSC", "NCK", "W"))
        for g in range(4):
            P.dma("pool", f"win{g}", lambda e, g=g: e.dma_start(out=WIN[:, :, 128 * g:128 * (g + 1)], in_=wq[:, :, g, h, :]),
                  writes=(WINB,), indep=False)
        P.dma("pool", "wo", lambda e: e.dma_start(out=WO, in_=W["dn_w_o"][j][128 * h:128 * (h + 1), :]), writes=(WOB,))
        for q in range(4):
            r = 32 * q + h
            dve(lambda e, q=q, r=r: e.tensor_copy(SELR[:, q, :], self.IDF[:, r:r + 1].to_broadcast([128, 128])),
                (self.IDFB,), (SELRB,))
        segs = [(0, 0)] + [(260 + 256 * k, NCTX + 256 * k) for k in range(NLAT // 256)]
        for g in (0, 1, 2):
            wk = self.CONST[:, cwo + (8 * g + h) * 5:cwo + (8 * g + h + 1) * 5]
            dve(lambda e, wk=wk: e.tensor_tensor(
                DG5, self.IDENT[:, None, :].to_broadcast([128, 5, 128]),
                wk[:, :, None].to_broadcast([128, 5, 128]), ALU.mult), (self.IDENTB, self.CONSTB), (DG5B,))
            for (pc0, tk0) in segs:
                ps, psb = self.psum()
                for k in range(NCH):
                    self.mm(ps[:, 0:260], WIN[:, k, 128 * g:128 * (g + 1)], HNP[:, k, pc0:pc0 + 260], k == 0, k == NCH - 1,
                            reads=(WINB, HNPB), writes=(psb,))
                self.act(PRET, ps[:, 0:260], AF.Identity, reads=(psb,), writes=(PRETB,))
                p2, p2b = self.psum()
                for tap in range(5):
                    self.mm(p2[:, 0:256], DG5[:, tap, :], PRET[:, tap:tap + 256], tap == 0, tap == 4,
                            reads=(DG5B, PRETB), writes=(p2b,))
                if g < 2:
                    tm, tmb = self.tmp()
                    self.act(tm[:, 0:256], p2[:, 0:256], AF.Silu, reads=(p2b,), writes=(tmb,))
                    self.act(self.SQ[:, 0, 0:256], tm[:, 0:256], AF.Square, reads=(tmb,), writes=(self.SQB,))
                    p3, p3b = self.psum()
                    self.mm(p3[:, 0:256], self.ONESS, self.SQ[:, 0, 0:256], True, True, reads=(self.SQB, self.ONESSB), writes=(p3b,))
                    self.rstd_from(p3[:, 0:256], p3b, 256, self.RSTD, self.RSTDB)
                    dst_, dstb_ = (QT, QTB) if g == 0 else (KF, KFB)
                    dve(lambda e, g=g, tm=tm, tk0=tk0, dst_=dst_: e.scalar_tensor_tensor(
                        dst_[:, tk0:tk0 + 256], tm[:, 0:256], QSC if g == 0 else 1.0, self.RSTD[:, 0:256], ALU.mult, ALU.mult),
                        (tmb, self.RSTDB), (dstb_,))
                else:
                    self.act(VTT[:, 0:256], p2[:, 0:256], AF.Silu, reads=(p2b,), writes=(VTTB,))
                    for cc in range(2):
                        c = tk0 // 128 + cc
                        p4, p4b = self.psum()
                        self.mm(p4[:, 0:128], VTT[:, 128 * cc:128 * (cc + 1)], self.IDENT, True, True,
                                reads=(VTTB, self.IDENTB), writes=(p4b,))
                        self.act(VTM[:, c, :], p4[:, 0:128], AF.Identity, reads=(p4b,), writes=(VTMB,))

        def chunk(d, c, first_touch):
            n = 8 * d + h
            ck = slice(128 * c, 128 * (c + 1))
            col = lambda Tn, off=0: Tn[:, c, off + n:off + n + 1]
            (E1, E1B), (E2, E2B), (EROW, EROWB), (M2I, M2IB) = (nxt(f32t, "E1"), nxt(f32t, "E2"),
                                                               nxt(f32t, "EROW"), nxt(f32t, "M2I"))
            hr, hrb = self.psum()
            self.mm(hr[:, 0:128], SELR[:, d, :], G[:, ck], True, True, reads=(SELRB, GB), writes=(hrb,))
            br, brb = self.psum()
            self.mm(br[:, 0:128], SELR[:, 2 + d, :], G[:, ck], True, True, reads=(SELRB, GB), writes=(brb,))
            if d == 0:
                mDs, mTs, mTi = self.MLS, self.MUS, self.MU
            else:
                mDs, mTs, mTi = self.MUS, self.MLS, self.ML
            dve(lambda e: e.tensor_scalar(E1, hr[:, 0:128], col(COLS), 0.0, ALU.subtract, ALU.min), (hrb, COLSB), (E1B,))
            self.act(E1, E1, AF.Exp, reads=(E1B,), writes=(E1B,))
            dve(lambda e: e.tensor_tensor(E1, E1, mDs, ALU.mult), (E1B, self.MASKB), (E1B,))
            dve(lambda e: e.tensor_scalar(E2, hr[:, 0:128], col(COLS), 0.0, ALU.subtract, ALU.max), (hrb, COLSB), (E2B,))
            self.act(E2, E2, AF.Exp, reads=(E2B,), writes=(E2B,), scale=-1.0)
            dve(lambda e: e.tensor_tensor(M2I, E2, mTi, ALU.mult), (E2B, self.MASKB), (M2IB,))
            dve(lambda e: e.tensor_tensor(E2, E2, mTs, ALU.mult), (E2B, self.MASKB), (E2B,))
            self.act(EROW, hr[:, 0:128], AF.Exp, reads=(hrb,), writes=(EROWB,), scale=-1.0)
            KB, KBB = nxt(b16t, "KB")
            self.act(KB, KF[:, ck], AF.Identity, reads=(KFB,), writes=(KBB,))
            kk, kkb = self.psum()
            self.mm(kk[:, 0:128], KF[:, ck], KF[:, ck], True, True, reads=(KFB,), writes=(kkb,))
            Pm, PmB = nxt(f32t, "P")
            PTm, PTmB = nxt(f32t, "PT")
            Zm, ZmB = nxt(f32t, "Z")
            dve(lambda e, Pm=Pm: e.scalar_tensor_tensor(Pm, kk[:, 0:128], col(NB), E1, ALU.mult, ALU.mult), (kkb, NBB, E1B), (PmB,))
            dve(lambda e: e.scalar_tensor_tensor(E2, kk[:, 0:128], -1.0, E2, ALU.mult, ALU.mult), (kkb, E2B), (E2B,))
            dve(lambda e, PTm=PTm: e.tensor_tensor(PTm, E2, br[:, 0:128], ALU.mult), (E2B, brb), (PTmB,))
            qk_, qkb_ = self.psum()
            self.mm(qk_[:, 0:128], KB, QT[:, ck], True, True, reads=(KBB, QTB), writes=(qkb_,))
            QKTM, QKTMB = nxt(b16t, "QKTM")
            dve(lambda e: e.tensor_tensor(QKTM, qk_[:, 0:128], M2I, ALU.mult), (qkb_, M2IB), (QKTMB,))
            dve(lambda e, Zm=Zm, PTm=PTm: e.tensor_tensor(Zm, self.IDF, PTm, ALU.add), (self.IDFB, PTmB), (ZmB,))
            for m in range(1, 6):
                Pn, PnB = nxt(f32t, "P")
                pp, ppb = self.psum()
                self.mm(pp[:, 0:128], PTm, Pm, True, True, reads=(PTmB, PmB), writes=(ppb,))
                self.act(Pn, pp[:, 0:128], AF.Identity, reads=(ppb,), writes=(PnB,))
                if m < 5:
                    PTn, PTnB = nxt(f32t, "PT")
                    pt, ptb = self.psum()
                    self.P.op("pe", lambda e, pt=pt, Pn=Pn: e.transpose(pt[:, 0:128], Pn, self.IDF),
                              reads=(PnB, self.IDFB), writes=(ptb,))
                    self.act(PTn, pt[:, 0:128], AF.Identity, reads=(ptb,), writes=(PTnB,))
                Zn, ZnB = nxt(f32t, "Z")
                pz, pzb = self.psum()
                self.mm(pz[:, 0:128], Pn, Zm, True, True, reads=(PnB, ZmB), writes=(pzb,))
                dve(lambda e, Zn=Zn, Zm=Zm, pz=pz: e.tensor_tensor(Zn, pz[:, 0:128], Zm, ALU.add), (pzb, ZmB), (ZnB,))
                Pm, PmB = Pn, PnB
                if m < 5:
                    PTm, PTmB = PTn, PTnB
                Zm, ZmB = Zn, ZnB
            BV, BVB = nxt(b16t, "BV")
            KD, KDB = nxt(b16t, "KD")
            QG, QGB = nxt(b16t, "QG")
            dve(lambda e: e.tensor_scalar(BV, VTM[:, c, :], col(COLS, 16), None, ALU.mult), (VTMB, COLSB), (BVB,))
            p4, p4b = self.psum()
            self.mm(p4[:, 0:128], KB, self.IDENT, True, True, reads=(KBB, self.IDENTB), writes=(p4b,))
            dve(lambda e: e.tensor_scalar(KD, p4[:, 0:128], col(KDF), None, ALU.mult), (p4b, KDFB), (KDB,))
            dve(lambda e: e.tensor_tensor(QG, QT[:, ck], EROW, ALU.mult), (QTB, EROWB), (QGB,))
            Y, YB = nxt(b16t, "YB")
            VN, VNB = nxt(b16t, "VN")
            ZB, ZBB = nxt(b16t, "ZB")
            self.act(ZB, Zm, AF.Identity, reads=(ZmB,), writes=(ZBB,))
            for a_ in ((0, 1) if d == 0 else (1, 0)):
                hs = slice(64 * a_, 64 * a_ + 64)
                tk = slice(128 * c + 64 * a_, 128 * c + 64 * a_ + 64)
                ks, ksb = self.psum()
                self.mm(ks[:, 0:128], KB, SBF, True, True, reads=(KBB, SBFB), writes=(ksb,))
                dve(lambda e, ks=ks: e.scalar_tensor_tensor(Y, ks[:, 0:128], col(NBE), BV, ALU.mult, ALU.add),
                    (ksb, NBEB, BVB), (YB,))
                vn, vnb = self.psum()
                self.mm(vn[:, 0:128], ZB, Y, True, True, reads=(ZBB, YB), writes=(vnb,))
                self.act(VN, vn[:, 0:128], AF.Identity, reads=(vnb,), writes=(VNB,))
                ot, otb = self.psum()
                self.mm(ot[:, 0:64], SBF, QG[:, hs], True, False, reads=(SBFB, QGB), writes=(otb,))
                self.mm(ot[:, 0:64], VN, QKTM[:, hs], False, True, reads=(VNB, QKTMB), writes=(otb,))
                if first_touch:
                    dve(lambda e, ot=ot, tk=tk: e.tensor_copy(O[:, tk], ot[:, 0:64]), (otb,), (OB,))
                else:
                    dve(lambda e, ot=ot, tk=tk: e.tensor_tensor(O[:, tk], O[:, tk], ot[:, 0:64], ALU.add), (otb, OB), (OB,))
                sn, snb = self.psum()
                self.mm(sn[:, 0:128], KD[hs, :], VN[hs, :], True, True, reads=(KDB, VNB), writes=(snb,))
                dve(lambda e, sn=sn, a_=a_: e.scalar_tensor_tensor(S, S, col(CD[a_]), sn[:, 0:128], ALU.mult, ALU.add),
                    (SB_, CDB[a_], snb), (SB_,))
                self.act(SBF, S, AF.Identity, reads=(SB_,), writes=(SBFB,))

        def zero_state():
            P.op("pool", lambda e: e.memset(S, 0.0), writes=(SB_,))
            P.op("pool", lambda e: e.memset(SBF, 0.0), writes=(SBFB,))

        zero_state()
        for c in range(NCK):
            chunk(0, c, True)
        if self.dbg_stop == ("A", h):
            raise StopBuild()
        self.exchange("st", S, SB_, self.st_in, self.STI, self.st_out, self.STO, SG, SGB)
        pm = lambda r: self.cst("pm", r, 1)
        dve(lambda e: e.tensor_scalar(S, SG[:, 0, :], pm(0), None, ALU.mult), (SGB, self.CONSTB), (SB_,))
        dve(lambda e: e.scalar_tensor_tensor(S, SG[:, 1, :], pm(1), S, ALU.mult, ALU.add), (SGB, SB_, self.CONSTB), (SB_,))
        self.act(SBF, S, AF.Identity, reads=(SB_,), writes=(SBFB,))
        for c in range(NCK - 1, 1, -1):
            chunk(1, c, False)
        if not last:
            zero_state()
            for c in (1, 0):
                chunk(1, c, False)
        if self.dbg_stop == ("B", h):
            raise StopBuild()
        for (t0, n, s) in self.tiles(include_ctx=not last):
            self.head_out(i, j, t0, n, s, WIN, WINB, WO, WOB, HNP, HNPB, O, OB, OG, OGB, pcol)
        if self.dbg_stop == ("H", h):
            raise StopBuild()

    def head_out(self, i, j, t0, n, s, WIN, WINB, WO, WOB, HNP, HNPB, O, OB, OG, OGB, pcol):
        self.act(self.SQ[:, 0, 0:n], O[:, t0:t0 + n], AF.Square, reads=(OB,), writes=(self.SQB,))
        ps, psb = self.psum()
        self.mm(ps[:, 0:n], self.ONES128, self.SQ[:, 0, 0:n], True, True, reads=(self.SQB, self.ONES128B), writes=(psb,))
        self.rstd_from(ps[:, 0:n], psb, n, self.RSTD, self.RSTDB)
        zp, zpb = self.psum()
        for k in range(NCH):
            self.mm(zp[:, 0:n], WIN[:, k, 384:512], HNP[:, k, pcol(t0):pcol(t0) + n], k == 0, k == NCH - 1,
                    reads=(WINB, HNPB), writes=(zpb,))
        zs, zsb = self.tmp()
        self.act(zs[:, 0:n], zp[:, 0:n], AF.Silu, reads=(zpb,), writes=(zsb,))
        t1, t1b = self.tmp()
        self.dve(lambda e: e.scalar_tensor_tensor(t1[:, 0:n], O[:, t0:t0 + n], self.cst(f"ng{j}"), self.RSTD[:, 0:n],
                                                  ALU.mult, ALU.mult), (OB, self.RSTDB, self.CONSTB), (t1b,))
        self.dve(lambda e: e.tensor_tensor(OG[:, 0:n], t1[:, 0:n], zs[:, 0:n], ALU.mult), (t1b, zsb), (OGB,))
        for fc in range(NCH):
            py, pyb = self.psum()
            self.mm(py[:, 0:n], WO[:, 128 * fc:128 * (fc + 1)], OG[:, 0:n], True, True, reads=(WOB, OGB), writes=(pyb,))
            self.dve(lambda e, fc=fc, py=py: e.scalar_tensor_tensor(
                self.X[:, fc, t0:t0 + n], py[:, 0:n], self.MOD[:, i, 2, fc, s:s + 1], self.X[:, fc, t0:t0 + n],
                ALU.mult, ALU.add), (pyb, self.MODB, self.XB), (self.XB,))


def needed_weights(nlayers):
    if nlayers == 0:
        return ()
    if nlayers == 1:
        return ("w_mod", "conv_w_pw1", "conv_w_pw2", "mlp_w1", "mlp_w2")
    return ("w_mod", "conv_w_pw1", "conv_w_pw2", "dn_w_in", "dn_w_o", "mlp_w1", "mlp_w2")


def make_inputs(inp, nlayers=DEPTH):
    maps = []
    shared = {k: np.ascontiguousarray(np.asarray(inp[k], np.float32)) for k in needed_weights(nlayers)}
    cols = None
    for core in range(NCORES):
        b, hf = core // 2, core % 2
        cp = pack_consts(inp, b, hf)
        cols = cp.cols
        xc = np.asarray(inp["ctx"][b], np.float32)
        xl = np.asarray(inp["x"][b, NLAT * hf:NLAT * (hf + 1)], np.float32)
        if hf == 1:
            xc = xc[::-1]
            xl = xl[::-1]
        xt = np.concatenate([xc, xl], axis=0)
        m = dict(shared)
        if nlayers >= 2:
            m["wab"] = make_wab(inp, hf)
        m["xT"] = np.ascontiguousarray(xt.T)
        m["consts"] = cp.array()
        maps.append(m)
    return maps, cols, maps[0]["consts"].shape[1]


def run(inp, nlayers=DEPTH):
    maps, cols, ncols = make_inputs(inp, nlayers)
    bld = Builder(nlayers, cols, ncols)
    nc = bld.build()
    res = run_bass_kernel_spmd(nc, maps, core_ids=list(range(NCORES)))
    out = np.zeros((4, SEQ, D), np.float32)
    for core in range(NCORES):
        b, hf = core // 2, core % 2
        o = res.results[core]["outT"].T
        out[b, NLAT * hf:NLAT * (hf + 1)] = o[::-1] if hf == 1 else o
    return out


def kernel(**inputs):
    inp = {k: np.asarray(v) for k, v in inputs.items()}
    return run(inp, DEPTH)
```

```python
import numpy as np
import concourse.bass as bass
import concourse.mybir as mybir
from concourse.bass_utils import run_bass_kernel_spmd

F32 = mybir.dt.float32
BF16 = mybir.dt.bfloat16
F32R = mybir.dt.float32r
FP32R = False
ALU = mybir.AluOpType
AF = mybir.ActivationFunctionType

D = 1024
NCH = 8
NCTX = 256
NLAT = 2048
T = NCTX + NLAT
SEQ = 4096
DEPTH = 4
EPS = 1e-6
KW = 31
PADW = 15
NCORES = 8
DEBUG_TAGS = False
TAGMAP = {}


class Buf:
    __slots__ = ("name", "last_w", "readers")

    def __init__(self, name):
        self.name = name
        self.last_w = None
        self.readers = []


class Op:
    __slots__ = ("eng", "fn", "deps", "needs_inc", "idx", "dma_key", "dma_val", "tag")

    def __init__(self, eng, fn):
        self.eng = eng
        self.fn = fn
        self.deps = []
        self.needs_inc = False
        self.idx = 0
        self.dma_key = None
        self.dma_val = 0


class Prog:
    ENGS = ("pe", "act", "dve", "pool", "sp")

    def __init__(self):
        self.ops = {e: [] for e in self.ENGS}
        self.dma_count = {}
        self.dma_total_keys = set()
        self.last_dma = {}
        self.dma_inc = {}
        self.fence_deps = []

    def fence(self):
        f = []
        for e in self.ENGS:
            for o in reversed(self.ops[e]):
                if o.dma_key is None:
                    o.needs_inc = True
                    f.append(o)
                    break
        f.extend(self.last_dma.values())
        self.fence_deps = f

    def _add_dep(self, op, dep, war=False):
        if dep is None or dep is op:
            return
        if dep.dma_key is None and dep.eng == op.eng:
            if op.eng == "pe":
                return
        if dep.dma_key is None:
            dep.needs_inc = True
        op.deps.append(dep)

    def op(self, eng, fn, reads=(), writes=()):
        o = Op(eng, fn)
        if DEBUG_TAGS:
            import sys as _sys
            f = _sys._getframe(1)
            tg = []
            while f is not None and len(tg) < 4:
                tg.append(f.f_lineno)
                f = f.f_back
            o.tag = tg
        for dep in self.fence_deps:
            if dep.dma_key is not None or dep.eng != eng or eng != "pe":
                o.deps.append(dep)
        for b in reads:
            self._add_dep(o, b.last_w)
        for b in writes:
            self._add_dep(o, b.last_w)
            for r in b.readers:
                self._add_dep(o, r, war=True)
        for b in reads:
            b.readers.append(o)
        for b in writes:
            b.last_w = o
            b.readers = []
        self.ops[eng].append(o)
        return o

    def dma(self, queue, key, fn, reads=(), writes=(), wait_total=False, indep=False, inc=16):
        o = self.op(queue, fn, reads, writes)
        self.dma_inc[key] = inc
        if wait_total or indep:
            o.deps = [d for d in o.deps if d.dma_key != key]
        n = self.dma_count.get(key, 0) + 1
        self.dma_count[key] = n
        o.dma_key = key
        o.dma_val = inc * n
        self.last_dma[key] = o
        if wait_total:
            self.dma_total_keys.add(key)
        return o

    def emit(self, nc, block, sems, final_waits=()):
        for e in self.ENGS:
            c = 0
            for o in self.ops[e]:
                if o.dma_key is None and o.needs_inc:
                    c += 1
                    o.idx = c

        def token(dep):
            if dep.dma_key is not None:
                if dep.dma_key in self.dma_total_keys:
                    return dep.dma_key, self.dma_inc[dep.dma_key] * self.dma_count[dep.dma_key]
                return dep.dma_key, dep.dma_val
            return dep.eng, dep.idx

        def run(e, eng):
            known = {}
            for o in self.ops[e]:
                for dep in o.deps:
                    k, v = token(dep)
                    if known.get(k, 0) >= v:
                        continue
                    eng.wait_ge(sems[k], v)
                    known[k] = v
                ins = o.fn(eng)
                if DEBUG_TAGS:
                    try:
                        TAGMAP[ins.ins.name] = o.tag
                    except Exception:
                        pass
                if o.dma_key is not None:
                    ins.then_inc(sems[o.dma_key], self.dma_inc[o.dma_key])
                elif o.needs_inc:
                    ins.then_inc(sems[e], 1)
            for k in final_waits.get(e, ()) if isinstance(final_waits, dict) else ():
                eng.wait_ge(sems[k], 16 * self.dma_count[k])

        @block.tensor
        def _(eng):
            run("pe", eng)

        @block.scalar
        def _(eng):
            run("act", eng)

        @block.vector
        def _(eng):
            run("dve", eng)

        @block.gpsimd
        def _(eng):
            run("pool", eng)

        @block.sync
        def _(eng):
            run("sp", eng)


def fm(vec):
    v = np.asarray(vec, np.float32).reshape(-1, 128)
    return np.ascontiguousarray(v.T)


class ConstPack:
    def __init__(self):
        self.cols = {}
        self.n = 0
        self.parts = []

    def add(self, name, arr):
        arr = np.asarray(arr, np.float32)
        assert arr.shape[0] == 128
        arr = arr.reshape(128, -1)
        self.cols[name] = (self.n, arr.shape[1])
        self.n += arr.shape[1]
        self.parts.append(arr)

    def array(self):
        return np.ascontiguousarray(np.concatenate(self.parts, axis=1))


def pack_consts(inp, b, hf):
    cp = ConstPack()
    cc = np.stack([fm(inp["c"][b]), fm(inp["c_ctx"])], axis=2)
    cp.add("c", cc)
    for i in range(DEPTH):
        cp.add(f"bmod{i}", fm(inp["b_mod"][i]))
        cp.add(f"n1g{i}", fm(inp["norm1_g"][i]))
        cp.add(f"n2g{i}", fm(inp["norm2_g"][i]))
    cp.add("fg", fm(inp["final_g"]))
    for j in range(2):
        cp.add(f"b1{j}", fm(inp["conv_b_pw1"][j]))
        wdw = np.asarray(inp["conv_w_dw"][j], np.float32)
        if hf == 1:
            wdw = wdw[::-1]
        w = wdw.reshape(KW, NCH, 128).transpose(2, 1, 0)
        cp.add(f"wdw{j}", w)
        cp.add(f"bdw{j}", fm(inp["conv_b_dw"][j]))
        cp.add(f"lng{j}", fm(inp["conv_ln_g"][j]))
        cp.add(f"lnb{j}", fm(inp["conv_ln_b"][j]))
        cp.add(f"b2{j}", fm(inp["conv_b_pw2"][j]))
    dirmap = (hf, 1 - hf)
    for j in range(2):
        cw = np.asarray(inp["dn_conv_w"][j], np.float32)
        if hf == 1:
            cw = cw[::-1]
        w = cw.reshape(5, 24, 128).transpose(2, 1, 0)
        cp.add(f"cw{j}", w)
        cp.add(f"ng{j}", np.asarray(inp["dn_norm_g"][j], np.float32).reshape(128, 1))
        al = np.zeros((128, 1), np.float32)
        db = np.zeros((128, 1), np.float32)
        for d in range(2):
            al[32 * d:32 * d + 8, 0] = inp["dn_a_log"][j][dirmap[d]]
            db[32 * d:32 * d + 8, 0] = inp["dn_dt_bias"][j][dirmap[d]]
        cp.add(f"alog{j}", al)
        cp.add(f"dtb{j}", db)
    pm = np.zeros((128, 2), np.float32)
    pm[:, 1 - hf] = 1.0
    cp.add("pm", pm)
    return cp


def make_wab(inp, hf):
    dirmap = (hf, 1 - hf)
    out = np.zeros((2, D, 128), np.float32)
    for j in range(2):
        w = np.asarray(inp["dn_w_in"][j], np.float32)
        for d in range(2):
            out[j, :, 32 * d:32 * d + 8] = w[:, 4096 + dirmap[d] * 8:4096 + dirmap[d] * 8 + 8]
            out[j, :, 64 + 32 * d:64 + 32 * d + 8] = w[:, 4096 + 16 + dirmap[d] * 8:4096 + 16 + dirmap[d] * 8 + 8]
    return out


SLOT_ELEMS = 4096
ARENA_BYTES = 107 * 1024


class StopBuild(Exception):
    pass


class Builder:
    dbg_stop = None

    def __init__(self, nlayers, cols, ncols, ncores=NCORES):
        self.ncores = ncores
        self.nlayers = nlayers
        self.cols = cols
        self.ncols = ncols
        self.P = Prog()
        self.nc = bass.Bass("TRN2", target_bir_lowering=False)
        self.psum_rr = 0
        self.slot_rr = 0
        self.ar_off = 0

    def phase(self, nslots):
        self.P.fence()
        self.ar_off = 0
        self.SLOT = []
        self.SLOTB = []
        for i in range(nslots):
            v, b = self.aalloc(f"SLOT{i}", [SLOT_ELEMS], BF16)
            self.SLOT.append(v)
            self.SLOTB.append(b)
        self.slot_rr = 0

    def aalloc(self, name, shape, dtype):
        n = 1
        for d_ in shape:
            n *= d_
        esz = 4 if dtype == F32 else 2
        nbytes = (n * esz + 63) // 64 * 64
        off = self.ar_off
        self.ar_off += nbytes
        assert self.ar_off <= ARENA_BYTES, (name, self.ar_off)
        if not hasattr(self, "amap"):
            self.amap = {}
        self.amap[name] = (off, tuple(shape), esz)
        v = self.ARENA[:, off // 2: off // 2 + n * esz // 2]
        if dtype == F32:
            v = v.bitcast(F32)
        if len(shape) > 1:
            names = [f"d{k}" for k in range(len(shape))]
            kw = {nm: sz for nm, sz in zip(names[:-1], shape[:-1])}
            v = v.rearrange(f"p ({' '.join(names)}) -> p {' '.join(names)}", **kw)
        return v, Buf(name)

    def cst(self, name, c0=0, n=None):
        o, w = self.cols[name]
        if n is None:
            n = w - c0
        return self.CONST[:, o + c0:o + c0 + n]

    def psum(self):
        i = self.psum_rr % len(self.PS)
        self.psum_rr += 1
        return self.PS[i], self.PSB[i]

    def tmp(self):
        i = self.tmp_rr % len(self.TMP)
        self.tmp_rr += 1
        return self.TMP[i], self.TMPB[i]

    def load_slot(self, dram_ap):
        i = self.slot_rr % len(self.SLOT)
        self.slot_rr += 1
        st, sb = self.SLOT[i], self.SLOTB[i]
        shp = dram_ap.shape
        view = st[:, 0:shp[1] * shp[2]].rearrange("p (k n) -> p k n", k=shp[1])
        self.P.dma("pool", f"slot{i}", lambda e, o=view, a=dram_ap: e.dma_start(out=o, in_=a),
                   reads=(), writes=(sb,))
        return view, sb

    def mm(self, out, lhsT, rhs, start, stop, reads, writes):
        if FP32R and lhsT.dtype == F32:
            lhsT = lhsT.bitcast(F32R)
            rhs = rhs.bitcast(F32R)
        self.P.op("pe", lambda e: e.matmul(out, lhsT, rhs, start=start, stop=stop),
                  reads=reads, writes=writes)

    def act(self, out, in_, func, reads, writes, bias=None, scale=None):
        kw = {}
        if bias is not None:
            kw["bias"] = bias
        if scale is not None:
            kw["scale"] = scale
        self.P.op("act", lambda e: e.activation(out, in_, func, **kw), reads=reads, writes=writes)

    def dve(self, fn, reads, writes):
        self.P.op("dve", fn, reads=reads, writes=writes)

    def rstd_from(self, src, srcb, n, out, outb):
        self.act(self.LNT[:, 0:n], src, AF.Ln, reads=(srcb, self.EPSB), writes=(self.LNTB,),
                 bias=self.EPST[:, 0:1])
        self.act(out[:, 0:n], self.LNT[:, 0:n], AF.Exp, reads=(self.LNTB,), writes=(outb,), scale=-0.5)

    def norm_mod(self, t0, n, Acol, Bcol, out3, outb, dst_dram=None):
        X, XB = self.X, self.XB
        self.act(self.SQ[:, :, 0:n], X[:, :, t0:t0 + n], AF.Square, reads=(XB,), writes=(self.SQB,))
        ps, psb = self.psum()
        for c in range(NCH):
            self.mm(ps[:, 0:n], self.ONES[:, :], self.SQ[:, c, 0:n], c == 0, c == NCH - 1,
                    reads=(self.SQB, self.ONESB), writes=(psb,))
        self.rstd_from(ps[:, 0:n], psb, n, self.RSTD, self.RSTDB)
        for c in range(NCH):
            tm, tmb = self.tmp()
            self.dve(lambda e, c=c, tm=tm: e.scalar_tensor_tensor(
                tm[:, 0:n], X[:, c, t0:t0 + n], Acol(c), self.RSTD[:, 0:n], ALU.mult, ALU.mult),
                reads=(XB, self.RSTDB, self.MODB, self.CONSTB), writes=(tmb,))
            if dst_dram is not None:
                self.P.dma("sp", "out_" + tmb.name, lambda e, c=c, tm=tm: e.dma_start(out=dst_dram(c), in_=tm[:, 0:n]),
                           reads=(tmb,), writes=())
            else:
                self.act(out3[:, c, 0:n], tm[:, 0:n], AF.Identity, reads=(tmb, self.MODB),
                         writes=(outb,), bias=Bcol(c))

    def compute_mod(self, i, w_mod):
        wv = w_mod[i].rearrange("(k p) n -> p k n", p=128)
        ps, psb = self.psum()
        psv = ps[:, 0:96].rearrange("p (n s) -> p n s", s=2)
        for s in range(12):
            sv, sb = self.load_slot(wv[:, :, 512 * s:512 * (s + 1)])
            for q in range(4):
                n = 4 * s + q
                for k in range(NCH):
                    self.mm(psv[:, n, :], sv[:, k, 128 * q:128 * (q + 1)], self.SC[:, k, :], k == 0, k == NCH - 1,
                            reads=(sb, self.SCB), writes=(psb,))
        o, _ = self.cols[f"bmod{i}"]
        bm = self.CONST[:, o:o + 48]
        M = self.MOD[:, i]
        for s in range(2):
            self.dve(lambda e, s=s: e.tensor_tensor(
                M[:, :, :, s], psv[:, :, s].rearrange("p (m c) -> p m c", m=6),
                bm.rearrange("p (m c) -> p m c", m=6), ALU.add),
                reads=(psb, self.CONSTB), writes=(self.MODB,))
        for m, gname in ((1, f"n1g{i}"), (4, f"n2g{i}")):
            for s in range(2):
                self.dve(lambda e, m=m, s=s, gname=gname: e.scalar_tensor_tensor(
                    M[:, m, :, s], M[:, m, :, s], 1.0, self.cst(gname), ALU.add, ALU.mult),
                    reads=(self.MODB, self.CONSTB), writes=(self.MODB,))

    def modcol(self, i, m, s):
        return lambda c: self.MOD[:, i, m, c, s:s + 1]

    def tiles(self, include_ctx=True):
        r = []
        if include_ctx:
            r.append((0, NCTX, 1))
        for k in range(NLAT // 512):
            r.append((NCTX + 512 * k, 512, 0))
        return r

    def conv_module(self, i, j, last, W):
        self.phase(6)
        HNT, HNTB = self.aalloc("HNT", [NCH, 512], BF16)
        UL, ULB = self.aalloc("UL", [NCH, 8 * (64 + 2 * PADW)], BF16)
        UC, UCB = self.aalloc("UC", [NCH, NCTX + 2 * PADW], BF16)
        DG, DGB = self.aalloc("DIAG", [KW, 128], BF16)
        CB, CBB = self.aalloc("CB", [NCH, 512], BF16)
        VT, VTB = self.aalloc("VT", [NCH, 512], BF16)
        self.MEAN, self.MEANB = self.aalloc("MEAN", [512], F32)
        self.VAR, self.VARB = self.aalloc("VAR", [512], F32)
        self.SIG, self.SIGB = self.aalloc("SIG", [512], F32)
        self.P.op("pool", lambda e: e.memset(UL, 0.0), writes=(ULB,))
        self.P.op("pool", lambda e: e.memset(UC, 0.0), writes=(UCB,))
        w1v = W["conv_w_pw1"][j].rearrange("(k p) n -> p k n", p=128)
        w2v = W["conv_w_pw2"][j].rearrange("(k p) n -> p k n", p=128)
        s1 = [self.load_slot(w1v[:, :, 512 * s:512 * (s + 1)]) for s in range(4)]
        s2 = [self.load_slot(w2v[:, :, 512 * s:512 * (s + 1)]) for s in range(2)]
        b1 = lambda c: self.cst(f"b1{j}", c, 1)
        for tile_ in self.tiles(include_ctx=not last):
            self.conv_tile(i, j, tile_, s1, s2, b1, HNT, HNTB, UL, ULB, UC, UCB, DG, DGB, CB, CBB, VT, VTB)

    def conv_tile(self, i, j, tile_, s1, s2, b1, HNT, HNTB, UL, ULB, UC, UCB, DG, DGB, CB, CBB, VT, VTB):
        if True:
            t0, n, s = tile_
            nrow = 1 if s == 1 else n // 64
            rl = n // nrow
            rs = rl + 2 * PADW
            self.norm_mod(t0, n, self.modcol(i, 1, s), self.modcol(i, 0, s), HNT, HNTB)
            U = UC if s == 1 else UL
            UB = UCB if s == 1 else ULB
            Uv = U.rearrange("p c (r w) -> p c r w", w=rs)
            for c in range(NCH):
                pa, pab = self.psum()
                sv, sb = s1[c // 4]
                for k in range(NCH):
                    self.mm(pa[:, 0:n], sv[:, k, 128 * (c % 4):128 * (c % 4 + 1)], HNT[:, k, 0:n],
                            k == 0, k == NCH - 1, reads=(sb, HNTB), writes=(pab,))
                pg, pgb = self.psum()
                sv, sb = s1[2 + c // 4]
                for k in range(NCH):
                    self.mm(pg[:, 0:n], sv[:, k, 128 * (c % 4):128 * (c % 4 + 1)], HNT[:, k, 0:n],
                            k == 0, k == NCH - 1, reads=(sb, HNTB), writes=(pgb,))
                self.act(self.SIG[:, 0:n], pg[:, 0:n], AF.Sigmoid, reads=(pgb, self.CONSTB),
                         writes=(self.SIGB,), bias=b1(NCH + c))
                self.dve(lambda e, c=c, pa=pa: e.scalar_tensor_tensor(
                    Uv[:, c, :, PADW:PADW + rl], pa[:, 0:n].rearrange("p (r w) -> p r w", w=rl), b1(c),
                    self.SIG[:, 0:n].rearrange("p (r w) -> p r w", w=rl), ALU.add, ALU.mult),
                    reads=(pab, self.SIGB, self.CONSTB), writes=(UB,))
            wo, _ = self.cols[f"wdw{j}"]
            for c in range(NCH):
                wk = self.CONST[:, wo + c * KW:wo + (c + 1) * KW]
                self.dve(lambda e, wk=wk: e.tensor_tensor(
                    DG, self.IDENT[:, None, :].to_broadcast([128, KW, 128]),
                    wk[:, :, None].to_broadcast([128, KW, 128]), ALU.mult),
                    reads=(self.IDENTB, self.CONSTB), writes=(DGB,))
                pc, pcb = self.psum()
                pcv = pc[:, 0:n].rearrange("p (r w) -> p r w", w=rl)
                for k in range(KW):
                    self.mm(pcv, DG[:, k, :], Uv[:, c, :, k:k + rl], k == 0, k == KW - 1,
                            reads=(DGB, UB), writes=(pcb,))
                bd = self.cst(f"bdw{j}", c, 1)
                self.act(CB[:, c, 0:n], pc[:, 0:n], AF.Identity, reads=(pcb, self.CONSTB),
                         writes=(CBB,), bias=bd)
                self.act(self.SQ[:, c, 0:n], pc[:, 0:n], AF.Square, reads=(pcb, self.CONSTB),
                         writes=(self.SQB,), bias=bd)
            pm, pmb = self.psum()
            for c in range(NCH):
                self.mm(pm[:, 0:n], self.ONES[:, :], CB[:, c, 0:n], c == 0, c == NCH - 1,
                        reads=(CBB, self.ONESB), writes=(pmb,))
            pq, pqb = self.psum()
            for c in range(NCH):
                self.mm(pq[:, 0:n], self.ONES[:, :], self.SQ[:, c, 0:n], c == 0, c == NCH - 1,
                        reads=(self.SQB, self.ONESB), writes=(pqb,))
            self.dve(lambda e, pm=pm: e.tensor_copy(self.MEAN[:, 0:n], pm[:, 0:n]),
                     reads=(pmb,), writes=(self.MEANB,))
            self.dve(lambda e: e.tensor_tensor(self.VAR[:, 0:n], self.MEAN[:, 0:n], self.MEAN[:, 0:n], ALU.mult),
                     reads=(self.MEANB,), writes=(self.VARB,))
            self.dve(lambda e, pq=pq: e.tensor_tensor(self.VAR[:, 0:n], pq[:, 0:n], self.VAR[:, 0:n], ALU.subtract),
                     reads=(pqb, self.VARB), writes=(self.VARB,))
            self.rstd_from(self.VAR[:, 0:n], self.VARB, n, self.RSTD, self.RSTDB)
            for c in range(NCH):
                tm, tmb = self.tmp()
                self.dve(lambda e, c=c, tm=tm: e.tensor_tensor(tm[:, 0:n], CB[:, c, 0:n], self.MEAN[:, 0:n], ALU.subtract),
                         reads=(CBB, self.MEANB), writes=(tmb,))
                self.dve(lambda e, tm=tm: e.tensor_tensor(tm[:, 0:n], tm[:, 0:n], self.RSTD[:, 0:n], ALU.mult),
                         reads=(tmb, self.RSTDB), writes=(tmb,))
                self.act(VT[:, c, 0:n], tm[:, 0:n], AF.Silu, reads=(tmb, self.CONSTB), writes=(VTB,),
                         bias=self.cst(f"lnb{j}", c, 1), scale=self.cst(f"lng{j}", c, 1))
            for fc in range(NCH):
                po, pob = self.psum()
                sv, sb = s2[fc // 4]
                for k in range(NCH):
                    self.mm(po[:, 0:n], sv[:, k, 128 * (fc % 4):128 * (fc % 4 + 1)], VT[:, k, 0:n],
                            k == 0, k == NCH - 1, reads=(sb, VTB), writes=(pob,))
                tm, tmb = self.tmp()
                self.dve(lambda e, fc=fc, po=po, tm=tm: e.tensor_scalar(
                    tm[:, 0:n], po[:, 0:n], self.cst(f"b2{j}", fc, 1), self.MOD[:, i, 2, fc, s:s + 1], ALU.add, ALU.mult),
                    reads=(pob, self.CONSTB, self.MODB), writes=(tmb,))
                self.dve(lambda e, fc=fc, tm=tm: e.tensor_tensor(
                    self.X[:, fc, t0:t0 + n], self.X[:, fc, t0:t0 + n], tm[:, 0:n], ALU.add),
                    reads=(tmb, self.XB), writes=(self.XB,))

    def mlp(self, i, last, W):
        if self.dbg_stop == ("M", i):
            raise StopBuild()
        self.phase(6)
        HN, HNB = self.aalloc("HN", [NCH, T], BF16)
        HQ, HQB = self.aalloc("HQ", [NCH, 512], BF16)
        w1v = W["mlp_w1"][i].rearrange("(k p) n -> p k n", p=128)
        w2v = W["mlp_w2"][i].rearrange("(k p) n -> p k n", p=128)
        tiles = self.tiles(include_ctx=not last)
        for (t0, n, s) in tiles:
            self.norm_mod(t0, n, self.modcol(i, 4, s), self.modcol(i, 3, s), HN[:, :, t0:t0 + n], HNB)
        for q in range(4):
            s1 = [self.load_slot(w1v[:, :, 1024 * q + 512 * s:1024 * q + 512 * (s + 1)]) for s in range(2)]
            s2 = [self.load_slot(w2v[:, 8 * q + 4 * s:8 * q + 4 * (s + 1), :]) for s in range(2)]
            for tile_ in tiles:
                self.mlp_tile(i, tile_, s1, s2, HN, HNB, HQ, HQB)

    def mlp_tile(self, i, tile_, s1, s2, HN, HNB, HQ, HQB):
        if True:
            if True:
                t0, n, s = tile_
                for hc in range(8):
                    ph, phb = self.psum()
                    sv, sb = s1[hc // 4]
                    for k in range(NCH):
                        self.mm(ph[:, 0:n], sv[:, k, 128 * (hc % 4):128 * (hc % 4 + 1)], HN[:, k, t0:t0 + n],
                                k == 0, k == NCH - 1, reads=(sb, HNB), writes=(phb,))
                    tm, tmb = self.tmp()
                    self.act(tm[:, 0:n], ph[:, 0:n], AF.Relu, reads=(phb,), writes=(tmb,))
                    self.dve(lambda e, hc=hc, tm=tm: e.tensor_tensor(HQ[:, hc, 0:n], tm[:, 0:n], tm[:, 0:n], ALU.mult),
                             reads=(tmb,), writes=(HQB,))
                for fc in range(NCH):
                    py, pyb = self.psum()
                    for hc in range(8):
                        sv, sb = s2[hc // 4]
                        self.mm(py[:, 0:n], sv[:, hc % 4, 128 * fc:128 * (fc + 1)], HQ[:, hc, 0:n],
                                hc == 0, hc == 7, reads=(sb, HQB), writes=(pyb,))
                    self.dve(lambda e, fc=fc, py=py: e.scalar_tensor_tensor(
                        self.X[:, fc, t0:t0 + n], py[:, 0:n], self.MOD[:, i, 5, fc, s:s + 1],
                        self.X[:, fc, t0:t0 + n], ALU.mult, ALU.add),
                        reads=(pyb, self.MODB, self.XB), writes=(self.XB,))

    def build(self):
        nc = self.nc
        P = self.P
        dt = nc.dram_tensor
        xT = dt("xT", [D, T], F32, kind="ExternalInput").ap()
        consts = dt("consts", [128, self.ncols], F32, kind="ExternalInput").ap()
        W = {}
        for name, shp in (("w_mod", [DEPTH, D, 6 * D]), ("conv_w_pw1", [2, D, 2 * D]), ("conv_w_pw2", [2, D, D]),
                          ("dn_w_in", [2, D, 4128]), ("dn_w_o", [2, D, D]),
                          ("mlp_w1", [DEPTH, D, 4 * D]), ("mlp_w2", [DEPTH, 4 * D, D])):
            if name in needed_weights(self.nlayers):
                W[name] = dt(name, shp, F32, kind="ExternalInput").ap()
        outT = dt("outT", [D, NLAT], F32, kind="ExternalOutput").ap()
        if self.nlayers >= 2:
            W["wab"] = dt("wab", [2, D, 128], F32, kind="ExternalInput").ap()
            self.hx_in = dt("hx_in", [128, 16], BF16)
            self.hx_out = dt("hx_out", [256, 16], BF16)
            self.st_in = dt("st_in", [128, 128], F32)
            self.st_out = dt("st_out", [256, 128], F32)
            self.HXI = Buf("hx_in"); self.HXO = Buf("hx_out"); self.STI = Buf("st_in"); self.STO = Buf("st_out")
        self.W = W

        from contextlib import ExitStack
        with ExitStack() as es:
            def sb(name, shape, dtype):
                return es.enter_context(nc.sbuf_tensor(name, shape, dtype))

            self.X = sb("X", [128, NCH, T], F32); self.XB = Buf("X")
            self.ARENA = sb("ARENA", [128, ARENA_BYTES // 2], BF16)
            self.CONST = sb("CONST", [128, self.ncols], F32); self.CONSTB = Buf("CONST")
            self.MOD = sb("MOD", [128, DEPTH, 6, NCH, 2], F32); self.MODB = Buf("MOD")
            self.SC = sb("SC", [128, NCH, 2], BF16); self.SCB = Buf("SC")
            self.ONES = sb("ONES", [128, 128], BF16); self.ONESB = Buf("ONES")
            self.IDENT = sb("IDENT", [128, 128], BF16); self.IDENTB = Buf("IDENT")
            self.IDF = sb("IDF", [128, 128], F32); self.IDFB = Buf("IDF")
            self.EPST = sb("EPST", [128, 1], F32); self.EPSB = Buf("EPS")
            for nm_ in ("ONES", "IDENT", "IDF"):
                setattr(self, nm_, getattr(self, nm_)[:, :])
            self.SQ = sb("SQ", [128, NCH, 512], BF16); self.SQB = Buf("SQ")
            self.LNT = sb("LNT", [128, 512], F32); self.LNTB = Buf("LNT")
            self.RSTD = sb("RSTD", [128, 512], F32); self.RSTDB = Buf("RSTD")
            self.ONE1 = sb("ONE1", [128, 1], F32); self.ONEB = Buf("ONE1")
            self.ONEF = sb("ONEF", [128, 128], F32); self.ONEFB = Buf("ONEF")
            self.ONESS = sb("ONESS", [128, 128], BF16); self.ONESSB = Buf("ONESS")
            self.ONES128 = sb("ONES128", [128, 128], BF16); self.ONES128B = Buf("ONES128")
            self.ML = sb("ML", [128, 128], F32); self.MLS = sb("MLS", [128, 128], F32)
            self.MU = sb("MU", [128, 128], F32); self.MUS = sb("MUS", [128, 128], F32)
            self.MASKB = Buf("MASK")
            for nm_ in ("ONEF", "ONESS", "ONES128", "ML", "MLS", "MU", "MUS", "ONE1"):
                setattr(self, nm_, getattr(self, nm_)[:, :])
            self.TMP = [sb(f"TMP{i}", [128, 512], F32) for i in range(2)]
            self.TMPB = [Buf(f"TMP{i}") for i in range(2)]
            self.tmp_rr = 0
            self.OUTB = Buf("OUT")
            self.PS = [es.enter_context(nc.psum_tensor(f"PS{i}", [128, 512], F32)) for i in range(8)]
            self.PSB = [Buf(f"PS{i}") for i in range(8)]

            P.dma("sp", "const", lambda e: e.dma_start(out=self.CONST[:, :], in_=consts[:, :]),
                  writes=(self.CONSTB,), wait_total=True)
            xv = xT.rearrange("(c p) t -> p c t", p=128)
            for c in range(NCH):
                P.dma("sp", "xin", lambda e, c=c: e.dma_start(out=self.X[:, c, :], in_=xv[:, c, :]),
                      writes=(self.XB,), wait_total=True)
            P.op("pool", lambda e: e.memset(self.ONES[:, :], 1.0 / D), writes=(self.ONESB,))
            P.op("pool", lambda e: e.memset(self.EPST[:, :], EPS), writes=(self.EPSB,))
            P.op("pool", lambda e: e.memset(self.IDF[:, :], 0.0), writes=(self.IDFB,))
            P.op("pool", lambda e: e.affine_select(self.IDF[:, :], self.IDF[:, :], pattern=[[-1, 128]],
                                                    compare_op=ALU.not_equal, fill=1.0, base=0, channel_multiplier=1),
                 reads=(self.IDFB,), writes=(self.IDFB,))
            P.op("pool", lambda e: e.tensor_copy(self.IDENT[:, :], self.IDF[:, :]), reads=(self.IDFB,), writes=(self.IDENTB,))
            P.op("pool", lambda e: e.memset(self.ONE1[:, :], 1.0), writes=(self.ONEB,))
            P.op("pool", lambda e: e.memset(self.ONEF[:, :], 1.0), writes=(self.ONEFB,))
            P.op("pool", lambda e: e.memset(self.ONESS[:, :], 1.0), writes=(self.ONESSB,))
            P.op("pool", lambda e: e.memset(self.ONES128[:, :], 1.0 / 128), writes=(self.ONES128B,))
            for mt, base, cm, pat in ((self.ML, 0, 1, -1), (self.MLS, -1, 1, -1), (self.MU, 0, -1, 1), (self.MUS, -1, -1, 1)):
                P.op("pool", lambda e, mt=mt, base=base, cm=cm, pat=pat: e.affine_select(
                    mt[:, :], self.ONEF[:, :], pattern=[[pat, 128]], compare_op=ALU.is_ge, fill=0.0,
                    base=base, channel_multiplier=cm), reads=(self.ONEFB,), writes=(self.MASKB,))
            self.BD = sb("BD", [128, 128], F32)[:, :]
            P.op("pool", lambda e: e.memset(self.BD, 0.0), writes=(self.MASKB,))
            P.op("pool", lambda e: e.memset(self.BD[0:64, 0:64], 1.0), writes=(self.MASKB,))
            P.op("pool", lambda e: e.memset(self.BD[64:128, 64:128], 1.0), writes=(self.MASKB,))
            for mt in (self.ML, self.MLS, self.MU, self.MUS):
                P.op("pool", lambda e, mt=mt: e.tensor_tensor(mt, mt, self.BD, ALU.mult),
                     reads=(self.MASKB,), writes=(self.MASKB,))
            co, _ = self.cols["c"]
            self.act(self.SC[:, :, :].rearrange("p k s -> p (k s)"), self.CONST[:, co:co + 16], AF.Silu,
                     reads=(self.CONSTB,), writes=(self.SCB,))
            self.phase(6)
            for i in range(self.nlayers):
                self.compute_mod(i, W["w_mod"])

            try:
                for i in range(self.nlayers):
                    last = i == self.nlayers - 1
                    j = i // 2
                    if i % 2 == 0:
                        self.conv_module(i, j, last, W)
                    else:
                        self.delta_module(i, j, last, W)
                    self.mlp(i, last, W)
            except StopBuild:
                pass

            ov = outT.rearrange("(c p) t -> p c t", p=128)
            for (t0, n, s) in self.tiles(include_ctx=False):
                fgc = lambda c: self.cst("fg", c, 1)
                self.norm_mod(t0, n, fgc, None, None, None,
                              dst_dram=lambda c, t0=t0, n=n: ov[:, c, t0 - NCTX:t0 - NCTX + n])

            keys = list(self.P.dma_count.keys())
            sems = {}
            for k in list(Prog.ENGS) + keys:
                sems[k] = es.enter_context(nc.semaphore(f"s_{k}"))
            block = es.enter_context(nc.Block())
            self.P.emit(nc, block, sems, final_waits={"sp": tuple(k for k in keys if k.startswith("out_"))})
        return nc

    def exchange(self, key, src_ap, src_b, din, din_b, dout, dout_b, dst_ap, dst_b, din_view=None):
        P = self.P
        dv = din[:, :] if din_view is None else din_view
        P.dma("sp", key + "_a", lambda e: e.dma_start(out=dv, in_=src_ap), reads=(src_b,), writes=(din_b,))
        groups = [[2 * k, 2 * k + 1] for k in range(self.ncores // 2)]
        P.dma("pool", key + "_c", lambda e: e.collective_compute(
            "AllGather", ALU.bypass, replica_groups=groups, ins=[din.ap().opt()], outs=[dout.ap().opt()]),
            reads=(din_b,), writes=(dout_b,), inc=1)
        P.dma("sp", key + "_b", lambda e: e.dma_start(
            out=dst_ap, in_=dout.ap().rearrange("(r p) n -> p r n", p=128)), reads=(dout_b,), writes=(dst_b,))

    def delta_module(self, i, j, last, W):
        P = self.P
        self.phase(0)
        A = self.aalloc
        NCK = T // 128
        PT_ = 2312
        pcol = lambda t: t + 2 if t < NCTX else t + 6
        WIN, WINB = A("WIN", [8, 512], BF16)
        WO, WOB = A("WO", [1024], BF16)
        WAB, WABB = WIN[:, :, 0:128], WINB
        HNP, HNPB = A("HNP", [8, PT_], BF16)
        G, GB = A("G", [T], F32)
        O, OB = A("O", [T], F32)
        COLS, COLSB = A("COLS", [NCK, 32], F32)
        HLB, HLBB = A("HLB", [NCK, 16], F32)
        KDF, KDFB = A("KDF", [NCK, 16], F32)
        CDA, CDAB = A("CDA", [NCK, 16], F32)
        CDB_, CDBB = A("CDB", [NCK, 16], F32)
        NBE, NBEB = A("NBE", [NCK, 16], F32)
        CD = (CDA, CDB_)
        CDB = (CDAB, CDBB)
        NB, NBB = HLB, HLBB
        TOT, TOTB = A("TOT", [2 * NCK], F32)
        EAL, EALB = A("EAL", [1], F32)
        SELC, SELCB = A("SELC", [32], F32)
        QT, QTB = A("QT", [T], BF16)
        KF, KFB = A("KF", [T], F32)
        VTM, VTMB = A("VTM", [NCK, 128], BF16)
        PRET, PRETB = A("PRET", [260], BF16)
        VTT, VTTB = A("VTT", [256], BF16)
        DG5, DG5B = A("DG5", [5, 128], BF16)
        SELR, SELRB = A("SELR", [4, 128], F32)
        HXG, HXGB = A("HXG", [2, 16], BF16)
        HXS, HXSB = A("HXS", [16], F32)
        SG, SGB = A("SG", [2, 128], F32)
        S, SB_ = A("S", [128], F32)
        SBF, SBFB = A("SBF", [128], BF16)
        OG, OGB = A("OG", [512], BF16)
        f32t = {}
        for nm in ("E1", "E2", "EROW", "M2I", "P", "PT", "Z"):
            nb_ = 1 if nm in ("E1", "E2", "EROW", "M2I") else 2
            bufs_ = [A(nm + str(k), [128], F32) for k in range(nb_)]
            f32t[nm] = bufs_ * (2 // nb_)
        b16t = {}
        for nm in ("KB", "QKTM", "BV", "KD", "QG", "VN", "YB", "ZB"):
            nb_ = 1 if nm in ("VN", "YB") else 2
            bufs_ = [A(nm + str(k), [128], BF16) for k in range(nb_)]
            b16t[nm] = bufs_ * (2 // nb_)
        rr = {}

        def nxt(d, nm):
            k = rr.get(nm, 0)
            rr[nm] = k + 1
            return d[nm][k % 2]

        def dve(fn, reads, writes):
            P.op("dve", fn, reads=reads, writes=writes)

        P.op("pool", lambda e: e.memset(HNP[:, :, 0:2], 0.0), writes=(HNPB,))
        P.op("pool", lambda e: e.memset(HNP[:, :, 258:262], 0.0), writes=(HNPB,))
        for (t0, n, s) in self.tiles():
            self.norm_mod(t0, n, self.modcol(i, 1, s), self.modcol(i, 0, s), HNP[:, :, pcol(t0):pcol(t0) + n], HNPB)
        self.exchange(f"hx", HNP[:, :, 2308:2310], HNPB, self.hx_in, self.HXI, self.hx_out, self.HXO,
                      HXG, HXGB, din_view=self.hx_in.ap().rearrange("p (k t) -> p k t", t=2))
        pm = lambda r: self.cst("pm", r, 1)
        dve(lambda e: e.tensor_scalar(HXS, HXG[:, 0, :], pm(0), None, ALU.mult), (HXGB, self.CONSTB), (HXSB,))
        dve(lambda e: e.scalar_tensor_tensor(HXS, HXG[:, 1, :], pm(1), HXS, ALU.mult, ALU.add),
            (HXGB, HXSB, self.CONSTB), (HXSB,))
        hv = HXS.rearrange("p (k t) -> p k t", t=2)
        dve(lambda e: e.tensor_copy(HNP[:, :, 2310:2311], hv[:, :, 1:2]), (HXSB,), (HNPB,))
        dve(lambda e: e.tensor_copy(HNP[:, :, 2311:2312], hv[:, :, 0:1]), (HXSB,), (HNPB,))

        P.dma("pool", "wab", lambda e: e.dma_start(out=WAB, in_=W["wab"][j].rearrange("(k p) n -> p k n", p=128)),
              writes=(WABB,))
        for q in range(4):
            dve(lambda e, q=q: e.tensor_copy(SELC[:, 8 * q:8 * q + 8], self.IDF[:, 32 * q:32 * q + 8]),
                (self.IDFB,), (SELCB,))
        self.act(EAL, self.cst(f"alog{j}"), AF.Exp, reads=(self.CONSTB,), writes=(EALB,))
        for (t0, n, s) in self.tiles():
            ps, psb = self.psum()
            for k in range(NCH):
                self.mm(ps[:, 0:n], WAB[:, k, :], HNP[:, k, pcol(t0):pcol(t0) + n], k == 0, k == NCH - 1,
                        reads=(WABB, HNPB), writes=(psb,))
            self.act(O[0:64, t0:t0 + n], ps[0:64, 0:n], AF.Exp, reads=(psb, self.CONSTB), writes=(OB,),
                     bias=self.cst(f"dtb{j}")[0:64, :])
            self.act(O[0:64, t0:t0 + n], O[0:64, t0:t0 + n], AF.Ln, reads=(OB, self.ONEB), writes=(OB,),
                     bias=self.ONE1[0:64, :])
            dve(lambda e, t0=t0, n=n: e.tensor_scalar(O[0:64, t0:t0 + n], O[0:64, t0:t0 + n], EAL[0:64, :], None, ALU.mult),
                (OB, EALB), (OB,))
            self.act(G[64:128, t0:t0 + n], ps[64:128, 0:n], AF.Sigmoid, reads=(psb,), writes=(GB,))
        for c in range(2 * NCK):
            dve(lambda e, c=c: e.tensor_tensor_scan(G[0:64, 64 * c:64 * (c + 1)], self.ONEF[0:64, 0:64],
                                                    O[0:64, 64 * c:64 * (c + 1)], 0.0, ALU.mult, ALU.add),
                (OB, self.ONEFB), (GB,))
        Gv = G[32:64, :].rearrange("p (c w) -> p c w", w=64)
        dve(lambda e: e.tensor_copy(TOT[32:64, :], Gv[:, :, 63]), (GB,), (TOTB,))
        dve(lambda e: e.tensor_tensor(Gv, TOT[32:64, :, None].to_broadcast([32, 2 * NCK, 64]), Gv, ALU.subtract),
            (GB, TOTB), (GB,))
        dve(lambda e: e.tensor_tensor(G[32:64, :], G[32:64, :], O[32:64, :], ALU.add), (GB, OB), (GB,))
        for c in range(NCK):
            ps, psb = self.psum()
            self.mm(ps[:, 0:32], G[:, 128 * c:128 * (c + 1)], SELC, True, True, reads=(GB, SELCB), writes=(psb,))
            self.act(COLS[:, c, :], ps[:, 0:32], AF.Identity, reads=(psb,), writes=(COLSB,))
        G3 = lambda lo, hi: G[lo:hi, :].rearrange("p (c w) -> p c w", w=128)
        for sub, (RH, RHB, HL, HLB_) in enumerate(((KDF, KDFB, CDA, CDAB), (NBE, NBEB, CDB_, CDBB))):
            P.op("pool", lambda e, RH=RH: e.memset(RH, 0.0), writes=(RHB,))
            tf_ = 63 + 64 * sub
            tb_ = 64 * sub
            dve(lambda e, RH=RH, tf_=tf_: e.tensor_tensor(
                RH[0:32], SELC[0:32, None, 0:16].to_broadcast([32, NCK, 16]),
                G3(0, 32)[:, :, tf_:tf_ + 1].to_broadcast([32, NCK, 16]), ALU.mult), (SELCB, GB, RHB), (RHB,))
            dve(lambda e, RH=RH, tb_=tb_: e.tensor_tensor(
                RH[32:64], SELC[32:64, None, 0:16].to_broadcast([32, NCK, 16]),
                G3(32, 64)[:, :, tb_:tb_ + 1].to_broadcast([32, NCK, 16]), ALU.mult), (SELCB, GB, RHB), (RHB,))
            ps, psb = self.psum()
            self.mm(ps[:, 0:NCK * 16], self.ONEF, RH.rearrange("p c n -> p (c n)"), True, True,
                    reads=(RHB, self.ONEFB), writes=(psb,))
            self.act(HL.rearrange("p c n -> p (c n)"), ps[:, 0:NCK * 16], AF.Identity, reads=(psb,), writes=(HLB_,))
            hs = slice(64 * sub, 64 * sub + 64)
            dve(lambda e, HL=HL, hs=hs: e.tensor_copy(HLB[hs], HL[hs]), (HLB_,), (HLBB,))
        self.act(CDA, CDA, AF.Exp, reads=(CDAB, HLBB), writes=(CDAB,), scale=-1.0)
        self.act(CDB_, CDB_, AF.Exp, reads=(CDBB, HLBB), writes=(CDBB,), scale=-1.0)
        dve(lambda e: e.tensor_tensor(KDF, COLS[:, :, 0:16], HLB, ALU.subtract), (COLSB, HLBB), (KDFB,))
        self.act(KDF, KDF, AF.Exp, reads=(KDFB,), writes=(KDFB,))
        self.act(NBE, COLS[:, :, 0:16], AF.Exp, reads=(COLSB,), writes=(NBEB,), scale=-1.0)
        dve(lambda e: e.scalar_tensor_tensor(NBE, COLS[:, :, 16:32], -1.0, NBE, ALU.mult, ALU.mult),
            (COLSB, NBEB), (NBEB,))
        dve(lambda e: e.tensor_scalar(NB, COLS[:, :, 16:32], -1.0, None, ALU.mult), (COLSB,), (NBB,))

        w_in = W["dn_w_in"][j]
        wq = w_in[:, 0:4096].rearrange("(k p) (g hh d) -> p k g hh d", p=128, g=4, hh=8)
        cwo, _ = self.cols[f"cw{j}"]
        QSC = float(128 ** -0.5)

        for h in range(8):
            self.delta_head(i, j, h, last, locals())

    def delta_head(self, i, j, h, last, L):
        P = self.P
        g_ = lambda k: L[k]
        (WIN, WINB, WO, WOB, HNP, HNPB, G, GB, O, OB, COLS, COLSB, KDF, KDFB, CD, CDB, NBE, NBEB, NB, NBB,
         QT, QTB, KF, KFB, VTM, VTMB, PRET, PRETB, VTT, VTTB, DG5, DG5B, SELR, SELRB, SG, SGB, S, SB_,
         SBF, SBFB, OG, OGB) = [g_(k) for k in (
            "WIN", "WINB", "WO", "WOB", "HNP", "HNPB", "G", "GB", "O", "OB", "COLS", "COLSB", "KDF", "KDFB", "CD", "CDB",
            "NBE", "NBEB", "NB", "NBB", "QT", "QTB", "KF", "KFB", "VTM", "VTMB", "PRET", "PRETB", "VTT", "VTTB",
            "DG5", "DG5B", "SELR", "SELRB", "SG", "SGB", "S", "SB_", "SBF", "SBFB", "OG", "OGB")]
        f32t, b16t, nxt, dve, pcol, wq, cwo, QSC, NCK, W = (g_(k) for k in (
            "f32t", "b16t", "nxt", "dve", "pcol", "wq", "cwo", "QSC", "NCK", "W"))
        for g in range(4):
            P.dma("pool", f"win{g}", lambda e, g=g: e.dma_start(out=WIN[:, :, 128 * g:128 * (g + 1)], in_=wq[:, :, g, h, :]),
                  writes=(WINB,), indep=False)
        P.dma("pool", "wo", lambda e: e.dma_start(out=WO, in_=W["dn_w_o"][j][128 * h:128 * (h + 1), :]), writes=(WOB,))
        for q in range(4):
            r = 32 * q + h
            dve(lambda e, q=q, r=r: e.tensor_copy(SELR[:, q, :], self.IDF[:, r:r + 1].to_broadcast([128, 128])),
                (self.IDFB,), (SELRB,))
        segs = [(0, 0)] + [(260 + 256 * k, NCTX + 256 * k) for k in range(NLAT // 256)]
        for g in (0, 1, 2):
            wk = self.CONST[:, cwo + (8 * g + h) * 5:cwo + (8 * g + h + 1) * 5]
            dve(lambda e, wk=wk: e.tensor_tensor(
                DG5, self.IDENT[:, None, :].to_broadcast([128, 5, 128]),
                wk[:, :, None].to_broadcast([128, 5, 128]), ALU.mult), (self.IDENTB, self.CONSTB), (DG5B,))
            for (pc0, tk0) in segs:
                ps, psb = self.psum()
                for k in range(NCH):
                    self.mm(ps[:, 0:260], WIN[:, k, 128 * g:128 * (g + 1)], HNP[:, k, pc0:pc0 + 260], k == 0, k == NCH - 1,
                            reads=(WINB, HNPB), writes=(psb,))
                self.act(PRET, ps[:, 0:260], AF.Identity, reads=(psb,), writes=(PRETB,))
                p2, p2b = self.psum()
                for tap in range(5):
                    self.mm(p2[:, 0:256], DG5[:, tap, :], PRET[:, tap:tap + 256], tap == 0, tap == 4,
                            reads=(DG5B, PRETB), writes=(p2b,))
                if g < 2:
                    tm, tmb = self.tmp()
                    self.act(tm[:, 0:256], p2[:, 0:256], AF.Silu, reads=(p2b,), writes=(tmb,))
                    self.act(self.SQ[:, 0, 0:256], tm[:, 0:256], AF.Square, reads=(tmb,), writes=(self.SQB,))
                    p3, p3b = self.psum()
                    self.mm(p3[:, 0:256], self.ONESS, self.SQ[:, 0, 0:256], True, True, reads=(self.SQB, self.ONESSB), writes=(p3b,))
                    self.rstd_from(p3[:, 0:256], p3b, 256, self.RSTD, self.RSTDB)
                    dst_, dstb_ = (QT, QTB) if g == 0 else (KF, KFB)
                    dve(lambda e, g=g, tm=tm, tk0=tk0, dst_=dst_: e.scalar_tensor_tensor(
                        dst_[:, tk0:tk0 + 256], tm[:, 0:256], QSC if g == 0 else 1.0, self.RSTD[:, 0:256], ALU.mult, ALU.mult),
                        (tmb, self.RSTDB), (dstb_,))
                else:
                    self.act(VTT[:, 0:256], p2[:, 0:256], AF.Silu, reads=(p2b,), writes=(VTTB,))
                    for cc in range(2):
                        c = tk0 // 128 + cc
                        p4, p4b = self.psum()
                        self.mm(p4[:, 0:128], VTT[:, 128 * cc:128 * (cc + 1)], self.IDENT, True, True,
                                reads=(VTTB, self.IDENTB), writes=(p4b,))
                        self.act(VTM[:, c, :], p4[:, 0:128], AF.Identity, reads=(p4b,), writes=(VTMB,))

        def pre(d, c, si):
            n = 8 * d + h
            ck = slice(128 * c, 128 * (c + 1))
            col = lambda Tn, off=0: Tn[:, c, off + n:off + n + 1]
            (E1, E1B), (E2, E2B), (EROW, EROWB), (M2I, M2IB) = (nxt(f32t, "E1"), nxt(f32t, "E2"),
                                                               nxt(f32t, "EROW"), nxt(f32t, "M2I"))
            hr, hrb = self.psum()
            self.mm(hr[:, 0:128], SELR[:, d, :], G[:, ck], True, True, reads=(SELRB, GB), writes=(hrb,))
            yield
            br, brb = self.psum()
            self.mm(br[:, 0:128], SELR[:, 2 + d, :], G[:, ck], True, True, reads=(SELRB, GB), writes=(brb,))
            yield
            if d == 0:
                mDs, mTs, mTi = self.MLS, self.MUS, self.MU
            else:
                mDs, mTs, mTi = self.MUS, self.MLS, self.ML
            dve(lambda e: e.tensor_scalar(E1, hr[:, 0:128], col(COLS), 0.0, ALU.subtract, ALU.min), (hrb, COLSB), (E1B,))
            self.act(E1, E1, AF.Exp, reads=(E1B,), writes=(E1B,))
            dve(lambda e: e.tensor_tensor(E1, E1, mDs, ALU.mult), (E1B, self.MASKB), (E1B,))
            dve(lambda e: e.tensor_scalar(E2, hr[:, 0:128], col(COLS), 0.0, ALU.subtract, ALU.max), (hrb, COLSB), (E2B,))
            self.act(E2, E2, AF.Exp, reads=(E2B,), writes=(E2B,), scale=-1.0)
            dve(lambda e: e.tensor_tensor(M2I, E2, mTi, ALU.mult), (E2B, self.MASKB), (M2IB,))
            dve(lambda e: e.tensor_tensor(E2, E2, mTs, ALU.mult), (E2B, self.MASKB), (E2B,))
            self.act(EROW, hr[:, 0:128], AF.Exp, reads=(hrb,), writes=(EROWB,), scale=-1.0)
            KB, KBB = b16t["KB"][si]
            self.act(KB, KF[:, ck], AF.Identity, reads=(KFB,), writes=(KBB,))
            kk, kkb = self.psum()
            self.mm(kk[:, 0:128], KF[:, ck], KF[:, ck], True, True, reads=(KFB,), writes=(kkb,))
            yield
            Pm, PmB = nxt(f32t, "P")
            PTm, PTmB = nxt(f32t, "PT")
            Zm, ZmB = nxt(f32t, "Z")
            dve(lambda e, Pm=Pm: e.scalar_tensor_tensor(Pm, kk[:, 0:128], col(NB), E1, ALU.mult, ALU.mult), (kkb, NBB, E1B), (PmB,))
            dve(lambda e: e.scalar_tensor_tensor(E2, kk[:, 0:128], -1.0, E2, ALU.mult, ALU.mult), (kkb, E2B), (E2B,))
            dve(lambda e, PTm=PTm: e.tensor_tensor(PTm, E2, br[:, 0:128], ALU.mult), (E2B, brb), (PTmB,))
            qk_, qkb_ = self.psum()
            self.mm(qk_[:, 0:128], KB, QT[:, ck], True, True, reads=(KBB, QTB), writes=(qkb_,))
            yield
            QKTM, QKTMB = b16t["QKTM"][si]
            dve(lambda e: e.tensor_tensor(QKTM, qk_[:, 0:128], M2I, ALU.mult), (qkb_, M2IB), (QKTMB,))
            dve(lambda e, Zm=Zm, PTm=PTm: e.tensor_tensor(Zm, self.IDF, PTm, ALU.add), (self.IDFB, PTmB), (ZmB,))
            for m in range(1, 6):
                Pn, PnB = nxt(f32t, "P")
                pp, ppb = self.psum()
                self.mm(pp[:, 0:128], PTm, Pm, True, True, reads=(PTmB, PmB), writes=(ppb,))
                yield
                self.act(Pn, pp[:, 0:128], AF.Identity, reads=(ppb,), writes=(PnB,))
                if m < 5:
                    PTn, PTnB = nxt(f32t, "PT")
                    pt, ptb = self.psum()
                    self.P.op("pe", lambda e, pt=pt, Pn=Pn: e.transpose(pt[:, 0:128], Pn, self.IDF),
                              reads=(PnB, self.IDFB), writes=(ptb,))
                    yield
                    self.act(PTn, pt[:, 0:128], AF.Identity, reads=(ptb,), writes=(PTnB,))
                Zn, ZnB = nxt(f32t, "Z")
                pz, pzb = self.psum()
                self.mm(pz[:, 0:128], Pn, Zm, True, True, reads=(PnB, ZmB), writes=(pzb,))
                yield
                dve(lambda e, Zn=Zn, Zm=Zm, pz=pz: e.tensor_tensor(Zn, pz[:, 0:128], Zm, ALU.add), (pzb, ZmB), (ZnB,))
                Pm, PmB = Pn, PnB
                if m < 5:
                    PTm, PTmB = PTn, PTnB
                Zm, ZmB = Zn, ZnB
            BV, BVB = b16t["BV"][si]
            KD, KDB = b16t["KD"][si]
            QG, QGB = b16t["QG"][si]
            dve(lambda e: e.tensor_scalar(BV, VTM[:, c, :], col(COLS, 16), None, ALU.mult), (VTMB, COLSB), (BVB,))
            p4, p4b = self.psum()
            self.mm(p4[:, 0:128], KB, self.IDENT, True, True, reads=(KBB, self.IDENTB), writes=(p4b,))
            yield
            dve(lambda e: e.tensor_scalar(KD, p4[:, 0:128], col(KDF), None, ALU.mult), (p4b, KDFB), (KDB,))
            dve(lambda e: e.tensor_tensor(QG, QT[:, ck], EROW, ALU.mult), (QTB, EROWB), (QGB,))
            ZB, ZBB = b16t["ZB"][si]
            self.act(ZB, Zm, AF.Identity, reads=(ZmB,), writes=(ZBB,))
            yield

        def seq(d, c, si, first_touch):
            n = 8 * d + h
            ck = slice(128 * c, 128 * (c + 1))
            col = lambda Tn, off=0: Tn[:, c, off + n:off + n + 1]
            (KB, KBB), (QKTM, QKTMB), (BV, BVB), (KD, KDB), (QG, QGB), (ZB, ZBB) = (
                b16t["KB"][si], b16t["QKTM"][si], b16t["BV"][si], b16t["KD"][si], b16t["QG"][si], b16t["ZB"][si])
            Y, YB = nxt(b16t, "YB")
            VN, VNB = nxt(b16t, "VN")
            for a_ in ((0, 1) if d == 0 else (1, 0)):
                hs = slice(64 * a_, 64 * a_ + 64)
                tk = slice(128 * c + 64 * a_, 128 * c + 64 * a_ + 64)
                ks, ksb = self.psum()
                self.mm(ks[:, 0:128], KB, SBF, True, True, reads=(KBB, SBFB), writes=(ksb,))
                yield
                dve(lambda e, ks=ks: e.scalar_tensor_tensor(Y, ks[:, 0:128], col(NBE), BV, ALU.mult, ALU.add),
                    (ksb, NBEB, BVB), (YB,))
                vn, vnb = self.psum()
                self.mm(vn[:, 0:128], ZB, Y, True, True, reads=(ZBB, YB), writes=(vnb,))
                yield
                self.act(VN, vn[:, 0:128], AF.Identity, reads=(vnb,), writes=(VNB,))
                ot, otb = self.psum()
                self.mm(ot[:, 0:64], SBF, QG[:, hs], True, False, reads=(SBFB, QGB), writes=(otb,))
                yield
                self.mm(ot[:, 0:64], VN, QKTM[:, hs], False, True, reads=(VNB, QKTMB), writes=(otb,))
                yield
                if first_touch:
                    dve(lambda e, ot=ot, tk=tk: e.tensor_copy(O[:, tk], ot[:, 0:64]), (otb,), (OB,))
                else:
                    dve(lambda e, ot=ot, tk=tk: e.tensor_tensor(O[:, tk], O[:, tk], ot[:, 0:64], ALU.add), (otb, OB), (OB,))
                sn, snb = self.psum()
                self.mm(sn[:, 0:128], KD[hs, :], VN[hs, :], True, True, reads=(KDB, VNB), writes=(snb,))
                yield
                dve(lambda e, sn=sn, a_=a_: e.scalar_tensor_tensor(S, S, col(CD[a_]), sn[:, 0:128], ALU.mult, ALU.add),
                    (SB_, CDB[a_], snb), (SB_,))
                self.act(SBF, S, AF.Identity, reads=(SB_,), writes=(SBFB,))


        def interleave(gens):
            gens = [g for g in gens if g is not None]
            while gens:
                for g in list(gens):
                    try:
                        next(g)
                    except StopIteration:
                        gens.remove(g)

        def run_scan(d, cs, first_touch):
            for k, c in enumerate(cs):
                interleave([pre(d, c, k % 2), seq(d, cs[k - 1], (k - 1) % 2, first_touch) if k > 0 else None])
            interleave([seq(d, cs[-1], (len(cs) - 1) % 2, first_touch)])

        def zero_state():
            P.op("pool", lambda e: e.memset(S, 0.0), writes=(SB_,))
            P.op("pool", lambda e: e.memset(SBF, 0.0), writes=(SBFB,))

        zero_state()
        run_scan(0, list(range(NCK)), True)
        if self.dbg_stop == ("A", h):
            raise StopBuild()
        self.exchange("st", S, SB_, self.st_in, self.STI, self.st_out, self.STO, SG, SGB)
        pm = lambda r: self.cst("pm", r, 1)
        dve(lambda e: e.tensor_scalar(S, SG[:, 0, :], pm(0), None, ALU.mult), (SGB, self.CONSTB), (SB_,))
        dve(lambda e: e.scalar_tensor_tensor(S, SG[:, 1, :], pm(1), S, ALU.mult, ALU.add), (SGB, SB_, self.CONSTB), (SB_,))
        self.act(SBF, S, AF.Identity, reads=(SB_,), writes=(SBFB,))
        run_scan(1, list(range(NCK - 1, 1, -1)), False)
        if not last:
            zero_state()
            run_scan(1, [1, 0], False)
        if self.dbg_stop == ("B", h):
            raise StopBuild()
        for (t0, n, s) in self.tiles(include_ctx=not last):
            self.head_out(i, j, t0, n, s, WIN, WINB, WO, WOB, HNP, HNPB, O, OB, OG, OGB, pcol)
        if self.dbg_stop == ("H", h):
            raise StopBuild()

    def head_out(self, i, j, t0, n, s, WIN, WINB, WO, WOB, HNP, HNPB, O, OB, OG, OGB, pcol):
        self.act(self.SQ[:, 0, 0:n], O[:, t0:t0 + n], AF.Square, reads=(OB,), writes=(self.SQB,))
        ps, psb = self.psum()
        self.mm(ps[:, 0:n], self.ONES128, self.SQ[:, 0, 0:n], True, True, reads=(self.SQB, self.ONES128B), writes=(psb,))
        self.rstd_from(ps[:, 0:n], psb, n, self.RSTD, self.RSTDB)
        zp, zpb = self.psum()
        for k in range(NCH):
            self.mm(zp[:, 0:n], WIN[:, k, 384:512], HNP[:, k, pcol(t0):pcol(t0) + n], k == 0, k == NCH - 1,
                    reads=(WINB, HNPB), writes=(zpb,))
        zs, zsb = self.tmp()
        self.act(zs[:, 0:n], zp[:, 0:n], AF.Silu, reads=(zpb,), writes=(zsb,))
        t1, t1b = self.tmp()
        self.dve(lambda e: e.scalar_tensor_tensor(t1[:, 0:n], O[:, t0:t0 + n], self.cst(f"ng{j}"), self.RSTD[:, 0:n],
                                                  ALU.mult, ALU.mult), (OB, self.RSTDB, self.CONSTB), (t1b,))
        self.dve(lambda e: e.tensor_tensor(OG[:, 0:n], t1[:, 0:n], zs[:, 0:n], ALU.mult), (t1b, zsb), (OGB,))
        for fc in range(NCH):
            py, pyb = self.psum()
            self.mm(py[:, 0:n], WO[:, 128 * fc:128 * (fc + 1)], OG[:, 0:n], True, True, reads=(WOB, OGB), writes=(pyb,))
            self.dve(lambda e, fc=fc, py=py: e.scalar_tensor_tensor(
                self.X[:, fc, t0:t0 + n], py[:, 0:n], self.MOD[:, i, 2, fc, s:s + 1], self.X[:, fc, t0:t0 + n],
                ALU.mult, ALU.add), (pyb, self.MODB, self.XB), (self.XB,))


def needed_weights(nlayers):
    if nlayers == 0:
        return ()
    if nlayers == 1:
        return ("w_mod", "conv_w_pw1", "conv_w_pw2", "mlp_w1", "mlp_w2")
    return ("w_mod", "conv_w_pw1", "conv_w_pw2", "dn_w_in", "dn_w_o", "mlp_w1", "mlp_w2")


def make_inputs(inp, nlayers=DEPTH):
    maps = []
    shared = {k: np.ascontiguousarray(np.asarray(inp[k], np.float32)) for k in needed_weights(nlayers)}
    cols = None
    for core in range(NCORES):
        b, hf = core // 2, core % 2
        cp = pack_consts(inp, b, hf)
        cols = cp.cols
        xc = np.asarray(inp["ctx"][b], np.float32)
        xl = np.asarray(inp["x"][b, NLAT * hf:NLAT * (hf + 1)], np.float32)
        if hf == 1:
            xc = xc[::-1]
            xl = xl[::-1]
        xt = np.concatenate([xc, xl], axis=0)
        m = dict(shared)
        if nlayers >= 2:
            m["wab"] = make_wab(inp, hf)
        m["xT"] = np.ascontiguousarray(xt.T)
        m["consts"] = cp.array()
        maps.append(m)
    return maps, cols, maps[0]["consts"].shape[1]


def run(inp, nlayers=DEPTH):
    maps, cols, ncols = make_inputs(inp, nlayers)
    bld = Builder(nlayers, cols, ncols)
    nc = bld.build()
    res = run_bass_kernel_spmd(nc, maps, core_ids=list(range(NCORES)))
    out = np.zeros((4, SEQ, D), np.float32)
    for core in range(NCORES):
        b, hf = core // 2, core % 2
        o = res.results[core]["outT"].T
        out[b, NLAT * hf:NLAT * (hf + 1)] = o[::-1] if hf == 1 else o
    return out


def kernel(**inputs):
    inp = {k: np.asarray(v) for k, v in inputs.items()}
    return run(inp, DEPTH)
```

```python
import numpy as np
import concourse.bass as bass
import concourse.mybir as mybir
from concourse.bass_utils import run_bass_kernel_spmd

F32 = mybir.dt.float32
BF16 = mybir.dt.bfloat16
F32R = mybir.dt.float32r
FP32R = False
ALU = mybir.AluOpType
AF = mybir.ActivationFunctionType

D = 1024
NCH = 8
NCTX = 256
NLAT = 2048
T = NCTX + NLAT
SEQ = 4096
DEPTH = 4
EPS = 1e-6
KW = 31
PADW = 15
NCORES = 8
DEBUG_TAGS = False
TAGMAP = {}


class Buf:
    __slots__ = ("name", "last_w", "readers")

    def __init__(self, name):
        self.name = name
        self.last_w = None
        self.readers = []


class Op:
    __slots__ = ("eng", "fn", "deps", "needs_inc", "idx", "dma_key", "dma_val", "tag")

    def __init__(self, eng, fn):
        self.eng = eng
        self.fn = fn
        self.deps = []
        self.needs_inc = False
        self.idx = 0
        self.dma_key = None
        self.dma_val = 0


class Prog:
    ENGS = ("pe", "act", "dve", "pool", "sp")

    def __init__(self):
        self.ops = {e: [] for e in self.ENGS}
        self.dma_count = {}
        self.dma_total_keys = set()
        self.last_dma = {}
        self.dma_inc = {}
        self.fence_deps = []

    def fence(self):
        f = []
        for e in self.ENGS:
            for o in reversed(self.ops[e]):
                if o.dma_key is None:
                    o.needs_inc = True
                    f.append(o)
                    break
        f.extend(self.last_dma.values())
        self.fence_deps = f

    def _add_dep(self, op, dep, war=False):
        if dep is None or dep is op:
            return
        if dep.dma_key is None and dep.eng == op.eng:
            if op.eng == "pe":
                return
        if dep.dma_key is None:
            dep.needs_inc = True
        op.deps.append(dep)

    def op(self, eng, fn, reads=(), writes=()):
        o = Op(eng, fn)
        if DEBUG_TAGS:
            import sys as _sys
            f = _sys._getframe(1)
            tg = []
            while f is not None and len(tg) < 4:
                tg.append(f.f_lineno)
                f = f.f_back
            o.tag = tg
        for dep in self.fence_deps:
            if dep.dma_key is not None or dep.eng != eng or eng != "pe":
                o.deps.append(dep)
        for b in reads:
            self._add_dep(o, b.last_w)
        for b in writes:
            self._add_dep(o, b.last_w)
            for r in b.readers:
                self._add_dep(o, r, war=True)
        for b in reads:
            b.readers.append(o)
        for b in writes:
            b.last_w = o
            b.readers = []
        self.ops[eng].append(o)
        return o

    def dma(self, queue, key, fn, reads=(), writes=(), wait_total=False, indep=False, inc=16):
        o = self.op(queue, fn, reads, writes)
        self.dma_inc[key] = inc
        if wait_total or indep:
            o.deps = [d for d in o.deps if d.dma_key != key]
        n = self.dma_count.get(key, 0) + 1
        self.dma_count[key] = n
        o.dma_key = key
        o.dma_val = inc * n
        self.last_dma[key] = o
        if wait_total:
            self.dma_total_keys.add(key)
        return o

    def emit(self, nc, block, sems, final_waits=()):
        for e in self.ENGS:
            c = 0
            for o in self.ops[e]:
                if o.dma_key is None and o.needs_inc:
                    c += 1
                    o.idx = c

        def token(dep):
            if dep.dma_key is not None:
                if dep.dma_key in self.dma_total_keys:
                    return dep.dma_key, self.dma_inc[dep.dma_key] * self.dma_count[dep.dma_key]
                return dep.dma_key, dep.dma_val
            return dep.eng, dep.idx

        def run(e, eng):
            known = {}
            for o in self.ops[e]:
                for dep in o.deps:
                    k, v = token(dep)
                    if known.get(k, 0) >= v:
                        continue
                    eng.wait_ge(sems[k], v)
                    known[k] = v
                ins = o.fn(eng)
                if DEBUG_TAGS:
                    try:
                        TAGMAP[ins.ins.name] = o.tag
                    except Exception:
                        pass
                if o.dma_key is not None:
                    ins.then_inc(sems[o.dma_key], self.dma_inc[o.dma_key])
                elif o.needs_inc:
                    ins.then_inc(sems[e], 1)
            for k in final_waits.get(e, ()) if isinstance(final_waits, dict) else ():
                eng.wait_ge(sems[k], 16 * self.dma_count[k])

        @block.tensor
        def _(eng):
            run("pe", eng)

        @block.scalar
        def _(eng):
            run("act", eng)

        @block.vector
        def _(eng):
            run("dve", eng)

        @block.gpsimd
        def _(eng):
            run("pool", eng)

        @block.sync
        def _(eng):
            run("sp", eng)


def fm(vec):
    v = np.asarray(vec, np.float32).reshape(-1, 128)
    return np.ascontiguousarray(v.T)


class ConstPack:
    def __init__(self):
        self.cols = {}
        self.n = 0
        self.parts = []

    def add(self, name, arr):
        arr = np.asarray(arr, np.float32)
        assert arr.shape[0] == 128
        arr = arr.reshape(128, -1)
        self.cols[name] = (self.n, arr.shape[1])
        self.n += arr.shape[1]
        self.parts.append(arr)

    def array(self):
        return np.ascontiguousarray(np.concatenate(self.parts, axis=1))


def pack_consts(inp, b, hf):
    cp = ConstPack()
    cc = np.stack([fm(inp["c"][b]), fm(inp["c_ctx"])], axis=2)
    cp.add("c", cc)
    for i in range(DEPTH):
        cp.add(f"bmod{i}", fm(inp["b_mod"][i]))
        cp.add(f"n1g{i}", fm(inp["norm1_g"][i]))
        cp.add(f"n2g{i}", fm(inp["norm2_g"][i]))
    cp.add("fg", fm(inp["final_g"]))
    for j in range(2):
        cp.add(f"b1{j}", fm(inp["conv_b_pw1"][j]))
        wdw = np.asarray(inp["conv_w_dw"][j], np.float32)
        if hf == 1:
            wdw = wdw[::-1]
        w = wdw.reshape(KW, NCH, 128).transpose(2, 1, 0)
        cp.add(f"wdw{j}", w)
        cp.add(f"bdw{j}", fm(inp["conv_b_dw"][j]))
        cp.add(f"lng{j}", fm(inp["conv_ln_g"][j]))
        cp.add(f"lnb{j}", fm(inp["conv_ln_b"][j]))
        cp.add(f"b2{j}", fm(inp["conv_b_pw2"][j]))
    dirmap = (hf, 1 - hf)
    for j in range(2):
        cw = np.asarray(inp["dn_conv_w"][j], np.float32)
        if hf == 1:
            cw = cw[::-1]
        w = cw.reshape(5, 24, 128).transpose(2, 1, 0)
        cp.add(f"cw{j}", w)
        cp.add(f"ng{j}", np.asarray(inp["dn_norm_g"][j], np.float32).reshape(128, 1))
        al = np.zeros((128, 1), np.float32)
        db = np.zeros((128, 1), np.float32)
        for d in range(2):
            al[32 * d:32 * d + 8, 0] = inp["dn_a_log"][j][dirmap[d]]
            db[32 * d:32 * d + 8, 0] = inp["dn_dt_bias"][j][dirmap[d]]
        cp.add(f"alog{j}", al)
        cp.add(f"dtb{j}", db)
    pm = np.zeros((128, 2), np.float32)
    pm[:, 1 - hf] = 1.0
    cp.add("pm", pm)
    return cp


def make_wab(inp, hf):
    dirmap = (hf, 1 - hf)
    out = np.zeros((2, D, 128), np.float32)
    for j in range(2):
        w = np.asarray(inp["dn_w_in"][j], np.float32)
        for d in range(2):
            out[j, :, 32 * d:32 * d + 8] = w[:, 4096 + dirmap[d] * 8:4096 + dirmap[d] * 8 + 8]
            out[j, :, 64 + 32 * d:64 + 32 * d + 8] = w[:, 4096 + 16 + dirmap[d] * 8:4096 + 16 + dirmap[d] * 8 + 8]
    return out


SLOT_ELEMS = 4096
ARENA_BYTES = 115 * 1024 + 128


class StopBuild(Exception):
    pass


class Builder:
    dbg_stop = None

    def __init__(self, nlayers, cols, ncols, ncores=NCORES):
        self.ncores = ncores
        self.nlayers = nlayers
        self.cols = cols
        self.ncols = ncols
        self.P = Prog()
        self.nc = bass.Bass("TRN2", target_bir_lowering=False)
        self.psum_rr = 0
        self.slot_rr = 0
        self.ar_off = 0

    def phase(self, nslots, sq=True):
        self.P.fence()
        self.ar_off = 0
        self.SLOT = []
        self.SLOTB = []
        for i in range(nslots):
            v, b = self.aalloc(f"SLOT{i}", [SLOT_ELEMS], BF16)
            self.SLOT.append(v)
            self.SLOTB.append(b)
        self.slot_rr = 0
        if sq:
            self.SQ, self.SQB = self.aalloc("SQ", [NCH, 512], BF16)

    def aalloc(self, name, shape, dtype):
        n = 1
        for d_ in shape:
            n *= d_
        esz = 4 if dtype == F32 else 2
        nbytes = (n * esz + 63) // 64 * 64
        off = self.ar_off
        self.ar_off += nbytes
        assert self.ar_off <= ARENA_BYTES, (name, self.ar_off)
        if not hasattr(self, "amap"):
            self.amap = {}
        self.amap[name] = (off, tuple(shape), esz)
        v = self.ARENA[:, off // 2: off // 2 + n * esz // 2]
        if dtype == F32:
            v = v.bitcast(F32)
        if len(shape) > 1:
            names = [f"d{k}" for k in range(len(shape))]
            kw = {nm: sz for nm, sz in zip(names[:-1], shape[:-1])}
            v = v.rearrange(f"p ({' '.join(names)}) -> p {' '.join(names)}", **kw)
        return v, Buf(name)

    def cst(self, name, c0=0, n=None):
        o, w = self.cols[name]
        if n is None:
            n = w - c0
        return self.CONST[:, o + c0:o + c0 + n]

    def psum(self):
        i = self.psum_rr % len(self.PS)
        self.psum_rr += 1
        return self.PS[i], self.PSB[i]

    def tmp(self):
        i = self.tmp_rr % len(self.TMP)
        self.tmp_rr += 1
        return self.TMP[i], self.TMPB[i]

    def load_slot(self, dram_ap):
        i = self.slot_rr % len(self.SLOT)
        self.slot_rr += 1
        st, sb = self.SLOT[i], self.SLOTB[i]
        shp = dram_ap.shape
        view = st[:, 0:shp[1] * shp[2]].rearrange("p (k n) -> p k n", k=shp[1])
        self.P.dma("pool", f"slot{i}", lambda e, o=view, a=dram_ap: e.dma_start(out=o, in_=a),
                   reads=(), writes=(sb,))
        return view, sb

    def mm(self, out, lhsT, rhs, start, stop, reads, writes):
        if FP32R and lhsT.dtype == F32:
            lhsT = lhsT.bitcast(F32R)
            rhs = rhs.bitcast(F32R)
        self.P.op("pe", lambda e: e.matmul(out, lhsT, rhs, start=start, stop=stop),
                  reads=reads, writes=writes)

    def act(self, out, in_, func, reads, writes, bias=None, scale=None):
        kw = {}
        if bias is not None:
            kw["bias"] = bias
        if scale is not None:
            kw["scale"] = scale
        self.P.op("act", lambda e: e.activation(out, in_, func, **kw), reads=reads, writes=writes)

    def dve(self, fn, reads, writes):
        self.P.op("dve", fn, reads=reads, writes=writes)

    def rstd_from(self, src, srcb, n, out, outb):
        self.act(self.LNT[:, 0:n], src, AF.Ln, reads=(srcb, self.EPSB), writes=(self.LNTB,),
                 bias=self.EPST[:, 0:1])
        self.act(out[:, 0:n], self.LNT[:, 0:n], AF.Exp, reads=(self.LNTB,), writes=(outb,), scale=-0.5)

    def norm_mod(self, t0, n, Acol, Bcol, out3, outb, dst_dram=None):
        X, XB = self.X, self.XB
        self.act(self.SQ[:, :, 0:n], X[:, :, t0:t0 + n], AF.Square, reads=(XB,), writes=(self.SQB,))
        ps, psb = self.psum()
        for c in range(NCH):
            self.mm(ps[:, 0:n], self.ONES[:, :], self.SQ[:, c, 0:n], c == 0, c == NCH - 1,
                    reads=(self.SQB, self.ONESB), writes=(psb,))
        self.rstd_from(ps[:, 0:n], psb, n, self.RSTD, self.RSTDB)
        for c in range(NCH):
            tm, tmb = self.tmp()
            self.dve(lambda e, c=c, tm=tm: e.scalar_tensor_tensor(
                tm[:, 0:n], X[:, c, t0:t0 + n], Acol(c), self.RSTD[:, 0:n], ALU.mult, ALU.mult),
                reads=(XB, self.RSTDB, self.MODB, self.CONSTB), writes=(tmb,))
            if dst_dram is not None:
                self.P.dma("sp", "out_" + tmb.name, lambda e, c=c, tm=tm: e.dma_start(out=dst_dram(c), in_=tm[:, 0:n]),
                           reads=(tmb,), writes=())
            else:
                self.act(out3[:, c, 0:n], tm[:, 0:n], AF.Identity, reads=(tmb, self.MODB),
                         writes=(outb,), bias=Bcol(c))

    def compute_mod(self, i, w_mod):
        wv = w_mod[i].rearrange("(k p) n -> p k n", p=128)
        ps, psb = self.psum()
        psv = ps[:, 0:96].rearrange("p (n s) -> p n s", s=2)
        for s in range(12):
            sv, sb = self.load_slot(wv[:, :, 512 * s:512 * (s + 1)])
            for q in range(4):
                n = 4 * s + q
                for k in range(NCH):
                    self.mm(psv[:, n, :], sv[:, k, 128 * q:128 * (q + 1)], self.SC[:, k, :], k == 0, k == NCH - 1,
                            reads=(sb, self.SCB), writes=(psb,))
        o, _ = self.cols[f"bmod{i}"]
        bm = self.CONST[:, o:o + 48]
        M = self.MOD[:, i]
        for s in range(2):
            self.dve(lambda e, s=s: e.tensor_tensor(
                M[:, :, :, s], psv[:, :, s].rearrange("p (m c) -> p m c", m=6),
                bm.rearrange("p (m c) -> p m c", m=6), ALU.add),
                reads=(psb, self.CONSTB), writes=(self.MODB,))
        for m, gname in ((1, f"n1g{i}"), (4, f"n2g{i}")):
            for s in range(2):
                self.dve(lambda e, m=m, s=s, gname=gname: e.scalar_tensor_tensor(
                    M[:, m, :, s], M[:, m, :, s], 1.0, self.cst(gname), ALU.add, ALU.mult),
                    reads=(self.MODB, self.CONSTB), writes=(self.MODB,))

    def modcol(self, i, m, s):
        return lambda c: self.MOD[:, i, m, c, s:s + 1]

    def tiles(self, include_ctx=True):
        r = []
        if include_ctx:
            r.append((0, NCTX, 1))
        for k in range(NLAT // 512):
            r.append((NCTX + 512 * k, 512, 0))
        return r

    def conv_module(self, i, j, last, W):
        self.phase(6)
        HNT, HNTB = self.aalloc("HNT", [NCH, 512], BF16)
        UL, ULB = self.aalloc("UL", [NCH, 8 * (64 + 2 * PADW)], BF16)
        UC, UCB = self.aalloc("UC", [NCH, NCTX + 2 * PADW], BF16)
        DG, DGB = self.aalloc("DIAG", [KW, 128], BF16)
        CB, CBB = self.aalloc("CB", [NCH, 512], BF16)
        VT, VTB = self.aalloc("VT", [NCH, 512], BF16)
        self.MEAN, self.MEANB = self.aalloc("MEAN", [512], F32)
        self.VAR, self.VARB = self.aalloc("VAR", [512], F32)
        self.SIG, self.SIGB = self.aalloc("SIG", [512], F32)
        self.P.op("pool", lambda e: e.memset(UL, 0.0), writes=(ULB,))
        self.P.op("pool", lambda e: e.memset(UC, 0.0), writes=(UCB,))
        w1v = W["conv_w_pw1"][j].rearrange("(k p) n -> p k n", p=128)
        w2v = W["conv_w_pw2"][j].rearrange("(k p) n -> p k n", p=128)
        s1 = [self.load_slot(w1v[:, :, 512 * s:512 * (s + 1)]) for s in range(4)]
        s2 = [self.load_slot(w2v[:, :, 512 * s:512 * (s + 1)]) for s in range(2)]
        b1 = lambda c: self.cst(f"b1{j}", c, 1)
        for tile_ in self.tiles(include_ctx=not last):
            self.conv_tile(i, j, tile_, s1, s2, b1, HNT, HNTB, UL, ULB, UC, UCB, DG, DGB, CB, CBB, VT, VTB)

    def conv_tile(self, i, j, tile_, s1, s2, b1, HNT, HNTB, UL, ULB, UC, UCB, DG, DGB, CB, CBB, VT, VTB):
        if True:
            t0, n, s = tile_
            nrow = 1 if s == 1 else n // 64
            rl = n // nrow
            rs = rl + 2 * PADW
            self.norm_mod(t0, n, self.modcol(i, 1, s), self.modcol(i, 0, s), HNT, HNTB)
            U = UC if s == 1 else UL
            UB = UCB if s == 1 else ULB
            Uv = U.rearrange("p c (r w) -> p c r w", w=rs)
            for c in range(NCH):
                pa, pab = self.psum()
                sv, sb = s1[c // 4]
                for k in range(NCH):
                    self.mm(pa[:, 0:n], sv[:, k, 128 * (c % 4):128 * (c % 4 + 1)], HNT[:, k, 0:n],
                            k == 0, k == NCH - 1, reads=(sb, HNTB), writes=(pab,))
                pg, pgb = self.psum()
                sv, sb = s1[2 + c // 4]
                for k in range(NCH):
                    self.mm(pg[:, 0:n], sv[:, k, 128 * (c % 4):128 * (c % 4 + 1)], HNT[:, k, 0:n],
                            k == 0, k == NCH - 1, reads=(sb, HNTB), writes=(pgb,))
                self.act(self.SIG[:, 0:n], pg[:, 0:n], AF.Sigmoid, reads=(pgb, self.CONSTB),
                         writes=(self.SIGB,), bias=b1(NCH + c))
                self.dve(lambda e, c=c, pa=pa: e.scalar_tensor_tensor(
                    Uv[:, c, :, PADW:PADW + rl], pa[:, 0:n].rearrange("p (r w) -> p r w", w=rl), b1(c),
                    self.SIG[:, 0:n].rearrange("p (r w) -> p r w", w=rl), ALU.add, ALU.mult),
                    reads=(pab, self.SIGB, self.CONSTB), writes=(UB,))
            wo, _ = self.cols[f"wdw{j}"]
            for c in range(NCH):
                wk = self.CONST[:, wo + c * KW:wo + (c + 1) * KW]
                self.dve(lambda e, wk=wk: e.tensor_tensor(
                    DG, self.IDENT[:, None, :].to_broadcast([128, KW, 128]),
                    wk[:, :, None].to_broadcast([128, KW, 128]), ALU.mult),
                    reads=(self.IDENTB, self.CONSTB), writes=(DGB,))
                pc, pcb = self.psum()
                pcv = pc[:, 0:n].rearrange("p (r w) -> p r w", w=rl)
                for k in range(KW):
                    self.mm(pcv, DG[:, k, :], Uv[:, c, :, k:k + rl], k == 0, k == KW - 1,
                            reads=(DGB, UB), writes=(pcb,))
                bd = self.cst(f"bdw{j}", c, 1)
                self.act(CB[:, c, 0:n], pc[:, 0:n], AF.Identity, reads=(pcb, self.CONSTB),
                         writes=(CBB,), bias=bd)
                self.act(self.SQ[:, c, 0:n], pc[:, 0:n], AF.Square, reads=(pcb, self.CONSTB),
                         writes=(self.SQB,), bias=bd)
            pm, pmb = self.psum()
            for c in range(NCH):
                self.mm(pm[:, 0:n], self.ONES[:, :], CB[:, c, 0:n], c == 0, c == NCH - 1,
                        reads=(CBB, self.ONESB), writes=(pmb,))
            pq, pqb = self.psum()
            for c in range(NCH):
                self.mm(pq[:, 0:n], self.ONES[:, :], self.SQ[:, c, 0:n], c == 0, c == NCH - 1,
                        reads=(self.SQB, self.ONESB), writes=(pqb,))
            self.dve(lambda e, pm=pm: e.tensor_copy(self.MEAN[:, 0:n], pm[:, 0:n]),
                     reads=(pmb,), writes=(self.MEANB,))
            self.dve(lambda e: e.tensor_tensor(self.VAR[:, 0:n], self.MEAN[:, 0:n], self.MEAN[:, 0:n], ALU.mult),
                     reads=(self.MEANB,), writes=(self.VARB,))
            self.dve(lambda e, pq=pq: e.tensor_tensor(self.VAR[:, 0:n], pq[:, 0:n], self.VAR[:, 0:n], ALU.subtract),
                     reads=(pqb, self.VARB), writes=(self.VARB,))
            self.rstd_from(self.VAR[:, 0:n], self.VARB, n, self.RSTD, self.RSTDB)
            for c in range(NCH):
                tm, tmb = self.tmp()
                self.dve(lambda e, c=c, tm=tm: e.tensor_tensor(tm[:, 0:n], CB[:, c, 0:n], self.MEAN[:, 0:n], ALU.subtract),
                         reads=(CBB, self.MEANB), writes=(tmb,))
                self.dve(lambda e, tm=tm: e.tensor_tensor(tm[:, 0:n], tm[:, 0:n], self.RSTD[:, 0:n], ALU.mult),
                         reads=(tmb, self.RSTDB), writes=(tmb,))
                self.act(VT[:, c, 0:n], tm[:, 0:n], AF.Silu, reads=(tmb, self.CONSTB), writes=(VTB,),
                         bias=self.cst(f"lnb{j}", c, 1), scale=self.cst(f"lng{j}", c, 1))
            for fc in range(NCH):
                po, pob = self.psum()
                sv, sb = s2[fc // 4]
                for k in range(NCH):
                    self.mm(po[:, 0:n], sv[:, k, 128 * (fc % 4):128 * (fc % 4 + 1)], VT[:, k, 0:n],
                            k == 0, k == NCH - 1, reads=(sb, VTB), writes=(pob,))
                tm, tmb = self.tmp()
                self.dve(lambda e, fc=fc, po=po, tm=tm: e.tensor_scalar(
                    tm[:, 0:n], po[:, 0:n], self.cst(f"b2{j}", fc, 1), self.MOD[:, i, 2, fc, s:s + 1], ALU.add, ALU.mult),
                    reads=(pob, self.CONSTB, self.MODB), writes=(tmb,))
                self.dve(lambda e, fc=fc, tm=tm: e.tensor_tensor(
                    self.X[:, fc, t0:t0 + n], self.X[:, fc, t0:t0 + n], tm[:, 0:n], ALU.add),
                    reads=(tmb, self.XB), writes=(self.XB,))

    def mlp(self, i, last, W):
        if self.dbg_stop == ("M", i):
            raise StopBuild()
        self.phase(6)
        HN, HNB = self.aalloc("HN", [NCH, T], BF16)
        HQ, HQB = self.aalloc("HQ", [NCH, 512], BF16)
        w1v = W["mlp_w1"][i].rearrange("(k p) n -> p k n", p=128)
        w2v = W["mlp_w2"][i].rearrange("(k p) n -> p k n", p=128)
        tiles = self.tiles(include_ctx=not last)
        for (t0, n, s) in tiles:
            self.norm_mod(t0, n, self.modcol(i, 4, s), self.modcol(i, 3, s), HN[:, :, t0:t0 + n], HNB)
        for q in range(4):
            s1 = [self.load_slot(w1v[:, :, 1024 * q + 512 * s:1024 * q + 512 * (s + 1)]) for s in range(2)]
            s2 = [self.load_slot(w2v[:, 8 * q + 4 * s:8 * q + 4 * (s + 1), :]) for s in range(2)]
            for tile_ in tiles:
                self.mlp_tile(i, tile_, s1, s2, HN, HNB, HQ, HQB)

    def mlp_tile(self, i, tile_, s1, s2, HN, HNB, HQ, HQB):
        if True:
            if True:
                t0, n, s = tile_
                for hc in range(8):
                    ph, phb = self.psum()
                    sv, sb = s1[hc // 4]
                    for k in range(NCH):
                        self.mm(ph[:, 0:n], sv[:, k, 128 * (hc % 4):128 * (hc % 4 + 1)], HN[:, k, t0:t0 + n],
                                k == 0, k == NCH - 1, reads=(sb, HNB), writes=(phb,))
                    tm, tmb = self.tmp()
                    self.act(tm[:, 0:n], ph[:, 0:n], AF.Relu, reads=(phb,), writes=(tmb,))
                    self.dve(lambda e, hc=hc, tm=tm: e.tensor_tensor(HQ[:, hc, 0:n], tm[:, 0:n], tm[:, 0:n], ALU.mult),
                             reads=(tmb,), writes=(HQB,))
                for fc in range(NCH):
                    py, pyb = self.psum()
                    for hc in range(8):
                        sv, sb = s2[hc // 4]
                        self.mm(py[:, 0:n], sv[:, hc % 4, 128 * fc:128 * (fc + 1)], HQ[:, hc, 0:n],
                                hc == 0, hc == 7, reads=(sb, HQB), writes=(pyb,))
                    self.dve(lambda e, fc=fc, py=py: e.scalar_tensor_tensor(
                        self.X[:, fc, t0:t0 + n], py[:, 0:n], self.MOD[:, i, 5, fc, s:s + 1],
                        self.X[:, fc, t0:t0 + n], ALU.mult, ALU.add),
                        reads=(pyb, self.MODB, self.XB), writes=(self.XB,))

    def build(self):
        nc = self.nc
        P = self.P
        dt = nc.dram_tensor
        xT = dt("xT", [D, T], F32, kind="ExternalInput").ap()
        consts = dt("consts", [128, self.ncols], F32, kind="ExternalInput").ap()
        W = {}
        for name, shp in (("w_mod", [DEPTH, D, 6 * D]), ("conv_w_pw1", [2, D, 2 * D]), ("conv_w_pw2", [2, D, D]),
                          ("dn_w_in", [2, D, 4128]), ("dn_w_o", [2, D, D]),
                          ("mlp_w1", [DEPTH, D, 4 * D]), ("mlp_w2", [DEPTH, 4 * D, D])):
            if name in needed_weights(self.nlayers):
                W[name] = dt(name, shp, F32, kind="ExternalInput").ap()
        outT = dt("outT", [D, NLAT], F32, kind="ExternalOutput").ap()
        if self.nlayers >= 2:
            W["wab"] = dt("wab", [2, D, 128], F32, kind="ExternalInput").ap()
            self.hx_in = dt("hx_in", [128, 16], BF16)
            self.hx_out = dt("hx_out", [256, 16], BF16)
            self.st_in = dt("st_in", [128, 128], F32)
            self.st_out = dt("st_out", [256, 128], F32)
            self.HXI = Buf("hx_in"); self.HXO = Buf("hx_out"); self.STI = Buf("st_in"); self.STO = Buf("st_out")
        self.W = W

        from contextlib import ExitStack
        with ExitStack() as es:
            def sb(name, shape, dtype):
                return es.enter_context(nc.sbuf_tensor(name, shape, dtype))

            self.X = sb("X", [128, NCH, T], F32); self.XB = Buf("X")
            self.ARENA = sb("ARENA", [128, ARENA_BYTES // 2], BF16)
            self.CONST = sb("CONST", [128, self.ncols], F32); self.CONSTB = Buf("CONST")
            self.MOD = sb("MOD", [128, DEPTH, 6, NCH, 2], F32); self.MODB = Buf("MOD")
            self.SC = sb("SC", [128, NCH, 2], BF16); self.SCB = Buf("SC")
            self.ONES = sb("ONES", [128, 128], BF16); self.ONESB = Buf("ONES")
            self.IDENT = sb("IDENT", [128, 128], BF16); self.IDENTB = Buf("IDENT")
            self.IDF = sb("IDF", [128, 128], F32); self.IDFB = Buf("IDF")
            self.EPST = sb("EPST", [128, 1], F32); self.EPSB = Buf("EPS")
            for nm_ in ("ONES", "IDENT", "IDF"):
                setattr(self, nm_, getattr(self, nm_)[:, :])
            self.LNT = sb("LNT", [128, 512], F32); self.LNTB = Buf("LNT")
            self.RSTD = sb("RSTD", [128, 512], F32); self.RSTDB = Buf("RSTD")
            self.ONE1 = sb("ONE1", [128, 1], F32); self.ONEB = Buf("ONE1")
            self.ONEF = sb("ONEF", [128, 128], F32); self.ONEFB = Buf("ONEF")
            self.ONESS = sb("ONESS", [128, 128], BF16); self.ONESSB = Buf("ONESS")
            self.ONES128 = sb("ONES128", [128, 128], BF16); self.ONES128B = Buf("ONES128")
            self.ML = sb("ML", [128, 128], F32); self.MLS = sb("MLS", [128, 128], F32)
            self.MU = sb("MU", [128, 128], F32); self.MUS = sb("MUS", [128, 128], F32)
            self.MASKB = Buf("MASK")
            for nm_ in ("ONEF", "ONESS", "ONES128", "ML", "MLS", "MU", "MUS", "ONE1"):
                setattr(self, nm_, getattr(self, nm_)[:, :])
            self.TMP = [sb(f"TMP{i}", [128, 512], F32) for i in range(2)]
            self.TMPB = [Buf(f"TMP{i}") for i in range(2)]
            self.tmp_rr = 0
            self.OUTB = Buf("OUT")
            self.PS = [es.enter_context(nc.psum_tensor(f"PS{i}", [128, 512], F32)) for i in range(8)]
            self.PSB = [Buf(f"PS{i}") for i in range(8)]

            P.dma("sp", "const", lambda e: e.dma_start(out=self.CONST[:, :], in_=consts[:, :]),
                  writes=(self.CONSTB,), wait_total=True)
            xv = xT.rearrange("(c p) t -> p c t", p=128)
            for c in range(NCH):
                P.dma("sp", "xin", lambda e, c=c: e.dma_start(out=self.X[:, c, :], in_=xv[:, c, :]),
                      writes=(self.XB,), wait_total=True)
            P.op("pool", lambda e: e.memset(self.ONES[:, :], 1.0 / D), writes=(self.ONESB,))
            P.op("pool", lambda e: e.memset(self.EPST[:, :], EPS), writes=(self.EPSB,))
            P.op("pool", lambda e: e.memset(self.IDF[:, :], 0.0), writes=(self.IDFB,))
            P.op("pool", lambda e: e.affine_select(self.IDF[:, :], self.IDF[:, :], pattern=[[-1, 128]],
                                                    compare_op=ALU.not_equal, fill=1.0, base=0, channel_multiplier=1),
                 reads=(self.IDFB,), writes=(self.IDFB,))
            P.op("pool", lambda e: e.tensor_copy(self.IDENT[:, :], self.IDF[:, :]), reads=(self.IDFB,), writes=(self.IDENTB,))
            P.op("pool", lambda e: e.memset(self.ONE1[:, :], 1.0), writes=(self.ONEB,))
            P.op("pool", lambda e: e.memset(self.ONEF[:, :], 1.0), writes=(self.ONEFB,))
            P.op("pool", lambda e: e.memset(self.ONESS[:, :], 1.0), writes=(self.ONESSB,))
            P.op("pool", lambda e: e.memset(self.ONES128[:, :], 1.0 / 128), writes=(self.ONES128B,))
            for mt, base, cm, pat in ((self.ML, 0, 1, -1), (self.MLS, -1, 1, -1), (self.MU, 0, -1, 1), (self.MUS, -1, -1, 1)):
                P.op("pool", lambda e, mt=mt, base=base, cm=cm, pat=pat: e.affine_select(
                    mt[:, :], self.ONEF[:, :], pattern=[[pat, 128]], compare_op=ALU.is_ge, fill=0.0,
                    base=base, channel_multiplier=cm), reads=(self.ONEFB,), writes=(self.MASKB,))
            self.BD = sb("BD", [128, 128], F32)[:, :]
            P.op("pool", lambda e: e.memset(self.BD, 0.0), writes=(self.MASKB,))
            P.op("pool", lambda e: e.memset(self.BD[0:64, 0:64], 1.0), writes=(self.MASKB,))
            P.op("pool", lambda e: e.memset(self.BD[64:128, 64:128], 1.0), writes=(self.MASKB,))
            for mt in (self.ML, self.MLS, self.MU, self.MUS):
                P.op("pool", lambda e, mt=mt: e.tensor_tensor(mt, mt, self.BD, ALU.mult),
                     reads=(self.MASKB,), writes=(self.MASKB,))
            co, _ = self.cols["c"]
            self.act(self.SC[:, :, :].rearrange("p k s -> p (k s)"), self.CONST[:, co:co + 16], AF.Silu,
                     reads=(self.CONSTB,), writes=(self.SCB,))
            self.phase(6)
            for i in range(self.nlayers):
                self.compute_mod(i, W["w_mod"])

            try:
                for i in range(self.nlayers):
                    last = i == self.nlayers - 1
                    j = i // 2
                    if i % 2 == 0:
                        self.conv_module(i, j, last, W)
                    else:
                        self.delta_module(i, j, last, W)
                    self.mlp(i, last, W)
            except StopBuild:
                pass

            ov = outT.rearrange("(c p) t -> p c t", p=128)
            for (t0, n, s) in self.tiles(include_ctx=False):
                fgc = lambda c: self.cst("fg", c, 1)
                self.norm_mod(t0, n, fgc, None, None, None,
                              dst_dram=lambda c, t0=t0, n=n: ov[:, c, t0 - NCTX:t0 - NCTX + n])

            keys = list(self.P.dma_count.keys())
            sems = {}
            for k in list(Prog.ENGS) + keys:
                sems[k] = es.enter_context(nc.semaphore(f"s_{k}"))
            block = es.enter_context(nc.Block())
            self.P.emit(nc, block, sems, final_waits={"sp": tuple(k for k in keys if k.startswith("out_"))})
        return nc

    def exchange(self, key, src_ap, src_b, din, din_b, dout, dout_b, dst_ap, dst_b, din_view=None):
        P = self.P
        dv = din[:, :] if din_view is None else din_view
        P.dma("sp", key + "_a", lambda e: e.dma_start(out=dv, in_=src_ap), reads=(src_b,), writes=(din_b,))
        groups = [[2 * k, 2 * k + 1] for k in range(self.ncores // 2)]
        P.dma("pool", key + "_c", lambda e: e.collective_compute(
            "AllGather", ALU.bypass, replica_groups=groups, ins=[din.ap().opt()], outs=[dout.ap().opt()]),
            reads=(din_b,), writes=(dout_b,), inc=1)
        P.dma("sp", key + "_b", lambda e: e.dma_start(
            out=dst_ap, in_=dout.ap().rearrange("(r p) n -> p r n", p=128)), reads=(dout_b,), writes=(dst_b,))

    def delta_module(self, i, j, last, W):
        P = self.P
        self.phase(0, sq=False)
        A = self.aalloc
        NCK = T // 128
        PT_ = 2312
        pcol = lambda t: t + 2 if t < NCTX else t + 6
        WIN, WINB = A("WIN", [8, 512], BF16)
        WO, WOB = A("WO", [1024], BF16)
        WAB, WABB = WIN[:, :, 0:128], WINB
        HNP, HNPB = A("HNP", [8, PT_], BF16)
        G, GB = A("G", [T], F32)
        O, OB = A("O", [T], F32)
        COLS, COLSB = A("COLS", [NCK, 32], F32)
        HLB, HLBB = A("HLB", [NCK, 16], F32)
        KDF, KDFB = A("KDF", [NCK, 16], F32)
        CDA, CDAB = A("CDA", [NCK, 16], F32)
        CDB_, CDBB = A("CDB", [NCK, 16], F32)
        NBE, NBEB = A("NBE", [NCK, 16], F32)
        CD = (CDA, CDB_)
        CDB = (CDAB, CDBB)
        NB, NBB = HLB, HLBB
        TOT, TOTB = A("TOT", [2 * NCK], F32)
        EAL, EALB = A("EAL", [1], F32)
        SELC, SELCB = A("SELC", [32], F32)
        QT, QTB = A("QT", [T], BF16)
        KF, KFB = A("KF", [T], F32)
        off_ = self.amap["KF"][0]
        self.SQ = self.ARENA[:, off_ // 2: off_ // 2 + NCH * 512].rearrange("p (c t) -> p c t", c=NCH)
        self.SQB = KFB
        SQ1, SQ1B = A("SQ1", [512], BF16)
        VTM, VTMB = A("VTM", [NCK, 128], BF16)
        PRET, PRETB = A("PRET", [260], BF16)
        VTT, VTTB = A("VTT", [256], BF16)
        DG5, DG5B = A("DG5", [5, 128], BF16)
        SELR, SELRB = A("SELR", [4, 128], F32)
        HXG, HXGB = A("HXG", [2, 16], BF16)
        HXS, HXSB = A("HXS", [16], F32)
        SG, SGB = A("SG", [2, 128], F32)
        S, SB_ = A("S", [128], F32)
        SBF, SBFB = A("SBF", [128], BF16)
        OG, OGB = A("OG", [512], BF16)
        f32t = {}
        for nm in ("E1", "E2", "EROW", "M2I", "P", "PT", "Z"):
            nb_ = 2 if nm in ("E1", "E2", "EROW", "M2I") else 4
            f32t[nm] = [A(nm + str(k), [128], F32) for k in range(nb_)]
        b16t = {}
        for nm in ("KB", "QKTM", "BV", "KD", "QG", "VN", "YB", "ZB"):
            nb_ = 1 if nm in ("VN", "YB") else 4
            bufs_ = [A(nm + str(k), [128], BF16) for k in range(nb_)]
            b16t[nm] = bufs_ * (4 // nb_)
        rr = {}

        def nxt(d, nm):
            k = rr.get(nm, 0)
            rr[nm] = k + 1
            return d[nm][k % 2]

        def dve(fn, reads, writes):
            P.op("dve", fn, reads=reads, writes=writes)

        P.op("pool", lambda e: e.memset(HNP[:, :, 0:2], 0.0), writes=(HNPB,))
        P.op("pool", lambda e: e.memset(HNP[:, :, 258:262], 0.0), writes=(HNPB,))
        for (t0, n, s) in self.tiles():
            self.norm_mod(t0, n, self.modcol(i, 1, s), self.modcol(i, 0, s), HNP[:, :, pcol(t0):pcol(t0) + n], HNPB)
        self.exchange(f"hx", HNP[:, :, 2308:2310], HNPB, self.hx_in, self.HXI, self.hx_out, self.HXO,
                      HXG, HXGB, din_view=self.hx_in.ap().rearrange("p (k t) -> p k t", t=2))
        pm = lambda r: self.cst("pm", r, 1)
        dve(lambda e: e.tensor_scalar(HXS, HXG[:, 0, :], pm(0), None, ALU.mult), (HXGB, self.CONSTB), (HXSB,))
        dve(lambda e: e.scalar_tensor_tensor(HXS, HXG[:, 1, :], pm(1), HXS, ALU.mult, ALU.add),
            (HXGB, HXSB, self.CONSTB), (HXSB,))
        hv = HXS.rearrange("p (k t) -> p k t", t=2)
        dve(lambda e: e.tensor_copy(HNP[:, :, 2310:2311], hv[:, :, 1:2]), (HXSB,), (HNPB,))
        dve(lambda e: e.tensor_copy(HNP[:, :, 2311:2312], hv[:, :, 0:1]), (HXSB,), (HNPB,))

        P.dma("pool", "wab", lambda e: e.dma_start(out=WAB, in_=W["wab"][j].rearrange("(k p) n -> p k n", p=128)),
              writes=(WABB,))
        for q in range(4):
            dve(lambda e, q=q: e.tensor_copy(SELC[:, 8 * q:8 * q + 8], self.IDF[:, 32 * q:32 * q + 8]),
                (self.IDFB,), (SELCB,))
        self.act(EAL, self.cst(f"alog{j}"), AF.Exp, reads=(self.CONSTB,), writes=(EALB,))
        for (t0, n, s) in self.tiles():
            ps, psb = self.psum()
            for k in range(NCH):
                self.mm(ps[:, 0:n], WAB[:, k, :], HNP[:, k, pcol(t0):pcol(t0) + n], k == 0, k == NCH - 1,
                        reads=(WABB, HNPB), writes=(psb,))
            self.act(O[0:64, t0:t0 + n], ps[0:64, 0:n], AF.Exp, reads=(psb, self.CONSTB), writes=(OB,),
                     bias=self.cst(f"dtb{j}")[0:64, :])
            self.act(O[0:64, t0:t0 + n], O[0:64, t0:t0 + n], AF.Ln, reads=(OB, self.ONEB), writes=(OB,),
                     bias=self.ONE1[0:64, :])
            dve(lambda e, t0=t0, n=n: e.tensor_scalar(O[0:64, t0:t0 + n], O[0:64, t0:t0 + n], EAL[0:64, :], None, ALU.mult),
                (OB, EALB), (OB,))
            self.act(G[64:128, t0:t0 + n], ps[64:128, 0:n], AF.Sigmoid, reads=(psb,), writes=(GB,))
        for c in range(2 * NCK):
            dve(lambda e, c=c: e.tensor_tensor_scan(G[0:64, 64 * c:64 * (c + 1)], self.ONEF[0:64, 0:64],
                                                    O[0:64, 64 * c:64 * (c + 1)], 0.0, ALU.mult, ALU.add),
                (OB, self.ONEFB), (GB,))
        Gv = G[32:64, :].rearrange("p (c w) -> p c w", w=64)
        dve(lambda e: e.tensor_copy(TOT[32:64, :], Gv[:, :, 63]), (GB,), (TOTB,))
        dve(lambda e: e.tensor_tensor(Gv, TOT[32:64, :, None].to_broadcast([32, 2 * NCK, 64]), Gv, ALU.subtract),
            (GB, TOTB), (GB,))
        dve(lambda e: e.tensor_tensor(G[32:64, :], G[32:64, :], O[32:64, :], ALU.add), (GB, OB), (GB,))
        for c in range(NCK):
            ps, psb = self.psum()
            self.mm(ps[:, 0:32], G[:, 128 * c:128 * (c + 1)], SELC, True, True, reads=(GB, SELCB), writes=(psb,))
            self.act(COLS[:, c, :], ps[:, 0:32], AF.Identity, reads=(psb,), writes=(COLSB,))
        G3 = lambda lo, hi: G[lo:hi, :].rearrange("p (c w) -> p c w", w=128)
        for sub, (RH, RHB, HL, HLB_) in enumerate(((KDF, KDFB, CDA, CDAB), (NBE, NBEB, CDB_, CDBB))):
            P.op("pool", lambda e, RH=RH: e.memset(RH, 0.0), writes=(RHB,))
            tf_ = 63 + 64 * sub
            tb_ = 64 * sub
            dve(lambda e, RH=RH, tf_=tf_: e.tensor_tensor(
                RH[0:32], SELC[0:32, None, 0:16].to_broadcast([32, NCK, 16]),
                G3(0, 32)[:, :, tf_:tf_ + 1].to_broadcast([32, NCK, 16]), ALU.mult), (SELCB, GB, RHB), (RHB,))
            dve(lambda e, RH=RH, tb_=tb_: e.tensor_tensor(
                RH[32:64], SELC[32:64, None, 0:16].to_broadcast([32, NCK, 16]),
                G3(32, 64)[:, :, tb_:tb_ + 1].to_broadcast([32, NCK, 16]), ALU.mult), (SELCB, GB, RHB), (RHB,))
            ps, psb = self.psum()
            self.mm(ps[:, 0:NCK * 16], self.ONEF, RH.rearrange("p c n -> p (c n)"), True, True,
                    reads=(RHB, self.ONEFB), writes=(psb,))
            self.act(HL.rearrange("p c n -> p (c n)"), ps[:, 0:NCK * 16], AF.Identity, reads=(psb,), writes=(HLB_,))
            hs = slice(64 * sub, 64 * sub + 64)
            dve(lambda e, HL=HL, hs=hs: e.tensor_copy(HLB[hs], HL[hs]), (HLB_,), (HLBB,))
        self.act(CDA, CDA, AF.Exp, reads=(CDAB, HLBB), writes=(CDAB,), scale=-1.0)
        self.act(CDB_, CDB_, AF.Exp, reads=(CDBB, HLBB), writes=(CDBB,), scale=-1.0)
        dve(lambda e: e.tensor_tensor(KDF, COLS[:, :, 0:16], HLB, ALU.subtract), (COLSB, HLBB), (KDFB,))
        self.act(KDF, KDF, AF.Exp, reads=(KDFB,), writes=(KDFB,))
        self.act(NBE, COLS[:, :, 0:16], AF.Exp, reads=(COLSB,), writes=(NBEB,), scale=-1.0)
        dve(lambda e: e.scalar_tensor_tensor(NBE, COLS[:, :, 16:32], -1.0, NBE, ALU.mult, ALU.mult),
            (COLSB, NBEB), (NBEB,))
        dve(lambda e: e.tensor_scalar(NB, COLS[:, :, 16:32], -1.0, None, ALU.mult), (COLSB,), (NBB,))

        w_in = W["dn_w_in"][j]
        wq = w_in[:, 0:4096].rearrange("(k p) (g hh d) -> p k g hh d", p=128, g=4, hh=8)
        cwo, _ = self.cols[f"cw{j}"]
        QSC = float(128 ** -0.5)

        for h in range(8):
            self.delta_head(i, j, h, last, locals())

    def delta_head(self, i, j, h, last, L):
        P = self.P
        g_ = lambda k: L[k]
        (WIN, WINB, WO, WOB, HNP, HNPB, G, GB, O, OB, COLS, COLSB, KDF, KDFB, CD, CDB, NBE, NBEB, NB, NBB,
         QT, QTB, KF, KFB, VTM, VTMB, PRET, PRETB, VTT, VTTB, DG5, DG5B, SELR, SELRB, SG, SGB, S, SB_,
         SBF, SBFB, OG, OGB) = [g_(k) for k in (
            "WIN", "WINB", "WO", "WOB", "HNP", "HNPB", "G", "GB", "O", "OB", "COLS", "COLSB", "KDF", "KDFB", "CD", "CDB",
            "NBE", "NBEB", "NB", "NBB", "QT", "QTB", "KF", "KFB", "VTM", "VTMB", "PRET", "PRETB", "VTT", "VTTB",
            "DG5", "DG5B", "SELR", "SELRB", "SG", "SGB", "S", "SB_", "SBF", "SBFB", "OG", "OGB")]
        f32t, b16t, nxt, dve, pcol, wq, cwo, QSC, NCK, W, SQ1, SQ1B = (g_(k) for k in (
            "f32t", "b16t", "nxt", "dve", "pcol", "wq", "cwo", "QSC", "NCK", "W", "SQ1", "SQ1B"))
        for g in range(4):
            P.dma("pool", f"win{g}", lambda e, g=g: e.dma_start(out=WIN[:, :, 128 * g:128 * (g + 1)], in_=wq[:, :, g, h, :]),
                  writes=(WINB,), indep=False)
        P.dma("pool", "wo", lambda e: e.dma_start(out=WO, in_=W["dn_w_o"][j][128 * h:128 * (h + 1), :]), writes=(WOB,))
        for q in range(4):
            r = 32 * q + h
            dve(lambda e, q=q, r=r: e.tensor_copy(SELR[:, q, :], self.IDF[:, r:r + 1].to_broadcast([128, 128])),
                (self.IDFB,), (SELRB,))
        segs = [(0, 0)] + [(260 + 256 * k, NCTX + 256 * k) for k in range(NLAT // 256)]
        for g in (0, 1, 2):
            wk = self.CONST[:, cwo + (8 * g + h) * 5:cwo + (8 * g + h + 1) * 5]
            dve(lambda e, wk=wk: e.tensor_tensor(
                DG5, self.IDENT[:, None, :].to_broadcast([128, 5, 128]),
                wk[:, :, None].to_broadcast([128, 5, 128]), ALU.mult), (self.IDENTB, self.CONSTB), (DG5B,))
            for (pc0, tk0) in segs:
                ps, psb = self.psum()
                for k in range(NCH):
                    self.mm(ps[:, 0:260], WIN[:, k, 128 * g:128 * (g + 1)], HNP[:, k, pc0:pc0 + 260], k == 0, k == NCH - 1,
                            reads=(WINB, HNPB), writes=(psb,))
                self.act(PRET, ps[:, 0:260], AF.Identity, reads=(psb,), writes=(PRETB,))
                p2, p2b = self.psum()
                for tap in range(5):
                    self.mm(p2[:, 0:256], DG5[:, tap, :], PRET[:, tap:tap + 256], tap == 0, tap == 4,
                            reads=(DG5B, PRETB), writes=(p2b,))
                if g < 2:
                    tm, tmb = self.tmp()
                    self.act(tm[:, 0:256], p2[:, 0:256], AF.Silu, reads=(p2b,), writes=(tmb,))
                    self.act(SQ1[:, 0:256], tm[:, 0:256], AF.Square, reads=(tmb,), writes=(SQ1B,))
                    p3, p3b = self.psum()
                    self.mm(p3[:, 0:256], self.ONESS, SQ1[:, 0:256], True, True, reads=(SQ1B, self.ONESSB), writes=(p3b,))
                    self.rstd_from(p3[:, 0:256], p3b, 256, self.RSTD, self.RSTDB)
                    dst_, dstb_ = (QT, QTB) if g == 0 else (KF, KFB)
                    dve(lambda e, g=g, tm=tm, tk0=tk0, dst_=dst_: e.scalar_tensor_tensor(
                        dst_[:, tk0:tk0 + 256], tm[:, 0:256], QSC if g == 0 else 1.0, self.RSTD[:, 0:256], ALU.mult, ALU.mult),
                        (tmb, self.RSTDB), (dstb_,))
                else:
                    self.act(VTT[:, 0:256], p2[:, 0:256], AF.Silu, reads=(p2b,), writes=(VTTB,))
                    for cc in range(2):
                        c = tk0 // 128 + cc
                        p4, p4b = self.psum()
                        self.mm(p4[:, 0:128], VTT[:, 128 * cc:128 * (cc + 1)], self.IDENT, True, True,
                                reads=(VTTB, self.IDENTB), writes=(p4b,))
                        self.act(VTM[:, c, :], p4[:, 0:128], AF.Identity, reads=(p4b,), writes=(VTMB,))

        def pre(d, c, si, ci):
            n = 8 * d + h
            ck = slice(128 * c, 128 * (c + 1))
            col = lambda Tn, off=0: Tn[:, c, off + n:off + n + 1]
            (E1, E1B), (E2, E2B), (EROW, EROWB), (M2I, M2IB) = (f32t["E1"][ci], f32t["E2"][ci],
                                                               f32t["EROW"][ci], f32t["M2I"][ci])
            cnt_ = {"P": 0, "PT": 0, "Z": 0}

            def nxc(nm):
                k_ = cnt_[nm]
                cnt_[nm] = k_ + 1
                return f32t[nm][2 * ci + k_ % 2]
            hr, hrb = self.psum()
            self.mm(hr[:, 0:128], SELR[:, d, :], G[:, ck], True, True, reads=(SELRB, GB), writes=(hrb,))
            yield
            br, brb = self.psum()
            self.mm(br[:, 0:128], SELR[:, 2 + d, :], G[:, ck], True, True, reads=(SELRB, GB), writes=(brb,))
            yield
            if d == 0:
                mDs, mTs, mTi = self.MLS, self.MUS, self.MU
            else:
                mDs, mTs, mTi = self.MUS, self.MLS, self.ML
            dve(lambda e: e.tensor_scalar(E1, hr[:, 0:128], col(COLS), 0.0, ALU.subtract, ALU.min), (hrb, COLSB), (E1B,))
            self.act(E1, E1, AF.Exp, reads=(E1B,), writes=(E1B,))
            dve(lambda e: e.tensor_tensor(E1, E1, mDs, ALU.mult), (E1B, self.MASKB), (E1B,))
            dve(lambda e: e.tensor_scalar(E2, hr[:, 0:128], col(COLS), 0.0, ALU.subtract, ALU.max), (hrb, COLSB), (E2B,))
            self.act(E2, E2, AF.Exp, reads=(E2B,), writes=(E2B,), scale=-1.0)
            dve(lambda e: e.tensor_tensor(M2I, E2, mTi, ALU.mult), (E2B, self.MASKB), (M2IB,))
            dve(lambda e: e.tensor_tensor(E2, E2, mTs, ALU.mult), (E2B, self.MASKB), (E2B,))
            self.act(EROW, hr[:, 0:128], AF.Exp, reads=(hrb,), writes=(EROWB,), scale=-1.0)
            KB, KBB = b16t["KB"][si]
            self.act(KB, KF[:, ck], AF.Identity, reads=(KFB,), writes=(KBB,))
            kk, kkb = self.psum()
            self.mm(kk[:, 0:128], KF[:, ck], KF[:, ck], True, True, reads=(KFB,), writes=(kkb,))
            yield
            Pm, PmB = nxc("P")
            PTm, PTmB = nxc("PT")
            Zm, ZmB = nxc("Z")
            dve(lambda e, Pm=Pm: e.scalar_tensor_tensor(Pm, kk[:, 0:128], col(NB), E1, ALU.mult, ALU.mult), (kkb, NBB, E1B), (PmB,))
            dve(lambda e: e.scalar_tensor_tensor(E2, kk[:, 0:128], -1.0, E2, ALU.mult, ALU.mult), (kkb, E2B), (E2B,))
            dve(lambda e, PTm=PTm: e.tensor_tensor(PTm, E2, br[:, 0:128], ALU.mult), (E2B, brb), (PTmB,))
            qk_, qkb_ = self.psum()
            self.mm(qk_[:, 0:128], KB, QT[:, ck], True, True, reads=(KBB, QTB), writes=(qkb_,))
            yield
            QKTM, QKTMB = b16t["QKTM"][si]
            dve(lambda e: e.tensor_tensor(QKTM, qk_[:, 0:128], M2I, ALU.mult), (qkb_, M2IB), (QKTMB,))
            dve(lambda e, Zm=Zm, PTm=PTm: e.tensor_tensor(Zm, self.IDF, PTm, ALU.add), (self.IDFB, PTmB), (ZmB,))
            for m in range(1, 6):
                Pn, PnB = nxc("P")
                pp, ppb = self.psum()
                self.mm(pp[:, 0:128], PTm, Pm, True, True, reads=(PTmB, PmB), writes=(ppb,))
                yield
                self.act(Pn, pp[:, 0:128], AF.Identity, reads=(ppb,), writes=(PnB,))
                if m < 5:
                    PTn, PTnB = nxc("PT")
                    pt, ptb = self.psum()
                    self.P.op("pe", lambda e, pt=pt, Pn=Pn: e.transpose(pt[:, 0:128], Pn, self.IDF),
                              reads=(PnB, self.IDFB), writes=(ptb,))
                    yield
                    self.act(PTn, pt[:, 0:128], AF.Identity, reads=(ptb,), writes=(PTnB,))
                Zn, ZnB = nxc("Z")
                pz, pzb = self.psum()
                self.mm(pz[:, 0:128], Pn, Zm, True, True, reads=(PnB, ZmB), writes=(pzb,))
                yield
                dve(lambda e, Zn=Zn, Zm=Zm, pz=pz: e.tensor_tensor(Zn, pz[:, 0:128], Zm, ALU.add), (pzb, ZmB), (ZnB,))
                Pm, PmB = Pn, PnB
                if m < 5:
                    PTm, PTmB = PTn, PTnB
                Zm, ZmB = Zn, ZnB
            BV, BVB = b16t["BV"][si]
            KD, KDB = b16t["KD"][si]
            QG, QGB = b16t["QG"][si]
            dve(lambda e: e.tensor_scalar(BV, VTM[:, c, :], col(COLS, 16), None, ALU.mult), (VTMB, COLSB), (BVB,))
            p4, p4b = self.psum()
            self.mm(p4[:, 0:128], KB, self.IDENT, True, True, reads=(KBB, self.IDENTB), writes=(p4b,))
            yield
            dve(lambda e: e.tensor_scalar(KD, p4[:, 0:128], col(KDF), None, ALU.mult), (p4b, KDFB), (KDB,))
            dve(lambda e: e.tensor_tensor(QG, QT[:, ck], EROW, ALU.mult), (QTB, EROWB), (QGB,))
            ZB, ZBB = b16t["ZB"][si]
            self.act(ZB, Zm, AF.Identity, reads=(ZmB,), writes=(ZBB,))
            yield

        def seq(d, c, si, first_touch):
            n = 8 * d + h
            ck = slice(128 * c, 128 * (c + 1))
            col = lambda Tn, off=0: Tn[:, c, off + n:off + n + 1]
            (KB, KBB), (QKTM, QKTMB), (BV, BVB), (KD, KDB), (QG, QGB), (ZB, ZBB) = (
                b16t["KB"][si], b16t["QKTM"][si], b16t["BV"][si], b16t["KD"][si], b16t["QG"][si], b16t["ZB"][si])
            Y, YB = nxt(b16t, "YB")
            VN, VNB = nxt(b16t, "VN")
            for a_ in ((0, 1) if d == 0 else (1, 0)):
                hs = slice(64 * a_, 64 * a_ + 64)
                tk = slice(128 * c + 64 * a_, 128 * c + 64 * a_ + 64)
                ks, ksb = self.psum()
                self.mm(ks[:, 0:128], KB, SBF, True, True, reads=(KBB, SBFB), writes=(ksb,))
                yield
                dve(lambda e, ks=ks: e.scalar_tensor_tensor(Y, ks[:, 0:128], col(NBE), BV, ALU.mult, ALU.add),
                    (ksb, NBEB, BVB), (YB,))
                vn, vnb = self.psum()
                self.mm(vn[:, 0:128], ZB, Y, True, True, reads=(ZBB, YB), writes=(vnb,))
                yield
                self.act(VN, vn[:, 0:128], AF.Identity, reads=(vnb,), writes=(VNB,))
                ot, otb = self.psum()
                self.mm(ot[:, 0:64], SBF, QG[:, hs], True, False, reads=(SBFB, QGB), writes=(otb,))
                yield
                self.mm(ot[:, 0:64], VN, QKTM[:, hs], False, True, reads=(VNB, QKTMB), writes=(otb,))
                yield
                if first_touch:
                    dve(lambda e, ot=ot, tk=tk: e.tensor_copy(O[:, tk], ot[:, 0:64]), (otb,), (OB,))
                else:
                    dve(lambda e, ot=ot, tk=tk: e.tensor_tensor(O[:, tk], O[:, tk], ot[:, 0:64], ALU.add), (otb, OB), (OB,))
                sn, snb = self.psum()
                self.mm(sn[:, 0:128], KD[hs, :], VN[hs, :], True, True, reads=(KDB, VNB), writes=(snb,))
                yield
                dve(lambda e, sn=sn, a_=a_: e.scalar_tensor_tensor(S, S, col(CD[a_]), sn[:, 0:128], ALU.mult, ALU.add),
                    (SB_, CDB[a_], snb), (SB_,))
                self.act(SBF, S, AF.Identity, reads=(SB_,), writes=(SBFB,))


        def interleave(gens):
            gens = [g for g in gens if g is not None]
            while gens:
                for g in list(gens):
                    try:
                        next(g)
                    except StopIteration:
                        gens.remove(g)

        def seq_many(d, items, first_touch):
            for (c_, si_) in items:
                yield from seq(d, c_, si_, first_touch)

        def run_scan(d, cs, first_touch):
            prev = []
            for t_ in range(0, len(cs), 2):
                cur = [(cs[k], k % 4) for k in range(t_, min(t_ + 2, len(cs)))]
                gens = [pre(d, c_, si_, idx) for idx, (c_, si_) in enumerate(cur)]
                if prev:
                    gens.append(seq_many(d, prev, first_touch))
                interleave(gens)
                prev = cur
            interleave([seq_many(d, prev, first_touch)])

        def zero_state():
            P.op("pool", lambda e: e.memset(S, 0.0), writes=(SB_,))
            P.op("pool", lambda e: e.memset(SBF, 0.0), writes=(SBFB,))

        zero_state()
        run_scan(0, list(range(NCK)), True)
        if self.dbg_stop == ("A", h):
            raise StopBuild()
        self.exchange("st", S, SB_, self.st_in, self.STI, self.st_out, self.STO, SG, SGB)
        pm = lambda r: self.cst("pm", r, 1)
        dve(lambda e: e.tensor_scalar(S, SG[:, 0, :], pm(0), None, ALU.mult), (SGB, self.CONSTB), (SB_,))
        dve(lambda e: e.scalar_tensor_tensor(S, SG[:, 1, :], pm(1), S, ALU.mult, ALU.add), (SGB, SB_, self.CONSTB), (SB_,))
        self.act(SBF, S, AF.Identity, reads=(SB_,), writes=(SBFB,))
        run_scan(1, list(range(NCK - 1, 1, -1)), False)
        if not last:
            zero_state()
            run_scan(1, [1, 0], False)
        if self.dbg_stop == ("B", h):
            raise StopBuild()
        for (t0, n, s) in self.tiles(include_ctx=not last):
            self.head_out(i, j, t0, n, s, WIN, WINB, WO, WOB, HNP, HNPB, O, OB, OG, OGB, pcol, SQ1, SQ1B)
        if self.dbg_stop == ("H", h):
            raise StopBuild()

    def head_out(self, i, j, t0, n, s, WIN, WINB, WO, WOB, HNP, HNPB, O, OB, OG, OGB, pcol, SQ1, SQ1B):
        self.act(SQ1[:, 0:n], O[:, t0:t0 + n], AF.Square, reads=(OB,), writes=(SQ1B,))
        ps, psb = self.psum()
        self.mm(ps[:, 0:n], self.ONES128, SQ1[:, 0:n], True, True, reads=(SQ1B, self.ONES128B), writes=(psb,))
        self.rstd_from(ps[:, 0:n], psb, n, self.RSTD, self.RSTDB)
        zp, zpb = self.psum()
        for k in range(NCH):
            self.mm(zp[:, 0:n], WIN[:, k, 384:512], HNP[:, k, pcol(t0):pcol(t0) + n], k == 0, k == NCH - 1,
                    reads=(WINB, HNPB), writes=(zpb,))
        zs, zsb = self.tmp()
        self.act(zs[:, 0:n], zp[:, 0:n], AF.Silu, reads=(zpb,), writes=(zsb,))
        t1, t1b = self.tmp()
        self.dve(lambda e: e.scalar_tensor_tensor(t1[:, 0:n], O[:, t0:t0 + n], self.cst(f"ng{j}"), self.RSTD[:, 0:n],
                                                  ALU.mult, ALU.mult), (OB, self.RSTDB, self.CONSTB), (t1b,))
        self.dve(lambda e: e.tensor_tensor(OG[:, 0:n], t1[:, 0:n], zs[:, 0:n], ALU.mult), (t1b, zsb), (OGB,))
        for fc in range(NCH):
            py, pyb = self.psum()
            self.mm(py[:, 0:n], WO[:, 128 * fc:128 * (fc + 1)], OG[:, 0:n], True, True, reads=(WOB, OGB), writes=(pyb,))
            self.dve(lambda e, fc=fc, py=py: e.scalar_tensor_tensor(
                self.X[:, fc, t0:t0 + n], py[:, 0:n], self.MOD[:, i, 2, fc, s:s + 1], self.X[:, fc, t0:t0 + n],
                ALU.mult, ALU.add), (pyb, self.MODB, self.XB), (self.XB,))


def needed_weights(nlayers):
    if nlayers == 0:
        return ()
    if nlayers == 1:
        return ("w_mod", "conv_w_pw1", "conv_w_pw2", "mlp_w1", "mlp_w2")
    return ("w_mod", "conv_w_pw1", "conv_w_pw2", "dn_w_in", "dn_w_o", "mlp_w1", "mlp_w2")


def make_inputs(inp, nlayers=DEPTH):
    maps = []
    shared = {k: np.ascontiguousarray(np.asarray(inp[k], np.float32)) for k in needed_weights(nlayers)}
    cols = None
    for core in range(NCORES):
        b, hf = core // 2, core % 2
        cp = pack_consts(inp, b, hf)
        cols = cp.cols
        xc = np.asarray(inp["ctx"][b], np.float32)
        xl = np.asarray(inp["x"][b, NLAT * hf:NLAT * (hf + 1)], np.float32)
        if hf == 1:
            xc = xc[::-1]
            xl = xl[::-1]
        xt = np.concatenate([xc, xl], axis=0)
        m = dict(shared)
        if nlayers >= 2:
            m["wab"] = make_wab(inp, hf)
        m["xT"] = np.ascontiguousarray(xt.T)
        m["consts"] = cp.array()
        maps.append(m)
    return maps, cols, maps[0]["consts"].shape[1]


def run(inp, nlayers=DEPTH):
    maps, cols, ncols = make_inputs(inp, nlayers)
    bld = Builder(nlayers, cols, ncols)
    nc = bld.build()
    res = run_bass_kernel_spmd(nc, maps, core_ids=list(range(NCORES)))
    out = np.zeros((4, SEQ, D), np.float32)
    for core in range(NCORES):
        b, hf = core // 2, core % 2
        o = res.results[core]["outT"].T
        out[b, NLAT * hf:NLAT * (hf + 1)] = o[::-1] if hf == 1 else o
    return out


def kernel(**inputs):
    inp = {k: np.asarray(v) for k, v in inputs.items()}
    return run(inp, DEPTH)
```

```python
import numpy as np
import concourse.bass as bass
import concourse.mybir as mybir
from concourse.bass_utils import run_bass_kernel_spmd

F32 = mybir.dt.float32
BF16 = mybir.dt.bfloat16
F32R = mybir.dt.float32r
FP32R = False
ALU = mybir.AluOpType
AF = mybir.ActivationFunctionType

D = 1024
NCH = 8
NCTX = 256
NLAT = 2048
T = NCTX + NLAT
SEQ = 4096
DEPTH = 4
EPS = 1e-6
KW = 31
PADW = 15
NCORES = 8
DEBUG_TAGS = False
TAGMAP = {}


class Buf:
    __slots__ = ("name", "last_w", "readers")

    def __init__(self, name):
        self.name = name
        self.last_w = None
        self.readers = []


class Op:
    __slots__ = ("eng", "fn", "deps", "needs_inc", "idx", "dma_key", "dma_val", "tag")

    def __init__(self, eng, fn):
        self.eng = eng
        self.fn = fn
        self.deps = []
        self.needs_inc = False
        self.idx = 0
        self.dma_key = None
        self.dma_val = 0


class Prog:
    ENGS = ("pe", "act", "dve", "pool", "sp")

    def __init__(self):
        self.ops = {e: [] for e in self.ENGS}
        self.dma_count = {}
        self.dma_total_keys = set()
        self.last_dma = {}
        self.dma_inc = {}
        self.fence_deps = []

    def fence(self):
        f = []
        for e in self.ENGS:
            for o in reversed(self.ops[e]):
                if o.dma_key is None:
                    o.needs_inc = True
                    f.append(o)
                    break
        f.extend(self.last_dma.values())
        self.fence_deps = f

    def _add_dep(self, op, dep, war=False):
        if dep is None or dep is op:
            return
        if dep.dma_key is None and dep.eng == op.eng:
            if op.eng == "pe":
                return
        if dep.dma_key is None:
            dep.needs_inc = True
        op.deps.append(dep)

    def op(self, eng, fn, reads=(), writes=()):
        o = Op(eng, fn)
        if DEBUG_TAGS:
            import sys as _sys
            f = _sys._getframe(1)
            tg = []
            while f is not None and len(tg) < 4:
                tg.append(f.f_lineno)
                f = f.f_back
            o.tag = tg
        for dep in self.fence_deps:
            if dep.dma_key is not None or dep.eng != eng or eng != "pe":
                o.deps.append(dep)
        for b in reads:
            self._add_dep(o, b.last_w)
        for b in writes:
            self._add_dep(o, b.last_w)
            for r in b.readers:
                self._add_dep(o, r, war=True)
        for b in reads:
            b.readers.append(o)
        for b in writes:
            b.last_w = o
            b.readers = []
        self.ops[eng].append(o)
        return o

    def dma(self, queue, key, fn, reads=(), writes=(), wait_total=False, indep=False, inc=16):
        o = self.op(queue, fn, reads, writes)
        self.dma_inc[key] = inc
        if wait_total or indep:
            o.deps = [d for d in o.deps if d.dma_key != key]
        n = self.dma_count.get(key, 0) + 1
        self.dma_count[key] = n
        o.dma_key = key
        o.dma_val = inc * n
        self.last_dma[key] = o
        if wait_total:
            self.dma_total_keys.add(key)
        return o

    def emit(self, nc, block, sems, final_waits=()):
        for e in self.ENGS:
            c = 0
            for o in self.ops[e]:
                if o.dma_key is None and o.needs_inc:
                    c += 1
                    o.idx = c

        def token(dep):
            if dep.dma_key is not None:
                if dep.dma_key in self.dma_total_keys:
                    return dep.dma_key, self.dma_inc[dep.dma_key] * self.dma_count[dep.dma_key]
                return dep.dma_key, dep.dma_val
            return dep.eng, dep.idx

        def run(e, eng):
            known = {}
            for o in self.ops[e]:
                for dep in o.deps:
                    k, v = token(dep)
                    if known.get(k, 0) >= v:
                        continue
                    eng.wait_ge(sems[k], v)
                    known[k] = v
                ins = o.fn(eng)
                if DEBUG_TAGS:
                    try:
                        TAGMAP[ins.ins.name] = o.tag
                    except Exception:
                        pass
                if o.dma_key is not None:
                    ins.then_inc(sems[o.dma_key], self.dma_inc[o.dma_key])
                elif o.needs_inc:
                    ins.then_inc(sems[e], 1)
            for k in final_waits.get(e, ()) if isinstance(final_waits, dict) else ():
                eng.wait_ge(sems[k], 16 * self.dma_count[k])

        @block.tensor
        def _(eng):
            run("pe", eng)

        @block.scalar
        def _(eng):
            run("act", eng)

        @block.vector
        def _(eng):
            run("dve", eng)

        @block.gpsimd
        def _(eng):
            run("pool", eng)

        @block.sync
        def _(eng):
            run("sp", eng)


def fm(vec):
    v = np.asarray(vec, np.float32).reshape(-1, 128)
    return np.ascontiguousarray(v.T)


class ConstPack:
    def __init__(self):
        self.cols = {}
        self.n = 0
        self.parts = []

    def add(self, name, arr):
        arr = np.asarray(arr, np.float32)
        assert arr.shape[0] == 128
        arr = arr.reshape(128, -1)
        self.cols[name] = (self.n, arr.shape[1])
        self.n += arr.shape[1]
        self.parts.append(arr)

    def array(self):
        return np.ascontiguousarray(np.concatenate(self.parts, axis=1))


def pack_consts(inp, b, hf):
    cp = ConstPack()
    cc = np.stack([fm(inp["c"][b]), fm(inp["c_ctx"])], axis=2)
    cp.add("c", cc)
    for i in range(DEPTH):
        cp.add(f"bmod{i}", fm(inp["b_mod"][i]))
        cp.add(f"n1g{i}", fm(inp["norm1_g"][i]))
        cp.add(f"n2g{i}", fm(inp["norm2_g"][i]))
    cp.add("fg", fm(inp["final_g"]))
    for j in range(2):
        cp.add(f"b1{j}", fm(inp["conv_b_pw1"][j]))
        wdw = np.asarray(inp["conv_w_dw"][j], np.float32)
        if hf == 1:
            wdw = wdw[::-1]
        w = wdw.reshape(KW, NCH, 128).transpose(2, 1, 0)
        cp.add(f"wdw{j}", w)
        cp.add(f"bdw{j}", fm(inp["conv_b_dw"][j]))
        cp.add(f"lng{j}", fm(inp["conv_ln_g"][j]))
        cp.add(f"lnb{j}", fm(inp["conv_ln_b"][j]))
        cp.add(f"b2{j}", fm(inp["conv_b_pw2"][j]))
    dirmap = (hf, 1 - hf)
    for j in range(2):
        cw = np.asarray(inp["dn_conv_w"][j], np.float32)
        if hf == 1:
            cw = cw[::-1]
        w = cw.reshape(5, 24, 128).transpose(2, 1, 0)
        cp.add(f"cw{j}", w)
        cp.add(f"ng{j}", np.asarray(inp["dn_norm_g"][j], np.float32).reshape(128, 1))
        al = np.zeros((128, 1), np.float32)
        db = np.zeros((128, 1), np.float32)
        for d in range(2):
            al[32 * d:32 * d + 8, 0] = inp["dn_a_log"][j][dirmap[d]]
            db[32 * d:32 * d + 8, 0] = inp["dn_dt_bias"][j][dirmap[d]]
        cp.add(f"alog{j}", al)
        cp.add(f"dtb{j}", db)
    pm = np.zeros((128, 2), np.float32)
    pm[:, 1 - hf] = 1.0
    cp.add("pm", pm)
    return cp


def make_wab(inp, hf):
    dirmap = (hf, 1 - hf)
    out = np.zeros((2, D, 128), np.float32)
    for j in range(2):
        w = np.asarray(inp["dn_w_in"][j], np.float32)
        for d in range(2):
            out[j, :, 32 * d:32 * d + 8] = w[:, 4096 + dirmap[d] * 8:4096 + dirmap[d] * 8 + 8]
            out[j, :, 64 + 32 * d:64 + 32 * d + 8] = w[:, 4096 + 16 + dirmap[d] * 8:4096 + 16 + dirmap[d] * 8 + 8]
    return out


SLOT_ELEMS = 4096
ARENA_BYTES = 115 * 1024 + 128


class StopBuild(Exception):
    pass


class Builder:
    dbg_stop = None

    def __init__(self, nlayers, cols, ncols, ncores=NCORES):
        self.ncores = ncores
        self.nlayers = nlayers
        self.cols = cols
        self.ncols = ncols
        self.P = Prog()
        self.nc = bass.Bass("TRN2", target_bir_lowering=False)
        self.psum_rr = 0
        self.slot_rr = 0
        self.ar_off = 0

    def phase(self, nslots, sq=True):
        self.P.fence()
        self.ar_off = 0
        self.SLOT = []
        self.SLOTB = []
        for i in range(nslots):
            v, b = self.aalloc(f"SLOT{i}", [SLOT_ELEMS], BF16)
            self.SLOT.append(v)
            self.SLOTB.append(b)
        self.slot_rr = 0
        if sq:
            self.SQ, self.SQB = self.aalloc("SQ", [NCH, 512], BF16)

    def aalloc(self, name, shape, dtype):
        n = 1
        for d_ in shape:
            n *= d_
        esz = 4 if dtype == F32 else 2
        nbytes = (n * esz + 63) // 64 * 64
        off = self.ar_off
        self.ar_off += nbytes
        assert self.ar_off <= ARENA_BYTES, (name, self.ar_off)
        if not hasattr(self, "amap"):
            self.amap = {}
        self.amap[name] = (off, tuple(shape), esz)
        v = self.ARENA[:, off // 2: off // 2 + n * esz // 2]
        if dtype == F32:
            v = v.bitcast(F32)
        if len(shape) > 1:
            names = [f"d{k}" for k in range(len(shape))]
            kw = {nm: sz for nm, sz in zip(names[:-1], shape[:-1])}
            v = v.rearrange(f"p ({' '.join(names)}) -> p {' '.join(names)}", **kw)
        return v, Buf(name)

    def cst(self, name, c0=0, n=None):
        o, w = self.cols[name]
        if n is None:
            n = w - c0
        return self.CONST[:, o + c0:o + c0 + n]

    def psum(self):
        i = self.psum_rr % len(self.PS)
        self.psum_rr += 1
        return self.PS[i], self.PSB[i]

    def tmp(self):
        i = self.tmp_rr % len(self.TMP)
        self.tmp_rr += 1
        return self.TMP[i], self.TMPB[i]

    def load_slot(self, dram_ap):
        i = self.slot_rr % len(self.SLOT)
        self.slot_rr += 1
        st, sb = self.SLOT[i], self.SLOTB[i]
        shp = dram_ap.shape
        view = st[:, 0:shp[1] * shp[2]].rearrange("p (k n) -> p k n", k=shp[1])
        self.P.dma("pool", f"slot{i}", lambda e, o=view, a=dram_ap: e.dma_start(out=o, in_=a),
                   reads=(), writes=(sb,))
        return view, sb

    def mm(self, out, lhsT, rhs, start, stop, reads, writes):
        if FP32R and lhsT.dtype == F32:
            lhsT = lhsT.bitcast(F32R)
            rhs = rhs.bitcast(F32R)
        self.P.op("pe", lambda e: e.matmul(out, lhsT, rhs, start=start, stop=stop),
                  reads=reads, writes=writes)

    def act(self, out, in_, func, reads, writes, bias=None, scale=None):
        kw = {}
        if bias is not None:
            kw["bias"] = bias
        if scale is not None:
            kw["scale"] = scale
        self.P.op("act", lambda e: e.activation(out, in_, func, **kw), reads=reads, writes=writes)

    def dve(self, fn, reads, writes):
        self.P.op("dve", fn, reads=reads, writes=writes)

    def rstd_from(self, src, srcb, n, out, outb):
        self.act(self.LNT[:, 0:n], src, AF.Ln, reads=(srcb, self.EPSB), writes=(self.LNTB,),
                 bias=self.EPST[:, 0:1])
        self.act(out[:, 0:n], self.LNT[:, 0:n], AF.Exp, reads=(self.LNTB,), writes=(outb,), scale=-0.5)

    def norm_mod(self, t0, n, Acol, Bcol, out3, outb, dst_dram=None):
        X, XB = self.X, self.XB
        self.act(self.SQ[:, :, 0:n], X[:, :, t0:t0 + n], AF.Square, reads=(XB,), writes=(self.SQB,))
        ps, psb = self.psum()
        for c in range(NCH):
            self.mm(ps[:, 0:n], self.ONES[:, :], self.SQ[:, c, 0:n], c == 0, c == NCH - 1,
                    reads=(self.SQB, self.ONESB), writes=(psb,))
        self.rstd_from(ps[:, 0:n], psb, n, self.RSTD, self.RSTDB)
        for c in range(NCH):
            tm, tmb = self.tmp()
            self.dve(lambda e, c=c, tm=tm: e.scalar_tensor_tensor(
                tm[:, 0:n], X[:, c, t0:t0 + n], Acol(c), self.RSTD[:, 0:n], ALU.mult, ALU.mult),
                reads=(XB, self.RSTDB, self.MODB, self.CONSTB), writes=(tmb,))
            if dst_dram is not None:
                self.P.dma("sp", "out_" + tmb.name, lambda e, c=c, tm=tm: e.dma_start(out=dst_dram(c), in_=tm[:, 0:n]),
                           reads=(tmb,), writes=())
            else:
                self.act(out3[:, c, 0:n], tm[:, 0:n], AF.Identity, reads=(tmb, self.MODB),
                         writes=(outb,), bias=Bcol(c))

    def compute_mod(self, i, w_mod):
        wv = w_mod[i].rearrange("(k p) n -> p k n", p=128)
        ps, psb = self.psum()
        psv = ps[:, 0:96].rearrange("p (n s) -> p n s", s=2)
        for s in range(12):
            sv, sb = self.load_slot(wv[:, :, 512 * s:512 * (s + 1)])
            for q in range(4):
                n = 4 * s + q
                for k in range(NCH):
                    self.mm(psv[:, n, :], sv[:, k, 128 * q:128 * (q + 1)], self.SC[:, k, :], k == 0, k == NCH - 1,
                            reads=(sb, self.SCB), writes=(psb,))
        o, _ = self.cols[f"bmod{i}"]
        bm = self.CONST[:, o:o + 48]
        M = self.MOD[:, i]
        for s in range(2):
            self.dve(lambda e, s=s: e.tensor_tensor(
                M[:, :, :, s], psv[:, :, s].rearrange("p (m c) -> p m c", m=6),
                bm.rearrange("p (m c) -> p m c", m=6), ALU.add),
                reads=(psb, self.CONSTB), writes=(self.MODB,))
        for m, gname in ((1, f"n1g{i}"), (4, f"n2g{i}")):
            for s in range(2):
                self.dve(lambda e, m=m, s=s, gname=gname: e.scalar_tensor_tensor(
                    M[:, m, :, s], M[:, m, :, s], 1.0, self.cst(gname), ALU.add, ALU.mult),
                    reads=(self.MODB, self.CONSTB), writes=(self.MODB,))

    def modcol(self, i, m, s):
        return lambda c: self.MOD[:, i, m, c, s:s + 1]

    def tiles(self, include_ctx=True):
        r = []
        if include_ctx:
            r.append((0, NCTX, 1))
        for k in range(NLAT // 512):
            r.append((NCTX + 512 * k, 512, 0))
        return r

    def conv_module(self, i, j, last, W):
        self.phase(6)
        HNT, HNTB = self.aalloc("HNT", [NCH, 512], BF16)
        UL, ULB = self.aalloc("UL", [NCH, 8 * (64 + 2 * PADW)], BF16)
        UC, UCB = self.aalloc("UC", [NCH, NCTX + 2 * PADW], BF16)
        DG, DGB = self.aalloc("DIAG", [KW, 128], BF16)
        CB, CBB = self.aalloc("CB", [NCH, 512], BF16)
        VT, VTB = self.aalloc("VT", [NCH, 512], BF16)
        self.MEAN, self.MEANB = self.aalloc("MEAN", [512], F32)
        self.VAR, self.VARB = self.aalloc("VAR", [512], F32)
        self.SIG, self.SIGB = self.aalloc("SIG", [512], F32)
        self.P.op("pool", lambda e: e.memset(UL, 0.0), writes=(ULB,))
        self.P.op("pool", lambda e: e.memset(UC, 0.0), writes=(UCB,))
        w1v = W["conv_w_pw1"][j].rearrange("(k p) n -> p k n", p=128)
        w2v = W["conv_w_pw2"][j].rearrange("(k p) n -> p k n", p=128)
        s1 = [self.load_slot(w1v[:, :, 512 * s:512 * (s + 1)]) for s in range(4)]
        s2 = [self.load_slot(w2v[:, :, 512 * s:512 * (s + 1)]) for s in range(2)]
        b1 = lambda c: self.cst(f"b1{j}", c, 1)
        for tile_ in self.tiles(include_ctx=not last):
            self.conv_tile(i, j, tile_, s1, s2, b1, HNT, HNTB, UL, ULB, UC, UCB, DG, DGB, CB, CBB, VT, VTB)

    def conv_tile(self, i, j, tile_, s1, s2, b1, HNT, HNTB, UL, ULB, UC, UCB, DG, DGB, CB, CBB, VT, VTB):
        if True:
            t0, n, s = tile_
            nrow = 1 if s == 1 else n // 64
            rl = n // nrow
            rs = rl + 2 * PADW
            self.norm_mod(t0, n, self.modcol(i, 1, s), self.modcol(i, 0, s), HNT, HNTB)
            U = UC if s == 1 else UL
            UB = UCB if s == 1 else ULB
            Uv = U.rearrange("p c (r w) -> p c r w", w=rs)
            for c in range(NCH):
                pa, pab = self.psum()
                sv, sb = s1[c // 4]
                for k in range(NCH):
                    self.mm(pa[:, 0:n], sv[:, k, 128 * (c % 4):128 * (c % 4 + 1)], HNT[:, k, 0:n],
                            k == 0, k == NCH - 1, reads=(sb, HNTB), writes=(pab,))
                pg, pgb = self.psum()
                sv, sb = s1[2 + c // 4]
                for k in range(NCH):
                    self.mm(pg[:, 0:n], sv[:, k, 128 * (c % 4):128 * (c % 4 + 1)], HNT[:, k, 0:n],
                            k == 0, k == NCH - 1, reads=(sb, HNTB), writes=(pgb,))
                self.act(self.SIG[:, 0:n], pg[:, 0:n], AF.Sigmoid, reads=(pgb, self.CONSTB),
                         writes=(self.SIGB,), bias=b1(NCH + c))
                self.dve(lambda e, c=c, pa=pa: e.scalar_tensor_tensor(
                    Uv[:, c, :, PADW:PADW + rl], pa[:, 0:n].rearrange("p (r w) -> p r w", w=rl), b1(c),
                    self.SIG[:, 0:n].rearrange("p (r w) -> p r w", w=rl), ALU.add, ALU.mult),
                    reads=(pab, self.SIGB, self.CONSTB), writes=(UB,))
            wo, _ = self.cols[f"wdw{j}"]
            for c in range(NCH):
                wk = self.CONST[:, wo + c * KW:wo + (c + 1) * KW]
                self.dve(lambda e, wk=wk: e.tensor_tensor(
                    DG, self.IDENT[:, None, :].to_broadcast([128, KW, 128]),
                    wk[:, :, None].to_broadcast([128, KW, 128]), ALU.mult),
                    reads=(self.IDENTB, self.CONSTB), writes=(DGB,))
                pc, pcb = self.psum()
                pcv = pc[:, 0:n].rearrange("p (r w) -> p r w", w=rl)
                for k in range(KW):
                    self.mm(pcv, DG[:, k, :], Uv[:, c, :, k:k + rl], k == 0, k == KW - 1,
                            reads=(DGB, UB), writes=(pcb,))
                bd = self.cst(f"bdw{j}", c, 1)
                self.act(CB[:, c, 0:n], pc[:, 0:n], AF.Identity, reads=(pcb, self.CONSTB),
                         writes=(CBB,), bias=bd)
                self.act(self.SQ[:, c, 0:n], pc[:, 0:n], AF.Square, reads=(pcb, self.CONSTB),
                         writes=(self.SQB,), bias=bd)
            pm, pmb = self.psum()
            for c in range(NCH):
                self.mm(pm[:, 0:n], self.ONES[:, :], CB[:, c, 0:n], c == 0, c == NCH - 1,
                        reads=(CBB, self.ONESB), writes=(pmb,))
            pq, pqb = self.psum()
            for c in range(NCH):
                self.mm(pq[:, 0:n], self.ONES[:, :], self.SQ[:, c, 0:n], c == 0, c == NCH - 1,
                        reads=(self.SQB, self.ONESB), writes=(pqb,))
            self.dve(lambda e, pm=pm: e.tensor_copy(self.MEAN[:, 0:n], pm[:, 0:n]),
                     reads=(pmb,), writes=(self.MEANB,))
            self.dve(lambda e: e.tensor_tensor(self.VAR[:, 0:n], self.MEAN[:, 0:n], self.MEAN[:, 0:n], ALU.mult),
                     reads=(self.MEANB,), writes=(self.VARB,))
            self.dve(lambda e, pq=pq: e.tensor_tensor(self.VAR[:, 0:n], pq[:, 0:n], self.VAR[:, 0:n], ALU.subtract),
                     reads=(pqb, self.VARB), writes=(self.VARB,))
            self.rstd_from(self.VAR[:, 0:n], self.VARB, n, self.RSTD, self.RSTDB)
            for c in range(NCH):
                tm, tmb = self.tmp()
                self.dve(lambda e, c=c, tm=tm: e.tensor_tensor(tm[:, 0:n], CB[:, c, 0:n], self.MEAN[:, 0:n], ALU.subtract),
                         reads=(CBB, self.MEANB), writes=(tmb,))
                self.dve(lambda e, tm=tm: e.tensor_tensor(tm[:, 0:n], tm[:, 0:n], self.RSTD[:, 0:n], ALU.mult),
                         reads=(tmb, self.RSTDB), writes=(tmb,))
                self.act(VT[:, c, 0:n], tm[:, 0:n], AF.Silu, reads=(tmb, self.CONSTB), writes=(VTB,),
                         bias=self.cst(f"lnb{j}", c, 1), scale=self.cst(f"lng{j}", c, 1))
            for fc in range(NCH):
                po, pob = self.psum()
                sv, sb = s2[fc // 4]
                for k in range(NCH):
                    self.mm(po[:, 0:n], sv[:, k, 128 * (fc % 4):128 * (fc % 4 + 1)], VT[:, k, 0:n],
                            k == 0, k == NCH - 1, reads=(sb, VTB), writes=(pob,))
                tm, tmb = self.tmp()
                self.dve(lambda e, fc=fc, po=po, tm=tm: e.tensor_scalar(
                    tm[:, 0:n], po[:, 0:n], self.cst(f"b2{j}", fc, 1), self.MOD[:, i, 2, fc, s:s + 1], ALU.add, ALU.mult),
                    reads=(pob, self.CONSTB, self.MODB), writes=(tmb,))
                self.dve(lambda e, fc=fc, tm=tm: e.tensor_tensor(
                    self.X[:, fc, t0:t0 + n], self.X[:, fc, t0:t0 + n], tm[:, 0:n], ALU.add),
                    reads=(tmb, self.XB), writes=(self.XB,))

    def mlp(self, i, last, W):
        if self.dbg_stop == ("M", i):
            raise StopBuild()
        self.phase(6)
        HN, HNB = self.aalloc("HN", [NCH, T], BF16)
        HQ, HQB = self.aalloc("HQ", [NCH, 512], BF16)
        w1v = W["mlp_w1"][i].rearrange("(k p) n -> p k n", p=128)
        w2v = W["mlp_w2"][i].rearrange("(k p) n -> p k n", p=128)
        tiles = self.tiles(include_ctx=not last)
        for (t0, n, s) in tiles:
            self.norm_mod(t0, n, self.modcol(i, 4, s), self.modcol(i, 3, s), HN[:, :, t0:t0 + n], HNB)
        for q in range(4):
            s1 = [self.load_slot(w1v[:, :, 1024 * q + 512 * s:1024 * q + 512 * (s + 1)]) for s in range(2)]
            s2 = [self.load_slot(w2v[:, 8 * q + 4 * s:8 * q + 4 * (s + 1), :]) for s in range(2)]
            for tile_ in tiles:
                self.mlp_tile(i, tile_, s1, s2, HN, HNB, HQ, HQB)

    def mlp_tile(self, i, tile_, s1, s2, HN, HNB, HQ, HQB):
        if True:
            if True:
                t0, n, s = tile_
                for hc in range(8):
                    ph, phb = self.psum()
                    sv, sb = s1[hc // 4]
                    for k in range(NCH):
                        self.mm(ph[:, 0:n], sv[:, k, 128 * (hc % 4):128 * (hc % 4 + 1)], HN[:, k, t0:t0 + n],
                                k == 0, k == NCH - 1, reads=(sb, HNB), writes=(phb,))
                    tm, tmb = self.tmp()
                    self.act(tm[:, 0:n], ph[:, 0:n], AF.Relu, reads=(phb,), writes=(tmb,))
                    self.dve(lambda e, hc=hc, tm=tm: e.tensor_tensor(HQ[:, hc, 0:n], tm[:, 0:n], tm[:, 0:n], ALU.mult),
                             reads=(tmb,), writes=(HQB,))
                for fc in range(NCH):
                    py, pyb = self.psum()
                    for hc in range(8):
                        sv, sb = s2[hc // 4]
                        self.mm(py[:, 0:n], sv[:, hc % 4, 128 * fc:128 * (fc + 1)], HQ[:, hc, 0:n],
                                hc == 0, hc == 7, reads=(sb, HQB), writes=(pyb,))
                    self.dve(lambda e, fc=fc, py=py: e.scalar_tensor_tensor(
                        self.X[:, fc, t0:t0 + n], py[:, 0:n], self.MOD[:, i, 5, fc, s:s + 1],
                        self.X[:, fc, t0:t0 + n], ALU.mult, ALU.add),
                        reads=(pyb, self.MODB, self.XB), writes=(self.XB,))

    def build(self):
        nc = self.nc
        P = self.P
        dt = nc.dram_tensor
        xT = dt("xT", [D, T], F32, kind="ExternalInput").ap()
        consts = dt("consts", [128, self.ncols], F32, kind="ExternalInput").ap()
        W = {}
        for name, shp in (("w_mod", [DEPTH, D, 6 * D]), ("conv_w_pw1", [2, D, 2 * D]), ("conv_w_pw2", [2, D, D]),
                          ("dn_w_in", [2, D, 4128]), ("dn_w_o", [2, D, D]),
                          ("mlp_w1", [DEPTH, D, 4 * D]), ("mlp_w2", [DEPTH, 4 * D, D])):
            if name in needed_weights(self.nlayers):
                W[name] = dt(name, shp, F32, kind="ExternalInput").ap()
        outT = dt("outT", [D, NLAT], F32, kind="ExternalOutput").ap()
        if self.nlayers >= 2:
            W["wab"] = dt("wab", [2, D, 128], F32, kind="ExternalInput").ap()
            self.hx_in = dt("hx_in", [128, 16], BF16)
            self.hx_out = dt("hx_out", [256, 16], BF16)
            self.st_in = dt("st_in", [128, 128], F32)
            self.st_out = dt("st_out", [256, 128], F32)
            self.HXI = Buf("hx_in"); self.HXO = Buf("hx_out"); self.STI = Buf("st_in"); self.STO = Buf("st_out")
        self.W = W

        from contextlib import ExitStack
        with ExitStack() as es:
            def sb(name, shape, dtype):
                return es.enter_context(nc.sbuf_tensor(name, shape, dtype))

            self.X = sb("X", [128, NCH, T], F32); self.XB = Buf("X")
            self.ARENA = sb("ARENA", [128, ARENA_BYTES // 2], BF16)
            self.CONST = sb("CONST", [128, self.ncols], F32); self.CONSTB = Buf("CONST")
            self.MOD = sb("MOD", [128, DEPTH, 6, NCH, 2], F32); self.MODB = Buf("MOD")
            self.SC = sb("SC", [128, NCH, 2], BF16); self.SCB = Buf("SC")
            self.ONES = sb("ONES", [128, 128], BF16); self.ONESB = Buf("ONES")
            self.IDENT = sb("IDENT", [128, 128], BF16); self.IDENTB = Buf("IDENT")
            self.IDF = sb("IDF", [128, 128], F32); self.IDFB = Buf("IDF")
            self.EPST = sb("EPST", [128, 1], F32); self.EPSB = Buf("EPS")
            for nm_ in ("ONES", "IDENT", "IDF"):
                setattr(self, nm_, getattr(self, nm_)[:, :])
            self.LNT = sb("LNT", [128, 512], F32); self.LNTB = Buf("LNT")
            self.RSTD = sb("RSTD", [128, 512], F32); self.RSTDB = Buf("RSTD")
            self.ONE1 = sb("ONE1", [128, 1], F32); self.ONEB = Buf("ONE1")
            self.ONEF = sb("ONEF", [128, 128], F32); self.ONEFB = Buf("ONEF")
            self.ONESS = sb("ONESS", [128, 128], BF16); self.ONESSB = Buf("ONESS")
            self.ONES128 = sb("ONES128", [128, 128], BF16); self.ONES128B = Buf("ONES128")
            self.ML = sb("ML", [128, 128], F32); self.MLS = sb("MLS", [128, 128], F32)
            self.MU = sb("MU", [128, 128], F32); self.MUS = sb("MUS", [128, 128], F32)
            self.MASKB = Buf("MASK")
            for nm_ in ("ONEF", "ONESS", "ONES128", "ML", "MLS", "MU", "MUS", "ONE1"):
                setattr(self, nm_, getattr(self, nm_)[:, :])
            self.TMP = [sb(f"TMP{i}", [128, 512], F32) for i in range(2)]
            self.TMPB = [Buf(f"TMP{i}") for i in range(2)]
            self.tmp_rr = 0
            self.OUTB = Buf("OUT")
            self.PS = [es.enter_context(nc.psum_tensor(f"PS{i}", [128, 512], F32)) for i in range(8)]
            self.PSB = [Buf(f"PS{i}") for i in range(8)]

            P.dma("sp", "const", lambda e: e.dma_start(out=self.CONST[:, :], in_=consts[:, :]),
                  writes=(self.CONSTB,), wait_total=True)
            xv = xT.rearrange("(c p) t -> p c t", p=128)
            for c in range(NCH):
                P.dma("sp", "xin", lambda e, c=c: e.dma_start(out=self.X[:, c, :], in_=xv[:, c, :]),
                      writes=(self.XB,), wait_total=True)
            P.op("pool", lambda e: e.memset(self.ONES[:, :], 1.0 / D), writes=(self.ONESB,))
            P.op("pool", lambda e: e.memset(self.EPST[:, :], EPS), writes=(self.EPSB,))
            P.op("pool", lambda e: e.memset(self.IDF[:, :], 0.0), writes=(self.IDFB,))
            P.op("pool", lambda e: e.affine_select(self.IDF[:, :], self.IDF[:, :], pattern=[[-1, 128]],
                                                    compare_op=ALU.not_equal, fill=1.0, base=0, channel_multiplier=1),
                 reads=(self.IDFB,), writes=(self.IDFB,))
            P.op("pool", lambda e: e.tensor_copy(self.IDENT[:, :], self.IDF[:, :]), reads=(self.IDFB,), writes=(self.IDENTB,))
            P.op("pool", lambda e: e.memset(self.ONE1[:, :], 1.0), writes=(self.ONEB,))
            P.op("pool", lambda e: e.memset(self.ONEF[:, :], 1.0), writes=(self.ONEFB,))
            P.op("pool", lambda e: e.memset(self.ONESS[:, :], 1.0), writes=(self.ONESSB,))
            P.op("pool", lambda e: e.memset(self.ONES128[:, :], 1.0 / 128), writes=(self.ONES128B,))
            for mt, base, cm, pat in ((self.ML, 0, 1, -1), (self.MLS, -1, 1, -1), (self.MU, 0, -1, 1), (self.MUS, -1, -1, 1)):
                P.op("pool", lambda e, mt=mt, base=base, cm=cm, pat=pat: e.affine_select(
                    mt[:, :], self.ONEF[:, :], pattern=[[pat, 128]], compare_op=ALU.is_ge, fill=0.0,
                    base=base, channel_multiplier=cm), reads=(self.ONEFB,), writes=(self.MASKB,))
            self.BD = sb("BD", [128, 128], F32)[:, :]
            P.op("pool", lambda e: e.memset(self.BD, 0.0), writes=(self.MASKB,))
            P.op("pool", lambda e: e.memset(self.BD[0:64, 0:64], 1.0), writes=(self.MASKB,))
            P.op("pool", lambda e: e.memset(self.BD[64:128, 64:128], 1.0), writes=(self.MASKB,))
            for mt in (self.ML, self.MLS, self.MU, self.MUS):
                P.op("pool", lambda e, mt=mt: e.tensor_tensor(mt, mt, self.BD, ALU.mult),
                     reads=(self.MASKB,), writes=(self.MASKB,))
            co, _ = self.cols["c"]
            self.act(self.SC[:, :, :].rearrange("p k s -> p (k s)"), self.CONST[:, co:co + 16], AF.Silu,
                     reads=(self.CONSTB,), writes=(self.SCB,))
            self.phase(6)
            for i in range(self.nlayers):
                self.compute_mod(i, W["w_mod"])

            try:
                for i in range(self.nlayers):
                    last = i == self.nlayers - 1
                    j = i // 2
                    if i % 2 == 0:
                        self.conv_module(i, j, last, W)
                    else:
                        self.delta_module(i, j, last, W)
                    self.mlp(i, last, W)
            except StopBuild:
                pass

            ov = outT.rearrange("(c p) t -> p c t", p=128)
            for (t0, n, s) in self.tiles(include_ctx=False):
                fgc = lambda c: self.cst("fg", c, 1)
                self.norm_mod(t0, n, fgc, None, None, None,
                              dst_dram=lambda c, t0=t0, n=n: ov[:, c, t0 - NCTX:t0 - NCTX + n])

            keys = list(self.P.dma_count.keys())
            sems = {}
            for k in list(Prog.ENGS) + keys:
                sems[k] = es.enter_context(nc.semaphore(f"s_{k}"))
            block = es.enter_context(nc.Block())
            self.P.emit(nc, block, sems, final_waits={"sp": tuple(k for k in keys if k.startswith("out_"))})
        return nc

    def exchange(self, key, src_ap, src_b, din, din_b, dout, dout_b, dst_ap, dst_b, din_view=None):
        P = self.P
        dv = din[:, :] if din_view is None else din_view
        P.dma("sp", key + "_a", lambda e: e.dma_start(out=dv, in_=src_ap), reads=(src_b,), writes=(din_b,))
        groups = [[2 * k, 2 * k + 1] for k in range(self.ncores // 2)]
        P.dma("pool", key + "_c", lambda e: e.collective_compute(
            "AllGather", ALU.bypass, replica_groups=groups, ins=[din.ap().opt()], outs=[dout.ap().opt()]),
            reads=(din_b,), writes=(dout_b,), inc=1)
        P.dma("sp", key + "_b", lambda e: e.dma_start(
            out=dst_ap, in_=dout.ap().rearrange("(r p) n -> p r n", p=128)), reads=(dout_b,),
            writes=dst_b if isinstance(dst_b, tuple) else (dst_b,))

    def delta_module(self, i, j, last, W):
        P = self.P
        self.phase(0, sq=False)
        A = self.aalloc
        NCK = T // 128
        PT_ = 2312
        pcol = lambda t: t + 2 if t < NCTX else t + 6
        WIN, WINB = A("WIN", [8, 512], BF16)
        WO, WOB = A("WO", [1024], BF16)
        WAB, WABB = WIN[:, :, 0:128], WINB
        HNP, HNPB = A("HNP", [8, PT_], BF16)
        G, GB = A("G", [T], F32)
        O, OB = A("O", [T], F32)
        COLS, COLSB = A("COLS", [NCK, 32], F32)
        HLB, HLBB = A("HLB", [NCK, 16], F32)
        KDF, KDFB = A("KDF", [NCK, 16], F32)
        CDA, CDAB = A("CDA", [NCK, 16], F32)
        CDB_, CDBB = A("CDB", [NCK, 16], F32)
        NBE, NBEB = A("NBE", [NCK, 16], F32)
        CD = (CDA, CDB_)
        CDB = (CDAB, CDBB)
        NB, NBB = HLB, HLBB
        TOT, TOTB = A("TOT", [2 * NCK], F32)
        EAL, EALB = A("EAL", [1], F32)
        SELC, SELCB = A("SELC", [32], F32)
        QT, QTB = A("QT", [T], BF16)
        KF, KFB = A("KF", [T], F32)
        off_ = self.amap["KF"][0]
        self.SQ = self.ARENA[:, off_ // 2: off_ // 2 + NCH * 512].rearrange("p (c t) -> p c t", c=NCH)
        self.SQB = KFB
        SQ1, SQ1B = A("SQ1", [512], BF16)
        SQH = [(SQ1[:, 0:256], Buf("SQ1a")), (SQ1[:, 256:512], Buf("SQ1b"))]
        VTM, VTMB = A("VTM", [NCK, 128], BF16)
        PRET2 = [A("PRET%d" % k_, [260], BF16) for k_ in range(2)]
        VTT2 = [A("VTT%d" % k_, [256], BF16) for k_ in range(2)]
        DG5, DG5B = A("DG5", [5, 128], BF16)
        SELR, SELRB = A("SELR", [4, 128], F32)
        HXG, HXGB = A("HXG", [2, 16], BF16)
        HXS, HXSB = A("HXS", [16], F32)
        S, SB_ = A("S", [128], F32)
        SBF, SBFB = A("SBF", [128], BF16)
        OG, OGB = A("OG", [512], BF16)
        f32t = {}
        for nm in ("E1", "E2", "EROW", "M2I", "P", "PT", "Z"):
            nb_ = 2 if nm in ("E1", "E2", "EROW", "M2I") else 4
            f32t[nm] = [A(nm + str(k), [128], F32) for k in range(nb_)]
        off_ = self.amap["E10"][0]
        SG = self.ARENA[:, off_ // 2: off_ // 2 + 512].bitcast(F32).rearrange("p (r n) -> p r n", r=2)
        SGB = (f32t["E1"][0][1], f32t["E1"][1][1])
        b16t = {}
        for nm in ("KB", "QKTM", "BV", "KD", "QG", "VN", "YB", "ZB"):
            nb_ = 1 if nm in ("VN", "YB") else 4
            bufs_ = [A(nm + str(k), [128], BF16) for k in range(nb_)]
            b16t[nm] = bufs_ * (4 // nb_)
        rr = {}

        def nxt(d, nm):
            k = rr.get(nm, 0)
            rr[nm] = k + 1
            return d[nm][k % 2]

        def dve(fn, reads, writes):
            P.op("dve", fn, reads=reads, writes=writes)

        P.op("pool", lambda e: e.memset(HNP[:, :, 0:2], 0.0), writes=(HNPB,))
        P.op("pool", lambda e: e.memset(HNP[:, :, 258:262], 0.0), writes=(HNPB,))
        for (t0, n, s) in self.tiles():
            self.norm_mod(t0, n, self.modcol(i, 1, s), self.modcol(i, 0, s), HNP[:, :, pcol(t0):pcol(t0) + n], HNPB)
        self.exchange(f"hx", HNP[:, :, 2308:2310], HNPB, self.hx_in, self.HXI, self.hx_out, self.HXO,
                      HXG, HXGB, din_view=self.hx_in.ap().rearrange("p (k t) -> p k t", t=2))
        pm = lambda r: self.cst("pm", r, 1)
        dve(lambda e: e.tensor_scalar(HXS, HXG[:, 0, :], pm(0), None, ALU.mult), (HXGB, self.CONSTB), (HXSB,))
        dve(lambda e: e.scalar_tensor_tensor(HXS, HXG[:, 1, :], pm(1), HXS, ALU.mult, ALU.add),
            (HXGB, HXSB, self.CONSTB), (HXSB,))
        hv = HXS.rearrange("p (k t) -> p k t", t=2)
        dve(lambda e: e.tensor_copy(HNP[:, :, 2310:2311], hv[:, :, 1:2]), (HXSB,), (HNPB,))
        dve(lambda e: e.tensor_copy(HNP[:, :, 2311:2312], hv[:, :, 0:1]), (HXSB,), (HNPB,))

        P.dma("pool", "wab", lambda e: e.dma_start(out=WAB, in_=W["wab"][j].rearrange("(k p) n -> p k n", p=128)),
              writes=(WABB,))
        for q in range(4):
            dve(lambda e, q=q: e.tensor_copy(SELC[:, 8 * q:8 * q + 8], self.IDF[:, 32 * q:32 * q + 8]),
                (self.IDFB,), (SELCB,))
        self.act(EAL, self.cst(f"alog{j}"), AF.Exp, reads=(self.CONSTB,), writes=(EALB,))
        for (t0, n, s) in self.tiles():
            ps, psb = self.psum()
            for k in range(NCH):
                self.mm(ps[:, 0:n], WAB[:, k, :], HNP[:, k, pcol(t0):pcol(t0) + n], k == 0, k == NCH - 1,
                        reads=(WABB, HNPB), writes=(psb,))
            self.act(O[0:64, t0:t0 + n], ps[0:64, 0:n], AF.Exp, reads=(psb, self.CONSTB), writes=(OB,),
                     bias=self.cst(f"dtb{j}")[0:64, :])
            self.act(O[0:64, t0:t0 + n], O[0:64, t0:t0 + n], AF.Ln, reads=(OB, self.ONEB), writes=(OB,),
                     bias=self.ONE1[0:64, :])
            dve(lambda e, t0=t0, n=n: e.tensor_scalar(O[0:64, t0:t0 + n], O[0:64, t0:t0 + n], EAL[0:64, :], None, ALU.mult),
                (OB, EALB), (OB,))
            self.act(G[64:128, t0:t0 + n], ps[64:128, 0:n], AF.Sigmoid, reads=(psb,), writes=(GB,))
        for c in range(2 * NCK):
            dve(lambda e, c=c: e.tensor_tensor_scan(G[0:64, 64 * c:64 * (c + 1)], self.ONEF[0:64, 0:64],
                                                    O[0:64, 64 * c:64 * (c + 1)], 0.0, ALU.mult, ALU.add),
                (OB, self.ONEFB), (GB,))
        Gv = G[32:64, :].rearrange("p (c w) -> p c w", w=64)
        dve(lambda e: e.tensor_copy(TOT[32:64, :], Gv[:, :, 63]), (GB,), (TOTB,))
        dve(lambda e: e.tensor_tensor(Gv, TOT[32:64, :, None].to_broadcast([32, 2 * NCK, 64]), Gv, ALU.subtract),
            (GB, TOTB), (GB,))
        dve(lambda e: e.tensor_tensor(G[32:64, :], G[32:64, :], O[32:64, :], ALU.add), (GB, OB), (GB,))
        for c in range(NCK):
            ps, psb = self.psum()
            self.mm(ps[:, 0:32], G[:, 128 * c:128 * (c + 1)], SELC, True, True, reads=(GB, SELCB), writes=(psb,))
            self.act(COLS[:, c, :], ps[:, 0:32], AF.Identity, reads=(psb,), writes=(COLSB,))
        G3 = lambda lo, hi: G[lo:hi, :].rearrange("p (c w) -> p c w", w=128)
        for sub, (RH, RHB, HL, HLB_) in enumerate(((KDF, KDFB, CDA, CDAB), (NBE, NBEB, CDB_, CDBB))):
            P.op("pool", lambda e, RH=RH: e.memset(RH, 0.0), writes=(RHB,))
            tf_ = 63 + 64 * sub
            tb_ = 64 * sub
            dve(lambda e, RH=RH, tf_=tf_: e.tensor_tensor(
                RH[0:32], SELC[0:32, None, 0:16].to_broadcast([32, NCK, 16]),
                G3(0, 32)[:, :, tf_:tf_ + 1].to_broadcast([32, NCK, 16]), ALU.mult), (SELCB, GB, RHB), (RHB,))
            dve(lambda e, RH=RH, tb_=tb_: e.tensor_tensor(
                RH[32:64], SELC[32:64, None, 0:16].to_broadcast([32, NCK, 16]),
                G3(32, 64)[:, :, tb_:tb_ + 1].to_broadcast([32, NCK, 16]), ALU.mult), (SELCB, GB, RHB), (RHB,))
            ps, psb = self.psum()
            self.mm(ps[:, 0:NCK * 16], self.ONEF, RH.rearrange("p c n -> p (c n)"), True, True,
                    reads=(RHB, self.ONEFB), writes=(psb,))
            self.act(HL.rearrange("p c n -> p (c n)"), ps[:, 0:NCK * 16], AF.Identity, reads=(psb,), writes=(HLB_,))
            hs = slice(64 * sub, 64 * sub + 64)
            dve(lambda e, HL=HL, hs=hs: e.tensor_copy(HLB[hs], HL[hs]), (HLB_,), (HLBB,))
        self.act(CDA, CDA, AF.Exp, reads=(CDAB, HLBB), writes=(CDAB,), scale=-1.0)
        self.act(CDB_, CDB_, AF.Exp, reads=(CDBB, HLBB), writes=(CDBB,), scale=-1.0)
        dve(lambda e: e.tensor_tensor(KDF, COLS[:, :, 0:16], HLB, ALU.subtract), (COLSB, HLBB), (KDFB,))
        self.act(KDF, KDF, AF.Exp, reads=(KDFB,), writes=(KDFB,))
        self.act(NBE, COLS[:, :, 0:16], AF.Exp, reads=(COLSB,), writes=(NBEB,), scale=-1.0)
        dve(lambda e: e.scalar_tensor_tensor(NBE, COLS[:, :, 16:32], -1.0, NBE, ALU.mult, ALU.mult),
            (COLSB, NBEB), (NBEB,))
        dve(lambda e: e.tensor_scalar(NB, COLS[:, :, 16:32], -1.0, None, ALU.mult), (COLSB,), (NBB,))

        w_in = W["dn_w_in"][j]
        wq = w_in[:, 0:4096].rearrange("(k p) (g hh d) -> p k g hh d", p=128, g=4, hh=8)
        cwo, _ = self.cols[f"cw{j}"]
        QSC = float(128 ** -0.5)

        for h in range(8):
            self.delta_head(i, j, h, last, locals())

    def delta_head(self, i, j, h, last, L):
        P = self.P
        g_ = lambda k: L[k]
        (WIN, WINB, WO, WOB, HNP, HNPB, G, GB, O, OB, COLS, COLSB, KDF, KDFB, CD, CDB, NBE, NBEB, NB, NBB,
         QT, QTB, KF, KFB, VTM, VTMB, PRET2, _p2, VTT2, _v2, DG5, DG5B, SELR, SELRB, SG, SGB, S, SB_,
         SBF, SBFB, OG, OGB) = [g_(k) for k in (
            "WIN", "WINB", "WO", "WOB", "HNP", "HNPB", "G", "GB", "O", "OB", "COLS", "COLSB", "KDF", "KDFB", "CD", "CDB",
            "NBE", "NBEB", "NB", "NBB", "QT", "QTB", "KF", "KFB", "VTM", "VTMB", "PRET2", "PRET2", "VTT2", "VTT2",
            "DG5", "DG5B", "SELR", "SELRB", "SG", "SGB", "S", "SB_", "SBF", "SBFB", "OG", "OGB")]
        f32t, b16t, nxt, dve, pcol, wq, cwo, QSC, NCK, W, SQ1, SQH = (g_(k) for k in (
            "f32t", "b16t", "nxt", "dve", "pcol", "wq", "cwo", "QSC", "NCK", "W", "SQ1", "SQH"))
        for g in range(4):
            P.dma("pool", f"win{g}", lambda e, g=g: e.dma_start(out=WIN[:, :, 128 * g:128 * (g + 1)], in_=wq[:, :, g, h, :]),
                  writes=(WINB,), indep=False)
        P.dma("pool", "wo", lambda e: e.dma_start(out=WO, in_=W["dn_w_o"][j][128 * h:128 * (h + 1), :]), writes=(WOB,))
        for q in range(4):
            r = 32 * q + h
            dve(lambda e, q=q, r=r: e.tensor_copy(SELR[:, q, :], self.IDF[:, r:r + 1].to_broadcast([128, 128])),
                (self.IDFB,), (SELRB,))
        segs = [(0, 0)] + [(260 + 256 * k, NCTX + 256 * k) for k in range(NLAT // 256)]
        for g in (0, 1, 2):
            wk = self.CONST[:, cwo + (8 * g + h) * 5:cwo + (8 * g + h + 1) * 5]
            dve(lambda e, wk=wk: e.tensor_tensor(
                DG5, self.IDENT[:, None, :].to_broadcast([128, 5, 128]),
                wk[:, :, None].to_broadcast([128, 5, 128]), ALU.mult), (self.IDENTB, self.CONSTB), (DG5B,))
            st = {}

            def stage_a(k, g=g, st=st):
                pc0, tk0 = segs[k]
                PRET, PRETB = PRET2[k % 2]
                ps, psb = self.psum()
                for kk_ in range(NCH):
                    self.mm(ps[:, 0:260], WIN[:, kk_, 128 * g:128 * (g + 1)], HNP[:, kk_, pc0:pc0 + 260], kk_ == 0, kk_ == NCH - 1,
                            reads=(WINB, HNPB), writes=(psb,))
                self.act(PRET, ps[:, 0:260], AF.Identity, reads=(psb,), writes=(PRETB,))

            def stage_b(k, g=g, st=st):
                pc0, tk0 = segs[k]
                PRET, PRETB = PRET2[k % 2]
                p2, p2b = self.psum()
                for tap in range(5):
                    self.mm(p2[:, 0:256], DG5[:, tap, :], PRET[:, tap:tap + 256], tap == 0, tap == 4,
                            reads=(DG5B, PRETB), writes=(p2b,))
                if g < 2:
                    tm, tmb = self.tmp()
                    sq, sqb = SQH[k % 2]
                    self.act(tm[:, 0:256], p2[:, 0:256], AF.Silu, reads=(p2b,), writes=(tmb,))
                    self.act(sq, tm[:, 0:256], AF.Square, reads=(tmb,), writes=(sqb,))
                    st[k] = (tm, tmb)
                else:
                    VTT, VTTB = VTT2[k % 2]
                    self.act(VTT, p2[:, 0:256], AF.Silu, reads=(p2b,), writes=(VTTB,))

            def stage_c(k, g=g, st=st):
                pc0, tk0 = segs[k]
                if g < 2:
                    tm, tmb = st[k]
                    sq, sqb = SQH[k % 2]
                    p3, p3b = self.psum()
                    self.mm(p3[:, 0:256], self.ONESS, sq, True, True, reads=(sqb, self.ONESSB), writes=(p3b,))
                    self.rstd_from(p3[:, 0:256], p3b, 256, self.RSTD, self.RSTDB)
                    dst_, dstb_ = (QT, QTB) if g == 0 else (KF, KFB)
                    dve(lambda e: e.scalar_tensor_tensor(
                        dst_[:, tk0:tk0 + 256], tm[:, 0:256], QSC if g == 0 else 1.0, self.RSTD[:, 0:256], ALU.mult, ALU.mult),
                        (tmb, self.RSTDB), (dstb_,))
                else:
                    VTT, VTTB = VTT2[k % 2]
                    for cc in range(2):
                        c = tk0 // 128 + cc
                        p4, p4b = self.psum()
                        self.mm(p4[:, 0:128], VTT[:, 128 * cc:128 * (cc + 1)], self.IDENT, True, True,
                                reads=(VTTB, self.IDENTB), writes=(p4b,))
                        self.act(VTM[:, c, :], p4[:, 0:128], AF.Identity, reads=(p4b,), writes=(VTMB,))

            ns_ = len(segs)
            for t_ in range(ns_ + 2):
                if t_ < ns_:
                    stage_a(t_)
                if 0 <= t_ - 1 < ns_:
                    stage_b(t_ - 1)
                if 0 <= t_ - 2 < ns_:
                    stage_c(t_ - 2)

        def pre(d, c, si, ci):
            n = 8 * d + h
            ck = slice(128 * c, 128 * (c + 1))
            col = lambda Tn, off=0: Tn[:, c, off + n:off + n + 1]
            (E1, E1B), (E2, E2B), (EROW, EROWB), (M2I, M2IB) = (f32t["E1"][ci], f32t["E2"][ci],
                                                               f32t["EROW"][ci], f32t["M2I"][ci])
            cnt_ = {"P": 0, "PT": 0, "Z": 0}

            def nxc(nm):
                k_ = cnt_[nm]
                cnt_[nm] = k_ + 1
                return f32t[nm][2 * ci + k_ % 2]
            hr, hrb = self.psum()
            self.mm(hr[:, 0:128], SELR[:, d, :], G[:, ck], True, True, reads=(SELRB, GB), writes=(hrb,))
            yield
            br, brb = self.psum()
            self.mm(br[:, 0:128], SELR[:, 2 + d, :], G[:, ck], True, True, reads=(SELRB, GB), writes=(brb,))
            yield
            if d == 0:
                mDs, mTs, mTi = self.MLS, self.MUS, self.MU
            else:
                mDs, mTs, mTi = self.MUS, self.MLS, self.ML
            dve(lambda e: e.tensor_scalar(E1, hr[:, 0:128], col(COLS), 0.0, ALU.subtract, ALU.min), (hrb, COLSB), (E1B,))
            self.act(E1, E1, AF.Exp, reads=(E1B,), writes=(E1B,))
            dve(lambda e: e.tensor_tensor(E1, E1, mDs, ALU.mult), (E1B, self.MASKB), (E1B,))
            dve(lambda e: e.tensor_scalar(E2, hr[:, 0:128], col(COLS), 0.0, ALU.subtract, ALU.max), (hrb, COLSB), (E2B,))
            self.act(E2, E2, AF.Exp, reads=(E2B,), writes=(E2B,), scale=-1.0)
            dve(lambda e: e.tensor_tensor(M2I, E2, mTi, ALU.mult), (E2B, self.MASKB), (M2IB,))
            dve(lambda e: e.tensor_tensor(E2, E2, mTs, ALU.mult), (E2B, self.MASKB), (E2B,))
            self.act(EROW, hr[:, 0:128], AF.Exp, reads=(hrb,), writes=(EROWB,), scale=-1.0)
            KB, KBB = b16t["KB"][si]
            self.act(KB, KF[:, ck], AF.Identity, reads=(KFB,), writes=(KBB,))
            kk, kkb = self.psum()
            self.mm(kk[:, 0:128], KF[:, ck], KF[:, ck], True, True, reads=(KFB,), writes=(kkb,))
            yield
            Pm, PmB = nxc("P")
            PTm, PTmB = nxc("PT")
            Zm, ZmB = nxc("Z")
            dve(lambda e, Pm=Pm: e.scalar_tensor_tensor(Pm, kk[:, 0:128], col(NB), E1, ALU.mult, ALU.mult), (kkb, NBB, E1B), (PmB,))
            dve(lambda e: e.scalar_tensor_tensor(E2, kk[:, 0:128], -1.0, E2, ALU.mult, ALU.mult), (kkb, E2B), (E2B,))
            dve(lambda e, PTm=PTm: e.tensor_tensor(PTm, E2, br[:, 0:128], ALU.mult), (E2B, brb), (PTmB,))
            qk_, qkb_ = self.psum()
            self.mm(qk_[:, 0:128], KB, QT[:, ck], True, True, reads=(KBB, QTB), writes=(qkb_,))
            yield
            QKTM, QKTMB = b16t["QKTM"][si]
            dve(lambda e: e.tensor_tensor(QKTM, qk_[:, 0:128], M2I, ALU.mult), (qkb_, M2IB), (QKTMB,))
            dve(lambda e, Zm=Zm, PTm=PTm: e.tensor_tensor(Zm, self.IDF, PTm, ALU.add), (self.IDFB, PTmB), (ZmB,))
            for m in range(1, 6):
                Pn, PnB = nxc("P")
                pp, ppb = self.psum()
                self.mm(pp[:, 0:128], PTm, Pm, True, True, reads=(PTmB, PmB), writes=(ppb,))
                yield
                self.act(Pn, pp[:, 0:128], AF.Identity, reads=(ppb,), writes=(PnB,))
                if m < 5:
                    PTn, PTnB = nxc("PT")
                    pt, ptb = self.psum()
                    self.P.op("pe", lambda e, pt=pt, Pn=Pn: e.transpose(pt[:, 0:128], Pn, self.IDF),
                              reads=(PnB, self.IDFB), writes=(ptb,))
                    yield
                    self.act(PTn, pt[:, 0:128], AF.Identity, reads=(ptb,), writes=(PTnB,))
                Zn, ZnB = nxc("Z")
                pz, pzb = self.psum()
                self.mm(pz[:, 0:128], Pn, Zm, True, True, reads=(PnB, ZmB), writes=(pzb,))
                yield
                dve(lambda e, Zn=Zn, Zm=Zm, pz=pz: e.tensor_tensor(Zn, pz[:, 0:128], Zm, ALU.add), (pzb, ZmB), (ZnB,))
                Pm, PmB = Pn, PnB
                if m < 5:
                    PTm, PTmB = PTn, PTnB
                Zm, ZmB = Zn, ZnB
            BV, BVB = b16t["BV"][si]
            KD, KDB = b16t["KD"][si]
            QG, QGB = b16t["QG"][si]
            dve(lambda e: e.tensor_scalar(BV, VTM[:, c, :], col(COLS, 16), None, ALU.mult), (VTMB, COLSB), (BVB,))
            p4, p4b = self.psum()
            self.mm(p4[:, 0:128], KB, self.IDENT, True, True, reads=(KBB, self.IDENTB), writes=(p4b,))
            yield
            dve(lambda e: e.tensor_scalar(KD, p4[:, 0:128], col(KDF), None, ALU.mult), (p4b, KDFB), (KDB,))
            dve(lambda e: e.tensor_tensor(QG, QT[:, ck], EROW, ALU.mult), (QTB, EROWB), (QGB,))
            ZB, ZBB = b16t["ZB"][si]
            self.act(ZB, Zm, AF.Identity, reads=(ZmB,), writes=(ZBB,))
            yield

        def seq(d, c, si, first_touch):
            n = 8 * d + h
            ck = slice(128 * c, 128 * (c + 1))
            col = lambda Tn, off=0: Tn[:, c, off + n:off + n + 1]
            (KB, KBB), (QKTM, QKTMB), (BV, BVB), (KD, KDB), (QG, QGB), (ZB, ZBB) = (
                b16t["KB"][si], b16t["QKTM"][si], b16t["BV"][si], b16t["KD"][si], b16t["QG"][si], b16t["ZB"][si])
            Y, YB = nxt(b16t, "YB")
            VN, VNB = nxt(b16t, "VN")
            for a_ in ((0, 1) if d == 0 else (1, 0)):
                hs = slice(64 * a_, 64 * a_ + 64)
                tk = slice(128 * c + 64 * a_, 128 * c + 64 * a_ + 64)
                ks, ksb = self.psum()
                self.mm(ks[:, 0:128], KB, SBF, True, True, reads=(KBB, SBFB), writes=(ksb,))
                yield
                dve(lambda e, ks=ks: e.scalar_tensor_tensor(Y, ks[:, 0:128], col(NBE), BV, ALU.mult, ALU.add),
                    (ksb, NBEB, BVB), (YB,))
                vn, vnb = self.psum()
                self.mm(vn[:, 0:128], ZB, Y, True, True, reads=(ZBB, YB), writes=(vnb,))
                yield
                self.act(VN, vn[:, 0:128], AF.Identity, reads=(vnb,), writes=(VNB,))
                ot, otb = self.psum()
                self.mm(ot[:, 0:64], SBF, QG[:, hs], True, False, reads=(SBFB, QGB), writes=(otb,))
                yield
                self.mm(ot[:, 0:64], VN, QKTM[:, hs], False, True, reads=(VNB, QKTMB), writes=(otb,))
                yield
                if first_touch:
                    dve(lambda e, ot=ot, tk=tk: e.tensor_copy(O[:, tk], ot[:, 0:64]), (otb,), (OB,))
                else:
                    dve(lambda e, ot=ot, tk=tk: e.tensor_tensor(O[:, tk], O[:, tk], ot[:, 0:64], ALU.add), (otb, OB), (OB,))
                sn, snb = self.psum()
                self.mm(sn[:, 0:128], KD[hs, :], VN[hs, :], True, True, reads=(KDB, VNB), writes=(snb,))
                yield
                dve(lambda e, sn=sn, a_=a_: e.scalar_tensor_tensor(S, S, col(CD[a_]), sn[:, 0:128], ALU.mult, ALU.add),
                    (SB_, CDB[a_], snb), (SB_,))
                self.act(SBF, S, AF.Identity, reads=(SB_,), writes=(SBFB,))


        def interleave(gens):
            gens = [g for g in gens if g is not None]
            while gens:
                for g in list(gens):
                    try:
                        next(g)
                    except StopIteration:
                        gens.remove(g)

        def seq_many(d, items, first_touch):
            for (c_, si_) in items:
                yield from seq(d, c_, si_, first_touch)

        def run_scan(d, cs, first_touch):
            prev = []
            for t_ in range(0, len(cs), 2):
                cur = [(cs[k], k % 4) for k in range(t_, min(t_ + 2, len(cs)))]
                gens = [pre(d, c_, si_, idx) for idx, (c_, si_) in enumerate(cur)]
                if prev:
                    gens.append(seq_many(d, prev, first_touch))
                interleave(gens)
                prev = cur
            interleave([seq_many(d, prev, first_touch)])

        def zero_state():
            P.op("pool", lambda e: e.memset(S, 0.0), writes=(SB_,))
            P.op("pool", lambda e: e.memset(SBF, 0.0), writes=(SBFB,))

        zero_state()
        run_scan(0, list(range(NCK)), True)
        if self.dbg_stop == ("A", h):
            raise StopBuild()
        self.exchange("st", S, SB_, self.st_in, self.STI, self.st_out, self.STO, SG, SGB)
        pm = lambda r: self.cst("pm", r, 1)
        dve(lambda e: e.tensor_scalar(S, SG[:, 0, :], pm(0), None, ALU.mult), SGB + (self.CONSTB,), (SB_,))
        dve(lambda e: e.scalar_tensor_tensor(S, SG[:, 1, :], pm(1), S, ALU.mult, ALU.add), SGB + (SB_, self.CONSTB), (SB_,))
        self.act(SBF, S, AF.Identity, reads=(SB_,), writes=(SBFB,))
        run_scan(1, list(range(NCK - 1, 1, -1)), False)
        if not last:
            zero_state()
            run_scan(1, [1, 0], False)
        if self.dbg_stop == ("B", h):
            raise StopBuild()
        for (t0, n, s) in self.tiles(include_ctx=not last):
            self.head_out(i, j, t0, n, s, WIN, WINB, WO, WOB, HNP, HNPB, O, OB, OG, OGB, pcol, SQ1, SQH)
        if self.dbg_stop == ("H", h):
            raise StopBuild()

    def head_out(self, i, j, t0, n, s, WIN, WINB, WO, WOB, HNP, HNPB, O, OB, OG, OGB, pcol, SQ1, SQH):
        sqbs = (SQH[0][1], SQH[1][1])
        self.act(SQ1[:, 0:n], O[:, t0:t0 + n], AF.Square, reads=(OB,), writes=sqbs)
        ps, psb = self.psum()
        self.mm(ps[:, 0:n], self.ONES128, SQ1[:, 0:n], True, True, reads=sqbs + (self.ONES128B,), writes=(psb,))
        self.rstd_from(ps[:, 0:n], psb, n, self.RSTD, self.RSTDB)
        zp, zpb = self.psum()
        for k in range(NCH):
            self.mm(zp[:, 0:n], WIN[:, k, 384:512], HNP[:, k, pcol(t0):pcol(t0) + n], k == 0, k == NCH - 1,
                    reads=(WINB, HNPB), writes=(zpb,))
        zs, zsb = self.tmp()
        self.act(zs[:, 0:n], zp[:, 0:n], AF.Silu, reads=(zpb,), writes=(zsb,))
        t1, t1b = self.tmp()
        self.dve(lambda e: e.scalar_tensor_tensor(t1[:, 0:n], O[:, t0:t0 + n], self.cst(f"ng{j}"), self.RSTD[:, 0:n],
                                                  ALU.mult, ALU.mult), (OB, self.RSTDB, self.CONSTB), (t1b,))
        self.dve(lambda e: e.tensor_tensor(OG[:, 0:n], t1[:, 0:n], zs[:, 0:n], ALU.mult), (t1b, zsb), (OGB,))
        for fc in range(NCH):
            py, pyb = self.psum()
            self.mm(py[:, 0:n], WO[:, 128 * fc:128 * (fc + 1)], OG[:, 0:n], True, True, reads=(WOB, OGB), writes=(pyb,))
            self.dve(lambda e, fc=fc, py=py: e.scalar_tensor_tensor(
                self.X[:, fc, t0:t0 + n], py[:, 0:n], self.MOD[:, i, 2, fc, s:s + 1], self.X[:, fc, t0:t0 + n],
                ALU.mult, ALU.add), (pyb, self.MODB, self.XB), (self.XB,))


def needed_weights(nlayers):
    if nlayers == 0:
        return ()
    if nlayers == 1:
        return ("w_mod", "conv_w_pw1", "conv_w_pw2", "mlp_w1", "mlp_w2")
    return ("w_mod", "conv_w_pw1", "conv_w_pw2", "dn_w_in", "dn_w_o", "mlp_w1", "mlp_w2")


def make_inputs(inp, nlayers=DEPTH):
    maps = []
    shared = {k: np.ascontiguousarray(np.asarray(inp[k], np.float32)) for k in needed_weights(nlayers)}
    cols = None
    for core in range(NCORES):
        b, hf = core // 2, core % 2
        cp = pack_consts(inp, b, hf)
        cols = cp.cols
        xc = np.asarray(inp["ctx"][b], np.float32)
        xl = np.asarray(inp["x"][b, NLAT * hf:NLAT * (hf + 1)], np.float32)
        if hf == 1:
            xc = xc[::-1]
            xl = xl[::-1]
        xt = np.concatenate([xc, xl], axis=0)
        m = dict(shared)
        if nlayers >= 2:
            m["wab"] = make_wab(inp, hf)
        m["xT"] = np.ascontiguousarray(xt.T)
        m["consts"] = cp.array()
        maps.append(m)
    return maps, cols, maps[0]["consts"].shape[1]


def run(inp, nlayers=DEPTH):
    maps, cols, ncols = make_inputs(inp, nlayers)
    bld = Builder(nlayers, cols, ncols)
    nc = bld.build()
    res = run_bass_kernel_spmd(nc, maps, core_ids=list(range(NCORES)))
    out = np.zeros((4, SEQ, D), np.float32)
    for core in range(NCORES):
        b, hf = core // 2, core % 2
        o = res.results[core]["outT"].T
        out[b, NLAT * hf:NLAT * (hf + 1)] = o[::-1] if hf == 1 else o
    return out


def kernel(**inputs):
    inp = {k: np.asarray(v) for k, v in inputs.items()}
    return run(inp, DEPTH)
```

```python
import numpy as np
import concourse.bass as bass
import concourse.mybir as mybir
from concourse.bass_utils import run_bass_kernel_spmd

F32 = mybir.dt.float32
BF16 = mybir.dt.bfloat16
F32R = mybir.dt.float32r
FP32R = False
ALU = mybir.AluOpType
AF = mybir.ActivationFunctionType

D = 1024
NCH = 8
NCTX = 256
NLAT = 2048
T = NCTX + NLAT
SEQ = 4096
DEPTH = 4
EPS = 1e-6
KW = 31
PADW = 15
NCORES = 8
DEBUG_TAGS = False
TAGMAP = {}


class Buf:
    __slots__ = ("name", "last_w", "readers")

    def __init__(self, name):
        self.name = name
        self.last_w = None
        self.readers = []


class Op:
    __slots__ = ("eng", "fn", "deps", "needs_inc", "idx", "dma_key", "dma_val", "tag")

    def __init__(self, eng, fn):
        self.eng = eng
        self.fn = fn
        self.deps = []
        self.needs_inc = False
        self.idx = 0
        self.dma_key = None
        self.dma_val = 0


class Prog:
    ENGS = ("pe", "act", "dve", "pool", "sp")

    def __init__(self):
        self.ops = {e: [] for e in self.ENGS}
        self.dma_count = {}
        self.dma_total_keys = set()
        self.last_dma = {}
        self.dma_inc = {}
        self.fence_deps = []

    def fence(self):
        f = []
        for e in self.ENGS:
            for o in reversed(self.ops[e]):
                if o.dma_key is None:
                    o.needs_inc = True
                    f.append(o)
                    break
        f.extend(self.last_dma.values())
        self.fence_deps = f

    def _add_dep(self, op, dep, war=False):
        if dep is None or dep is op:
            return
        if dep.dma_key is None and dep.eng == op.eng:
            if op.eng == "pe":
                return
        if dep.dma_key is None:
            dep.needs_inc = True
        op.deps.append(dep)

    def op(self, eng, fn, reads=(), writes=()):
        o = Op(eng, fn)
        if DEBUG_TAGS:
            import sys as _sys
            f = _sys._getframe(1)
            tg = []
            while f is not None and len(tg) < 4:
                tg.append(f.f_lineno)
                f = f.f_back
            o.tag = tg
        for dep in self.fence_deps:
            if dep.dma_key is not None or dep.eng != eng or eng != "pe":
                o.deps.append(dep)
        for b in reads:
            self._add_dep(o, b.last_w)
        for b in writes:
            self._add_dep(o, b.last_w)
            for r in b.readers:
                self._add_dep(o, r, war=True)
        for b in reads:
            b.readers.append(o)
        for b in writes:
            b.last_w = o
            b.readers = []
        self.ops[eng].append(o)
        return o

    def dma(self, queue, key, fn, reads=(), writes=(), wait_total=False, indep=False, inc=16):
        o = self.op(queue, fn, reads, writes)
        self.dma_inc[key] = inc
        if wait_total or indep:
            o.deps = [d for d in o.deps if d.dma_key != key]
        n = self.dma_count.get(key, 0) + 1
        self.dma_count[key] = n
        o.dma_key = key
        o.dma_val = inc * n
        self.last_dma[key] = o
        if wait_total:
            self.dma_total_keys.add(key)
        return o

    def emit(self, nc, block, sems, final_waits=()):
        for e in self.ENGS:
            c = 0
            for o in self.ops[e]:
                if o.dma_key is None and o.needs_inc:
                    c += 1
                    o.idx = c

        def token(dep):
            if dep.dma_key is not None:
                if dep.dma_key in self.dma_total_keys:
                    return dep.dma_key, self.dma_inc[dep.dma_key] * self.dma_count[dep.dma_key]
                return dep.dma_key, dep.dma_val
            return dep.eng, dep.idx

        def run(e, eng):
            known = {}
            for o in self.ops[e]:
                for dep in o.deps:
                    k, v = token(dep)
                    if known.get(k, 0) >= v:
                        continue
                    eng.wait_ge(sems[k], v)
                    known[k] = v
                ins = o.fn(eng)
                if DEBUG_TAGS:
                    try:
                        TAGMAP[ins.ins.name] = o.tag
                    except Exception:
                        pass
                if o.dma_key is not None:
                    ins.then_inc(sems[o.dma_key], self.dma_inc[o.dma_key])
                elif o.needs_inc:
                    ins.then_inc(sems[e], 1)
            for k in final_waits.get(e, ()) if isinstance(final_waits, dict) else ():
                eng.wait_ge(sems[k], 16 * self.dma_count[k])

        @block.tensor
        def _(eng):
            run("pe", eng)

        @block.scalar
        def _(eng):
            run("act", eng)

        @block.vector
        def _(eng):
            run("dve", eng)

        @block.gpsimd
        def _(eng):
            run("pool", eng)

        @block.sync
        def _(eng):
            run("sp", eng)


def fm(vec):
    v = np.asarray(vec, np.float32).reshape(-1, 128)
    return np.ascontiguousarray(v.T)


class ConstPack:
    def __init__(self):
        self.cols = {}
        self.n = 0
        self.parts = []

    def add(self, name, arr):
        arr = np.asarray(arr, np.float32)
        assert arr.shape[0] == 128
        arr = arr.reshape(128, -1)
        self.cols[name] = (self.n, arr.shape[1])
        self.n += arr.shape[1]
        self.parts.append(arr)

    def array(self):
        return np.ascontiguousarray(np.concatenate(self.parts, axis=1))


def pack_consts(inp, b, hf):
    cp = ConstPack()
    cc = np.stack([fm(inp["c"][b]), fm(inp["c_ctx"])], axis=2)
    cp.add("c", cc)
    for i in range(DEPTH):
        cp.add(f"bmod{i}", fm(inp["b_mod"][i]))
        cp.add(f"n1g{i}", fm(inp["norm1_g"][i]))
        cp.add(f"n2g{i}", fm(inp["norm2_g"][i]))
    cp.add("fg", fm(inp["final_g"]))
    for j in range(2):
        cp.add(f"b1{j}", fm(inp["conv_b_pw1"][j]))
        wdw = np.asarray(inp["conv_w_dw"][j], np.float32)
        if hf == 1:
            wdw = wdw[::-1]
        w = wdw.reshape(KW, NCH, 128).transpose(2, 1, 0)
        cp.add(f"wdw{j}", w)
        cp.add(f"bdw{j}", fm(inp["conv_b_dw"][j]))
        cp.add(f"lng{j}", fm(inp["conv_ln_g"][j]))
        cp.add(f"lnb{j}", fm(inp["conv_ln_b"][j]))
        cp.add(f"b2{j}", fm(inp["conv_b_pw2"][j]))
    dirmap = (hf, 1 - hf)
    for j in range(2):
        cw = np.asarray(inp["dn_conv_w"][j], np.float32)
        if hf == 1:
            cw = cw[::-1]
        w = cw.reshape(5, 24, 128).transpose(2, 1, 0)
        cp.add(f"cw{j}", w)
        cp.add(f"ng{j}", np.asarray(inp["dn_norm_g"][j], np.float32).reshape(128, 1))
        al = np.zeros((128, 1), np.float32)
        db = np.zeros((128, 1), np.float32)
        for d in range(2):
            al[32 * d:32 * d + 8, 0] = inp["dn_a_log"][j][dirmap[d]]
            db[32 * d:32 * d + 8, 0] = inp["dn_dt_bias"][j][dirmap[d]]
        cp.add(f"alog{j}", al)
        cp.add(f"dtb{j}", db)
    pm = np.zeros((128, 2), np.float32)
    pm[:, 1 - hf] = 1.0
    cp.add("pm", pm)
    return cp


def make_wab(inp, hf):
    dirmap = (hf, 1 - hf)
    out = np.zeros((2, D, 128), np.float32)
    for j in range(2):
        w = np.asarray(inp["dn_w_in"][j], np.float32)
        for d in range(2):
            out[j, :, 32 * d:32 * d + 8] = w[:, 4096 + dirmap[d] * 8:4096 + dirmap[d] * 8 + 8]
            out[j, :, 64 + 32 * d:64 + 32 * d + 8] = w[:, 4096 + 16 + dirmap[d] * 8:4096 + 16 + dirmap[d] * 8 + 8]
    return out


SLOT_ELEMS = 4096
ARENA_BYTES = 115 * 1024 + 128


class StopBuild(Exception):
    pass


class Builder:
    dbg_stop = None

    def __init__(self, nlayers, cols, ncols, ncores=NCORES):
        self.ncores = ncores
        self.nlayers = nlayers
        self.cols = cols
        self.ncols = ncols
        self.P = Prog()
        self.nc = bass.Bass("TRN2", target_bir_lowering=False)
        self.psum_rr = 0
        self.slot_rr = 0
        self.ar_off = 0

    def phase(self, nslots, sq=True):
        self.P.fence()
        self.ar_off = 0
        self.SLOT = []
        self.SLOTB = []
        for i in range(nslots):
            v, b = self.aalloc(f"SLOT{i}", [SLOT_ELEMS], BF16)
            self.SLOT.append(v)
            self.SLOTB.append(b)
        self.slot_rr = 0
        if sq:
            self.SQ, self.SQB = self.aalloc("SQ", [NCH, 512], BF16)

    def aalloc(self, name, shape, dtype):
        n = 1
        for d_ in shape:
            n *= d_
        esz = 4 if dtype == F32 else 2
        nbytes = (n * esz + 63) // 64 * 64
        off = self.ar_off
        self.ar_off += nbytes
        assert self.ar_off <= ARENA_BYTES, (name, self.ar_off)
        if not hasattr(self, "amap"):
            self.amap = {}
        self.amap[name] = (off, tuple(shape), esz)
        v = self.ARENA[:, off // 2: off // 2 + n * esz // 2]
        if dtype == F32:
            v = v.bitcast(F32)
        if len(shape) > 1:
            names = [f"d{k}" for k in range(len(shape))]
            kw = {nm: sz for nm, sz in zip(names[:-1], shape[:-1])}
            v = v.rearrange(f"p ({' '.join(names)}) -> p {' '.join(names)}", **kw)
        return v, Buf(name)

    def cst(self, name, c0=0, n=None):
        o, w = self.cols[name]
        if n is None:
            n = w - c0
        return self.CONST[:, o + c0:o + c0 + n]

    def psum(self):
        i = self.psum_rr % len(self.PS)
        self.psum_rr += 1
        return self.PS[i], self.PSB[i]

    def tmp(self):
        i = self.tmp_rr % len(self.TMP)
        self.tmp_rr += 1
        return self.TMP[i], self.TMPB[i]

    def load_slot(self, dram_ap):
        i = self.slot_rr % len(self.SLOT)
        self.slot_rr += 1
        st, sb = self.SLOT[i], self.SLOTB[i]
        shp = dram_ap.shape
        view = st[:, 0:shp[1] * shp[2]].rearrange("p (k n) -> p k n", k=shp[1])
        self.P.dma("pool", f"slot{i}", lambda e, o=view, a=dram_ap: e.dma_start(out=o, in_=a),
                   reads=(), writes=(sb,))
        return view, sb

    def mm(self, out, lhsT, rhs, start, stop, reads, writes):
        if FP32R and lhsT.dtype == F32:
            lhsT = lhsT.bitcast(F32R)
            rhs = rhs.bitcast(F32R)
        self.P.op("pe", lambda e: e.matmul(out, lhsT, rhs, start=start, stop=stop),
                  reads=reads, writes=writes)

    def act(self, out, in_, func, reads, writes, bias=None, scale=None):
        kw = {}
        if bias is not None:
            kw["bias"] = bias
        if scale is not None:
            kw["scale"] = scale
        self.P.op("act", lambda e: e.activation(out, in_, func, **kw), reads=reads, writes=writes)

    def dve(self, fn, reads, writes):
        self.P.op("dve", fn, reads=reads, writes=writes)

    def rstd_from(self, src, srcb, n, out, outb):
        self.act(self.LNT[:, 0:n], src, AF.Ln, reads=(srcb, self.EPSB), writes=(self.LNTB,),
                 bias=self.EPST[:, 0:1])
        self.act(out[:, 0:n], self.LNT[:, 0:n], AF.Exp, reads=(self.LNTB,), writes=(outb,), scale=-0.5)

    def norm_mod(self, t0, n, Acol, Bcol, out3, outb, dst_dram=None):
        X, XB = self.X, self.XB
        self.act(self.SQ[:, :, 0:n], X[:, :, t0:t0 + n], AF.Square, reads=(XB,), writes=(self.SQB,))
        ps, psb = self.psum()
        for c in range(NCH):
            self.mm(ps[:, 0:n], self.ONES[:, :], self.SQ[:, c, 0:n], c == 0, c == NCH - 1,
                    reads=(self.SQB, self.ONESB), writes=(psb,))
        self.rstd_from(ps[:, 0:n], psb, n, self.RSTD, self.RSTDB)
        for c in range(NCH):
            tm, tmb = self.tmp()
            self.dve(lambda e, c=c, tm=tm: e.scalar_tensor_tensor(
                tm[:, 0:n], X[:, c, t0:t0 + n], Acol(c), self.RSTD[:, 0:n], ALU.mult, ALU.mult),
                reads=(XB, self.RSTDB, self.MODB, self.CONSTB), writes=(tmb,))
            if dst_dram is not None:
                self.P.dma("sp", "out_" + tmb.name, lambda e, c=c, tm=tm: e.dma_start(out=dst_dram(c), in_=tm[:, 0:n]),
                           reads=(tmb,), writes=())
            else:
                self.act(out3[:, c, 0:n], tm[:, 0:n], AF.Identity, reads=(tmb, self.MODB),
                         writes=(outb,), bias=Bcol(c))

    def compute_mod(self, i, w_mod):
        wv = w_mod[i].rearrange("(k p) n -> p k n", p=128)
        ps, psb = self.psum()
        psv = ps[:, 0:96].rearrange("p (n s) -> p n s", s=2)
        for s in range(12):
            sv, sb = self.load_slot(wv[:, :, 512 * s:512 * (s + 1)])
            for q in range(4):
                n = 4 * s + q
                for k in range(NCH):
                    self.mm(psv[:, n, :], sv[:, k, 128 * q:128 * (q + 1)], self.SC[:, k, :], k == 0, k == NCH - 1,
                            reads=(sb, self.SCB), writes=(psb,))
        o, _ = self.cols[f"bmod{i}"]
        bm = self.CONST[:, o:o + 48]
        M = self.MOD[:, i]
        for s in range(2):
            self.dve(lambda e, s=s: e.tensor_tensor(
                M[:, :, :, s], psv[:, :, s].rearrange("p (m c) -> p m c", m=6),
                bm.rearrange("p (m c) -> p m c", m=6), ALU.add),
                reads=(psb, self.CONSTB), writes=(self.MODB,))
        for m, gname in ((1, f"n1g{i}"), (4, f"n2g{i}")):
            for s in range(2):
                self.dve(lambda e, m=m, s=s, gname=gname: e.scalar_tensor_tensor(
                    M[:, m, :, s], M[:, m, :, s], 1.0, self.cst(gname), ALU.add, ALU.mult),
                    reads=(self.MODB, self.CONSTB), writes=(self.MODB,))

    def modcol(self, i, m, s):
        return lambda c: self.MOD[:, i, m, c, s:s + 1]

    def tiles(self, include_ctx=True):
        r = []
        if include_ctx:
            r.append((0, NCTX, 1))
        for k in range(NLAT // 512):
            r.append((NCTX + 512 * k, 512, 0))
        return r

    def conv_module(self, i, j, last, W):
        self.phase(6)
        HNT, HNTB = self.aalloc("HNT", [NCH, 512], BF16)
        UL, ULB = self.aalloc("UL", [NCH, 8 * (64 + 2 * PADW)], BF16)
        UC, UCB = self.aalloc("UC", [NCH, NCTX + 2 * PADW], BF16)
        DG, DGB = self.aalloc("DIAG", [KW, 128], BF16)
        self.DGB2 = Buf("DIAG2")
        CB, CBB = self.aalloc("CB", [NCH, 512], BF16)
        VT, VTB = self.aalloc("VT", [NCH, 512], BF16)
        self.MEAN, self.MEANB = self.aalloc("MEAN", [512], F32)
        self.VAR, self.VARB = self.aalloc("VAR", [512], F32)
        self.SIG, self.SIGB = self.aalloc("SIG", [512], F32)
        self.P.op("pool", lambda e: e.memset(UL, 0.0), writes=(ULB,))
        self.P.op("pool", lambda e: e.memset(UC, 0.0), writes=(UCB,))
        w1v = W["conv_w_pw1"][j].rearrange("(k p) n -> p k n", p=128)
        w2v = W["conv_w_pw2"][j].rearrange("(k p) n -> p k n", p=128)
        s1 = [self.load_slot(w1v[:, :, 512 * s:512 * (s + 1)]) for s in range(4)]
        s2 = [self.load_slot(w2v[:, :, 512 * s:512 * (s + 1)]) for s in range(2)]
        b1 = lambda c: self.cst(f"b1{j}", c, 1)
        for tile_ in self.tiles(include_ctx=not last):
            self.conv_tile(i, j, tile_, s1, s2, b1, HNT, HNTB, UL, ULB, UC, UCB, DG, DGB, CB, CBB, VT, VTB)

    def conv_tile(self, i, j, tile_, s1, s2, b1, HNT, HNTB, UL, ULB, UC, UCB, DG, DGB, CB, CBB, VT, VTB):
        if True:
            t0, n, s = tile_
            nrow = 1 if s == 1 else n // 64
            rl = n // nrow
            rs = rl + 2 * PADW
            self.norm_mod(t0, n, self.modcol(i, 1, s), self.modcol(i, 0, s), HNT, HNTB)
            U = UC if s == 1 else UL
            UB = UCB if s == 1 else ULB
            Uv = U.rearrange("p c (r w) -> p c r w", w=rs)
            for c in range(NCH):
                pa, pab = self.psum()
                sv, sb = s1[c // 4]
                for k in range(NCH):
                    self.mm(pa[:, 0:n], sv[:, k, 128 * (c % 4):128 * (c % 4 + 1)], HNT[:, k, 0:n],
                            k == 0, k == NCH - 1, reads=(sb, HNTB), writes=(pab,))
                pg, pgb = self.psum()
                sv, sb = s1[2 + c // 4]
                for k in range(NCH):
                    self.mm(pg[:, 0:n], sv[:, k, 128 * (c % 4):128 * (c % 4 + 1)], HNT[:, k, 0:n],
                            k == 0, k == NCH - 1, reads=(sb, HNTB), writes=(pgb,))
                self.act(self.SIG[:, 0:n], pg[:, 0:n], AF.Sigmoid, reads=(pgb, self.CONSTB),
                         writes=(self.SIGB,), bias=b1(NCH + c))
                self.dve(lambda e, c=c, pa=pa: e.scalar_tensor_tensor(
                    Uv[:, c, :, PADW:PADW + rl], pa[:, 0:n].rearrange("p (r w) -> p r w", w=rl), b1(c),
                    self.SIG[:, 0:n].rearrange("p (r w) -> p r w", w=rl), ALU.add, ALU.mult),
                    reads=(pab, self.SIGB, self.CONSTB), writes=(UB,))
            wo, _ = self.cols[f"wdw{j}"]
            for c in range(NCH):
                wk = self.CONST[:, wo + c * KW:wo + (c + 1) * KW]
                KH = 16
                for (k0, k1, dgb_) in ((0, KH, DGB), (KH, KW, self.DGB2)):
                    self.dve(lambda e, wk=wk, k0=k0, k1=k1: e.tensor_tensor(
                        DG[:, k0:k1, :], self.IDENT[:, None, :].to_broadcast([128, k1 - k0, 128]),
                        wk[:, k0:k1, None].to_broadcast([128, k1 - k0, 128]), ALU.mult),
                        reads=(self.IDENTB, self.CONSTB), writes=(dgb_,))
                pc, pcb = self.psum()
                pcv = pc[:, 0:n].rearrange("p (r w) -> p r w", w=rl)
                for k in range(KW):
                    self.mm(pcv, DG[:, k, :], Uv[:, c, :, k:k + rl], k == 0, k == KW - 1,
                            reads=(DGB if k < KH else self.DGB2, UB), writes=(pcb,))
                bd = self.cst(f"bdw{j}", c, 1)
                self.act(CB[:, c, 0:n], pc[:, 0:n], AF.Identity, reads=(pcb, self.CONSTB),
                         writes=(CBB,), bias=bd)
                self.act(self.SQ[:, c, 0:n], pc[:, 0:n], AF.Square, reads=(pcb, self.CONSTB),
                         writes=(self.SQB,), bias=bd)
            pm, pmb = self.psum()
            for c in range(NCH):
                self.mm(pm[:, 0:n], self.ONES[:, :], CB[:, c, 0:n], c == 0, c == NCH - 1,
                        reads=(CBB, self.ONESB), writes=(pmb,))
            pq, pqb = self.psum()
            for c in range(NCH):
                self.mm(pq[:, 0:n], self.ONES[:, :], self.SQ[:, c, 0:n], c == 0, c == NCH - 1,
                        reads=(self.SQB, self.ONESB), writes=(pqb,))
            self.dve(lambda e, pm=pm: e.tensor_copy(self.MEAN[:, 0:n], pm[:, 0:n]),
                     reads=(pmb,), writes=(self.MEANB,))
            self.dve(lambda e: e.tensor_tensor(self.VAR[:, 0:n], self.MEAN[:, 0:n], self.MEAN[:, 0:n], ALU.mult),
                     reads=(self.MEANB,), writes=(self.VARB,))
            self.dve(lambda e, pq=pq: e.tensor_tensor(self.VAR[:, 0:n], pq[:, 0:n], self.VAR[:, 0:n], ALU.subtract),
                     reads=(pqb, self.VARB), writes=(self.VARB,))
            self.rstd_from(self.VAR[:, 0:n], self.VARB, n, self.RSTD, self.RSTDB)
            for c in range(NCH):
                tm, tmb = self.tmp()
                self.dve(lambda e, c=c, tm=tm: e.tensor_tensor(tm[:, 0:n], CB[:, c, 0:n], self.MEAN[:, 0:n], ALU.subtract),
                         reads=(CBB, self.MEANB), writes=(tmb,))
                self.dve(lambda e, tm=tm: e.tensor_tensor(tm[:, 0:n], tm[:, 0:n], self.RSTD[:, 0:n], ALU.mult),
                         reads=(tmb, self.RSTDB), writes=(tmb,))
                self.act(VT[:, c, 0:n], tm[:, 0:n], AF.Silu, reads=(tmb, self.CONSTB), writes=(VTB,),
                         bias=self.cst(f"lnb{j}", c, 1), scale=self.cst(f"lng{j}", c, 1))
            for fc in range(NCH):
                po, pob = self.psum()
                sv, sb = s2[fc // 4]
                for k in range(NCH):
                    self.mm(po[:, 0:n], sv[:, k, 128 * (fc % 4):128 * (fc % 4 + 1)], VT[:, k, 0:n],
                            k == 0, k == NCH - 1, reads=(sb, VTB), writes=(pob,))
                tm, tmb = self.tmp()
                self.dve(lambda e, fc=fc, po=po, tm=tm: e.tensor_scalar(
                    tm[:, 0:n], po[:, 0:n], self.cst(f"b2{j}", fc, 1), self.MOD[:, i, 2, fc, s:s + 1], ALU.add, ALU.mult),
                    reads=(pob, self.CONSTB, self.MODB), writes=(tmb,))
                self.dve(lambda e, fc=fc, tm=tm: e.tensor_tensor(
                    self.X[:, fc, t0:t0 + n], self.X[:, fc, t0:t0 + n], tm[:, 0:n], ALU.add),
                    reads=(tmb, self.XB), writes=(self.XB,))

    def mlp(self, i, last, W):
        if self.dbg_stop == ("M", i):
            raise StopBuild()
        self.phase(6)
        HN, HNB = self.aalloc("HN", [NCH, T], BF16)
        HQ, HQB = self.aalloc("HQ", [NCH, 512], BF16)
        w1v = W["mlp_w1"][i].rearrange("(k p) n -> p k n", p=128)
        w2v = W["mlp_w2"][i].rearrange("(k p) n -> p k n", p=128)
        tiles = self.tiles(include_ctx=not last)
        for (t0, n, s) in tiles:
            self.norm_mod(t0, n, self.modcol(i, 4, s), self.modcol(i, 3, s), HN[:, :, t0:t0 + n], HNB)
        for q in range(4):
            s1 = [self.load_slot(w1v[:, :, 1024 * q + 512 * s:1024 * q + 512 * (s + 1)]) for s in range(2)]
            s2 = [self.load_slot(w2v[:, 8 * q + 4 * s:8 * q + 4 * (s + 1), :]) for s in range(2)]
            for tile_ in tiles:
                self.mlp_tile(i, tile_, s1, s2, HN, HNB, HQ, HQB)

    def mlp_tile(self, i, tile_, s1, s2, HN, HNB, HQ, HQB):
        if True:
            if True:
                t0, n, s = tile_
                for hc in range(8):
                    ph, phb = self.psum()
                    sv, sb = s1[hc // 4]
                    for k in range(NCH):
                        self.mm(ph[:, 0:n], sv[:, k, 128 * (hc % 4):128 * (hc % 4 + 1)], HN[:, k, t0:t0 + n],
                                k == 0, k == NCH - 1, reads=(sb, HNB), writes=(phb,))
                    tm, tmb = self.tmp()
                    self.act(tm[:, 0:n], ph[:, 0:n], AF.Relu, reads=(phb,), writes=(tmb,))
                    self.dve(lambda e, hc=hc, tm=tm: e.tensor_tensor(HQ[:, hc, 0:n], tm[:, 0:n], tm[:, 0:n], ALU.mult),
                             reads=(tmb,), writes=(HQB,))
                for fc in range(NCH):
                    py, pyb = self.psum()
                    for hc in range(8):
                        sv, sb = s2[hc // 4]
                        self.mm(py[:, 0:n], sv[:, hc % 4, 128 * fc:128 * (fc + 1)], HQ[:, hc, 0:n],
                                hc == 0, hc == 7, reads=(sb, HQB), writes=(pyb,))
                    self.dve(lambda e, fc=fc, py=py: e.scalar_tensor_tensor(
                        self.X[:, fc, t0:t0 + n], py[:, 0:n], self.MOD[:, i, 5, fc, s:s + 1],
                        self.X[:, fc, t0:t0 + n], ALU.mult, ALU.add),
                        reads=(pyb, self.MODB, self.XB), writes=(self.XB,))

    def build(self):
        nc = self.nc
        P = self.P
        dt = nc.dram_tensor
        xT = dt("xT", [D, T], F32, kind="ExternalInput").ap()
        consts = dt("consts", [128, self.ncols], F32, kind="ExternalInput").ap()
        W = {}
        for name, shp in (("w_mod", [DEPTH, D, 6 * D]), ("conv_w_pw1", [2, D, 2 * D]), ("conv_w_pw2", [2, D, D]),
                          ("dn_w_in", [2, D, 4128]), ("dn_w_o", [2, D, D]),
                          ("mlp_w1", [DEPTH, D, 4 * D]), ("mlp_w2", [DEPTH, 4 * D, D])):
            if name in needed_weights(self.nlayers):
                W[name] = dt(name, shp, F32, kind="ExternalInput").ap()
        outT = dt("outT", [D, NLAT], F32, kind="ExternalOutput").ap()
        if self.nlayers >= 2:
            W["wab"] = dt("wab", [2, D, 128], F32, kind="ExternalInput").ap()
            self.hx_in = dt("hx_in", [128, 16], BF16)
            self.hx_out = dt("hx_out", [256, 16], BF16)
            self.st_in = dt("st_in", [128, 128], F32)
            self.st_out = dt("st_out", [256, 128], F32)
            self.HXI = Buf("hx_in"); self.HXO = Buf("hx_out"); self.STI = Buf("st_in"); self.STO = Buf("st_out")
        self.W = W

        from contextlib import ExitStack
        with ExitStack() as es:
            def sb(name, shape, dtype):
                return es.enter_context(nc.sbuf_tensor(name, shape, dtype))

            self.X = sb("X", [128, NCH, T], F32); self.XB = Buf("X")
            self.ARENA = sb("ARENA", [128, ARENA_BYTES // 2], BF16)
            self.CONST = sb("CONST", [128, self.ncols], F32); self.CONSTB = Buf("CONST")
            self.MOD = sb("MOD", [128, DEPTH, 6, NCH, 2], F32); self.MODB = Buf("MOD")
            self.SC = sb("SC", [128, NCH, 2], BF16); self.SCB = Buf("SC")
            self.ONES = sb("ONES", [128, 128], BF16); self.ONESB = Buf("ONES")
            self.IDENT = sb("IDENT", [128, 128], BF16); self.IDENTB = Buf("IDENT")
            self.IDF = sb("IDF", [128, 128], F32); self.IDFB = Buf("IDF")
            self.EPST = sb("EPST", [128, 1], F32); self.EPSB = Buf("EPS")
            for nm_ in ("ONES", "IDENT", "IDF"):
                setattr(self, nm_, getattr(self, nm_)[:, :])
            self.LNT = sb("LNT", [128, 512], F32); self.LNTB = Buf("LNT")
            self.RSTD = sb("RSTD", [128, 512], F32); self.RSTDB = Buf("RSTD")
            self.ONE1 = sb("ONE1", [128, 1], F32); self.ONEB = Buf("ONE1")
            self.ONEF = sb("ONEF", [128, 128], F32); self.ONEFB = Buf("ONEF")
            self.ONESS = sb("ONESS", [128, 128], BF16); self.ONESSB = Buf("ONESS")
            self.ONES128 = sb("ONES128", [128, 128], BF16); self.ONES128B = Buf("ONES128")
            self.ML = sb("ML", [128, 128], F32); self.MLS = sb("MLS", [128, 128], F32)
            self.MU = sb("MU", [128, 128], F32); self.MUS = sb("MUS", [128, 128], F32)
            self.MASKB = Buf("MASK")
            for nm_ in ("ONEF", "ONESS", "ONES128", "ML", "MLS", "MU", "MUS", "ONE1"):
                setattr(self, nm_, getattr(self, nm_)[:, :])
            self.TMP = [sb(f"TMP{i}", [128, 512], F32) for i in range(2)]
            self.TMPB = [Buf(f"TMP{i}") for i in range(2)]
            self.tmp_rr = 0
            self.OUTB = Buf("OUT")
            self.PS = [es.enter_context(nc.psum_tensor(f"PS{i}", [128, 512], F32)) for i in range(8)]
            self.PSB = [Buf(f"PS{i}") for i in range(8)]

            P.dma("sp", "const", lambda e: e.dma_start(out=self.CONST[:, :], in_=consts[:, :]),
                  writes=(self.CONSTB,), wait_total=True)
            xv = xT.rearrange("(c p) t -> p c t", p=128)
            for c in range(NCH):
                P.dma("sp", "xin", lambda e, c=c: e.dma_start(out=self.X[:, c, :], in_=xv[:, c, :]),
                      writes=(self.XB,), wait_total=True)
            P.op("pool", lambda e: e.memset(self.ONES[:, :], 1.0 / D), writes=(self.ONESB,))
            P.op("pool", lambda e: e.memset(self.EPST[:, :], EPS), writes=(self.EPSB,))
            P.op("pool", lambda e: e.memset(self.IDF[:, :], 0.0), writes=(self.IDFB,))
            P.op("pool", lambda e: e.affine_select(self.IDF[:, :], self.IDF[:, :], pattern=[[-1, 128]],
                                                    compare_op=ALU.not_equal, fill=1.0, base=0, channel_multiplier=1),
                 reads=(self.IDFB,), writes=(self.IDFB,))
            P.op("pool", lambda e: e.tensor_copy(self.IDENT[:, :], self.IDF[:, :]), reads=(self.IDFB,), writes=(self.IDENTB,))
            P.op("pool", lambda e: e.memset(self.ONE1[:, :], 1.0), writes=(self.ONEB,))
            P.op("pool", lambda e: e.memset(self.ONEF[:, :], 1.0), writes=(self.ONEFB,))
            P.op("pool", lambda e: e.memset(self.ONESS[:, :], 1.0), writes=(self.ONESSB,))
            P.op("pool", lambda e: e.memset(self.ONES128[:, :], 1.0 / 128), writes=(self.ONES128B,))
            for mt, base, cm, pat in ((self.ML, 0, 1, -1), (self.MLS, -1, 1, -1), (self.MU, 0, -1, 1), (self.MUS, -1, -1, 1)):
                P.op("pool", lambda e, mt=mt, base=base, cm=cm, pat=pat: e.affine_select(
                    mt[:, :], self.ONEF[:, :], pattern=[[pat, 128]], compare_op=ALU.is_ge, fill=0.0,
                    base=base, channel_multiplier=cm), reads=(self.ONEFB,), writes=(self.MASKB,))
            self.BD = sb("BD", [128, 128], F32)[:, :]
            P.op("pool", lambda e: e.memset(self.BD, 0.0), writes=(self.MASKB,))
            P.op("pool", lambda e: e.memset(self.BD[0:64, 0:64], 1.0), writes=(self.MASKB,))
            P.op("pool", lambda e: e.memset(self.BD[64:128, 64:128], 1.0), writes=(self.MASKB,))
            for mt in (self.ML, self.MLS, self.MU, self.MUS):
                P.op("pool", lambda e, mt=mt: e.tensor_tensor(mt, mt, self.BD, ALU.mult),
                     reads=(self.MASKB,), writes=(self.MASKB,))
            co, _ = self.cols["c"]
            self.act(self.SC[:, :, :].rearrange("p k s -> p (k s)"), self.CONST[:, co:co + 16], AF.Silu,
                     reads=(self.CONSTB,), writes=(self.SCB,))
            self.phase(6)
            for i in range(self.nlayers):
                self.compute_mod(i, W["w_mod"])

            try:
                for i in range(self.nlayers):
                    last = i == self.nlayers - 1
                    j = i // 2
                    if i % 2 == 0:
                        self.conv_module(i, j, last, W)
                    else:
                        self.delta_module(i, j, last, W)
                    self.mlp(i, last, W)
            except StopBuild:
                pass

            ov = outT.rearrange("(c p) t -> p c t", p=128)
            for (t0, n, s) in self.tiles(include_ctx=False):
                fgc = lambda c: self.cst("fg", c, 1)
                self.norm_mod(t0, n, fgc, None, None, None,
                              dst_dram=lambda c, t0=t0, n=n: ov[:, c, t0 - NCTX:t0 - NCTX + n])

            keys = list(self.P.dma_count.keys())
            sems = {}
            for k in list(Prog.ENGS) + keys:
                sems[k] = es.enter_context(nc.semaphore(f"s_{k}"))
            block = es.enter_context(nc.Block())
            self.P.emit(nc, block, sems, final_waits={"sp": tuple(k for k in keys if k.startswith("out_"))})
        return nc

    def exchange(self, key, src_ap, src_b, din, din_b, dout, dout_b, dst_ap, dst_b, din_view=None):
        P = self.P
        dv = din[:, :] if din_view is None else din_view
        P.dma("sp", key + "_a", lambda e: e.dma_start(out=dv, in_=src_ap), reads=(src_b,), writes=(din_b,))
        groups = [[2 * k, 2 * k + 1] for k in range(self.ncores // 2)]
        P.dma("pool", key + "_c", lambda e: e.collective_compute(
            "AllGather", ALU.bypass, replica_groups=groups, ins=[din.ap().opt()], outs=[dout.ap().opt()]),
            reads=(din_b,), writes=(dout_b,), inc=1)
        P.dma("sp", key + "_b", lambda e: e.dma_start(
            out=dst_ap, in_=dout.ap().rearrange("(r p) n -> p r n", p=128)), reads=(dout_b,),
            writes=dst_b if isinstance(dst_b, tuple) else (dst_b,))

    def delta_module(self, i, j, last, W):
        P = self.P
        self.phase(0, sq=False)
        A = self.aalloc
        NCK = T // 128
        PT_ = 2312
        pcol = lambda t: t + 2 if t < NCTX else t + 6
        WQKV, WQKVB = A("WQKV", [8, 384], BF16)
        WZ, WZB = A("WZ", [8, 128], BF16)
        WO, WOB = A("WO", [1024], BF16)
        WAB, WABB = WZ, WZB
        HNP, HNPB = A("HNP", [8, PT_], BF16)
        G, GB = A("G", [T], F32)
        O, OB = A("O", [T], F32)
        COLS, COLSB = A("COLS", [NCK, 32], F32)
        HLB, HLBB = A("HLB", [NCK, 16], F32)
        KDF, KDFB = A("KDF", [NCK, 16], F32)
        CDA, CDAB = A("CDA", [NCK, 16], F32)
        CDB_, CDBB = A("CDB", [NCK, 16], F32)
        NBE, NBEB = A("NBE", [NCK, 16], F32)
        CD = (CDA, CDB_)
        CDB = (CDAB, CDBB)
        NB, NBB = HLB, HLBB
        TOT, TOTB = A("TOT", [2 * NCK], F32)
        EAL, EALB = A("EAL", [1], F32)
        SELC, SELCB = A("SELC", [32], F32)
        QT, QTB = A("QT", [T], BF16)
        KF, KFB = A("KF", [T], F32)
        off_ = self.amap["KF"][0]
        self.SQ = self.ARENA[:, off_ // 2: off_ // 2 + NCH * 512].rearrange("p (c t) -> p c t", c=NCH)
        self.SQB = KFB
        SQ1, SQ1B = A("SQ1", [512], BF16)
        SQH = [(SQ1[:, 0:256], Buf("SQ1a")), (SQ1[:, 256:512], Buf("SQ1b"))]
        VTM, VTMB = A("VTM", [NCK, 128], BF16)
        PRET2 = [A("PRET%d" % k_, [260], BF16) for k_ in range(2)]
        VTT2 = [A("VTT%d" % k_, [256], BF16) for k_ in range(2)]
        DG5, DG5B = A("DG5", [5, 128], BF16)
        SELR, SELRB = A("SELR", [4, 128], F32)
        HXG, HXGB = A("HXG", [2, 16], BF16)
        HXS, HXSB = A("HXS", [16], F32)
        S, SB_ = A("S", [128], F32)
        SBF, SBFB = A("SBF", [128], BF16)
        OG, OGB = A("OG", [512], BF16)
        f32t = {}
        for nm in ("E1", "E2", "EROW", "M2I", "P", "PT", "Z"):
            nb_ = 2 if nm in ("E1", "E2", "EROW", "M2I") else 4
            f32t[nm] = [A(nm + str(k), [128], F32) for k in range(nb_)]
        def carve(base, boff, dtype):
            o_ = self.amap[base][0] + boff
            nb = 512 if dtype == F32 else 256
            v_ = self.ARENA[:, o_ // 2: o_ // 2 + nb // 2]
            return (v_.bitcast(F32) if dtype == F32 else v_), Buf(f"{base}+{boff}")
        aliasQ, aliasO = [], []
        bo = 0
        for nm in ("P", "PT", "Z"):
            for k_ in range(2):
                t_ = carve("WQKV", bo, F32); bo += 512
                f32t[nm].append(t_); aliasQ.append(t_[1])
        for nm in ("E1", "E2", "EROW", "M2I"):
            t_ = carve("WQKV", bo, F32); bo += 512
            f32t[nm].append(t_); aliasQ.append(t_[1])
        extra = []
        for k_ in range(4):
            t_ = carve("WQKV", bo, BF16); bo += 256
            extra.append(t_); aliasQ.append(t_[1])
        assert bo <= 6144
        for k_ in range(8):
            t_ = carve("WO", 256 * k_, BF16)
            extra.append(t_); aliasO.append(t_[1])
        off_ = self.amap["E10"][0]
        SG = self.ARENA[:, off_ // 2: off_ // 2 + 512].bitcast(F32).rearrange("p (r n) -> p r n", r=2)
        SGB = (f32t["E1"][0][1], f32t["E1"][1][1])
        b16t = {}
        for nm in ("KB", "QKTM", "BV", "KD", "QG", "VN", "YB", "ZB"):
            nb_ = 1 if nm in ("VN", "YB") else 4
            bufs_ = [A(nm + str(k), [128], BF16) for k in range(nb_)]
            b16t[nm] = bufs_ * (4 // nb_)
            if nb_ == 4:
                b16t[nm] = b16t[nm] + [extra.pop(0), extra.pop(0)]
        rr = {}

        def nxt(d, nm):
            k = rr.get(nm, 0)
            rr[nm] = k + 1
            return d[nm][k % 2]

        def dve(fn, reads, writes):
            P.op("dve", fn, reads=reads, writes=writes)

        P.op("pool", lambda e: e.memset(HNP[:, :, 0:2], 0.0), writes=(HNPB,))
        P.op("pool", lambda e: e.memset(HNP[:, :, 258:262], 0.0), writes=(HNPB,))
        for (t0, n, s) in self.tiles():
            self.norm_mod(t0, n, self.modcol(i, 1, s), self.modcol(i, 0, s), HNP[:, :, pcol(t0):pcol(t0) + n], HNPB)
        self.exchange(f"hx", HNP[:, :, 2308:2310], HNPB, self.hx_in, self.HXI, self.hx_out, self.HXO,
                      HXG, HXGB, din_view=self.hx_in.ap().rearrange("p (k t) -> p k t", t=2))
        pm = lambda r: self.cst("pm", r, 1)
        dve(lambda e: e.tensor_scalar(HXS, HXG[:, 0, :], pm(0), None, ALU.mult), (HXGB, self.CONSTB), (HXSB,))
        dve(lambda e: e.scalar_tensor_tensor(HXS, HXG[:, 1, :], pm(1), HXS, ALU.mult, ALU.add),
            (HXGB, HXSB, self.CONSTB), (HXSB,))
        hv = HXS.rearrange("p (k t) -> p k t", t=2)
        dve(lambda e: e.tensor_copy(HNP[:, :, 2310:2311], hv[:, :, 1:2]), (HXSB,), (HNPB,))
        dve(lambda e: e.tensor_copy(HNP[:, :, 2311:2312], hv[:, :, 0:1]), (HXSB,), (HNPB,))

        P.dma("pool", "wab", lambda e: e.dma_start(out=WAB, in_=W["wab"][j].rearrange("(k p) n -> p k n", p=128)),
              writes=(WABB,))
        for q in range(4):
            dve(lambda e, q=q: e.tensor_copy(SELC[:, 8 * q:8 * q + 8], self.IDF[:, 32 * q:32 * q + 8]),
                (self.IDFB,), (SELCB,))
        self.act(EAL, self.cst(f"alog{j}"), AF.Exp, reads=(self.CONSTB,), writes=(EALB,))
        for (t0, n, s) in self.tiles():
            ps, psb = self.psum()
            for k in range(NCH):
                self.mm(ps[:, 0:n], WAB[:, k, :], HNP[:, k, pcol(t0):pcol(t0) + n], k == 0, k == NCH - 1,
                        reads=(WABB, HNPB), writes=(psb,))
            self.act(O[0:64, t0:t0 + n], ps[0:64, 0:n], AF.Exp, reads=(psb, self.CONSTB), writes=(OB,),
                     bias=self.cst(f"dtb{j}")[0:64, :])
            self.act(O[0:64, t0:t0 + n], O[0:64, t0:t0 + n], AF.Ln, reads=(OB, self.ONEB), writes=(OB,),
                     bias=self.ONE1[0:64, :])
            dve(lambda e, t0=t0, n=n: e.tensor_scalar(O[0:64, t0:t0 + n], O[0:64, t0:t0 + n], EAL[0:64, :], None, ALU.mult),
                (OB, EALB), (OB,))
            self.act(G[64:128, t0:t0 + n], ps[64:128, 0:n], AF.Sigmoid, reads=(psb,), writes=(GB,))
        for c in range(2 * NCK):
            dve(lambda e, c=c: e.tensor_tensor_scan(G[0:64, 64 * c:64 * (c + 1)], self.ONEF[0:64, 0:64],
                                                    O[0:64, 64 * c:64 * (c + 1)], 0.0, ALU.mult, ALU.add),
                (OB, self.ONEFB), (GB,))
        Gv = G[32:64, :].rearrange("p (c w) -> p c w", w=64)
        dve(lambda e: e.tensor_copy(TOT[32:64, :], Gv[:, :, 63]), (GB,), (TOTB,))
        dve(lambda e: e.tensor_tensor(Gv, TOT[32:64, :, None].to_broadcast([32, 2 * NCK, 64]), Gv, ALU.subtract),
            (GB, TOTB), (GB,))
        dve(lambda e: e.tensor_tensor(G[32:64, :], G[32:64, :], O[32:64, :], ALU.add), (GB, OB), (GB,))
        for c in range(NCK):
            ps, psb = self.psum()
            self.mm(ps[:, 0:32], G[:, 128 * c:128 * (c + 1)], SELC, True, True, reads=(GB, SELCB), writes=(psb,))
            self.act(COLS[:, c, :], ps[:, 0:32], AF.Identity, reads=(psb,), writes=(COLSB,))
        G3 = lambda lo, hi: G[lo:hi, :].rearrange("p (c w) -> p c w", w=128)
        for sub, (RH, RHB, HL, HLB_) in enumerate(((KDF, KDFB, CDA, CDAB), (NBE, NBEB, CDB_, CDBB))):
            P.op("pool", lambda e, RH=RH: e.memset(RH, 0.0), writes=(RHB,))
            tf_ = 63 + 64 * sub
            tb_ = 64 * sub
            dve(lambda e, RH=RH, tf_=tf_: e.tensor_tensor(
                RH[0:32], SELC[0:32, None, 0:16].to_broadcast([32, NCK, 16]),
                G3(0, 32)[:, :, tf_:tf_ + 1].to_broadcast([32, NCK, 16]), ALU.mult), (SELCB, GB, RHB), (RHB,))
            dve(lambda e, RH=RH, tb_=tb_: e.tensor_tensor(
                RH[32:64], SELC[32:64, None, 0:16].to_broadcast([32, NCK, 16]),
                G3(32, 64)[:, :, tb_:tb_ + 1].to_broadcast([32, NCK, 16]), ALU.mult), (SELCB, GB, RHB), (RHB,))
            ps, psb = self.psum()
            self.mm(ps[:, 0:NCK * 16], self.ONEF, RH.rearrange("p c n -> p (c n)"), True, True,
                    reads=(RHB, self.ONEFB), writes=(psb,))
            self.act(HL.rearrange("p c n -> p (c n)"), ps[:, 0:NCK * 16], AF.Identity, reads=(psb,), writes=(HLB_,))
            hs = slice(64 * sub, 64 * sub + 64)
            dve(lambda e, HL=HL, hs=hs: e.tensor_copy(HLB[hs], HL[hs]), (HLB_,), (HLBB,))
        self.act(CDA, CDA, AF.Exp, reads=(CDAB, HLBB), writes=(CDAB,), scale=-1.0)
        self.act(CDB_, CDB_, AF.Exp, reads=(CDBB, HLBB), writes=(CDBB,), scale=-1.0)
        dve(lambda e: e.tensor_tensor(KDF, COLS[:, :, 0:16], HLB, ALU.subtract), (COLSB, HLBB), (KDFB,))
        self.act(KDF, KDF, AF.Exp, reads=(KDFB,), writes=(KDFB,))
        self.act(NBE, COLS[:, :, 0:16], AF.Exp, reads=(COLSB,), writes=(NBEB,), scale=-1.0)
        dve(lambda e: e.scalar_tensor_tensor(NBE, COLS[:, :, 16:32], -1.0, NBE, ALU.mult, ALU.mult),
            (COLSB, NBEB), (NBEB,))
        dve(lambda e: e.tensor_scalar(NB, COLS[:, :, 16:32], -1.0, None, ALU.mult), (COLSB,), (NBB,))

        w_in = W["dn_w_in"][j]
        wq = w_in[:, 0:4096].rearrange("(k p) (g hh d) -> p k g hh d", p=128, g=4, hh=8)
        cwo, _ = self.cols[f"cw{j}"]
        QSC = float(128 ** -0.5)

        for h in range(8):
            self.delta_head(i, j, h, last, locals())

    def delta_head(self, i, j, h, last, L):
        P = self.P
        g_ = lambda k: L[k]
        (WQKV, WQKVB, WZ, WZB, WO, WOB, HNP, HNPB, G, GB, O, OB, COLS, COLSB, KDF, KDFB, CD, CDB, NBE, NBEB, NB, NBB,
         QT, QTB, KF, KFB, VTM, VTMB, PRET2, _p2, VTT2, _v2, DG5, DG5B, SELR, SELRB, SG, SGB, S, SB_,
         SBF, SBFB, OG, OGB) = [g_(k) for k in (
            "WQKV", "WQKVB", "WZ", "WZB", "WO", "WOB", "HNP", "HNPB", "G", "GB", "O", "OB", "COLS", "COLSB", "KDF", "KDFB", "CD", "CDB",
            "NBE", "NBEB", "NB", "NBB", "QT", "QTB", "KF", "KFB", "VTM", "VTMB", "PRET2", "PRET2", "VTT2", "VTT2",
            "DG5", "DG5B", "SELR", "SELRB", "SG", "SGB", "S", "SB_", "SBF", "SBFB", "OG", "OGB")]
        f32t, b16t, nxt, dve, pcol, wq, cwo, QSC, NCK, W, SQ1, SQH, aliasQ, aliasO = (g_(k) for k in (
            "f32t", "b16t", "nxt", "dve", "pcol", "wq", "cwo", "QSC", "NCK", "W", "SQ1", "SQH", "aliasQ", "aliasO"))
        for g in range(3):
            P.dma("pool", f"win{g}", lambda e, g=g: e.dma_start(out=WQKV[:, :, 128 * g:128 * (g + 1)], in_=wq[:, :, g, h, :]),
                  writes=(WQKVB,) + tuple(aliasQ))
        P.dma("pool", "win3", lambda e: e.dma_start(out=WZ, in_=wq[:, :, 3, h, :]), writes=(WZB,))
        for q in range(4):
            r = 32 * q + h
            dve(lambda e, q=q, r=r: e.tensor_copy(SELR[:, q, :], self.IDF[:, r:r + 1].to_broadcast([128, 128])),
                (self.IDFB,), (SELRB,))
        segs = [(0, 0)] + [(260 + 256 * k, NCTX + 256 * k) for k in range(NLAT // 256)]
        for g in (0, 1, 2):
            wk = self.CONST[:, cwo + (8 * g + h) * 5:cwo + (8 * g + h + 1) * 5]
            dve(lambda e, wk=wk: e.tensor_tensor(
                DG5, self.IDENT[:, None, :].to_broadcast([128, 5, 128]),
                wk[:, :, None].to_broadcast([128, 5, 128]), ALU.mult), (self.IDENTB, self.CONSTB), (DG5B,))
            st = {}

            def stage_a(k, g=g, st=st):
                pc0, tk0 = segs[k]
                PRET, PRETB = PRET2[k % 2]
                ps, psb = self.psum()
                for kk_ in range(NCH):
                    self.mm(ps[:, 0:260], WQKV[:, kk_, 128 * g:128 * (g + 1)], HNP[:, kk_, pc0:pc0 + 260], kk_ == 0, kk_ == NCH - 1,
                            reads=(WQKVB, HNPB), writes=(psb,))
                self.act(PRET, ps[:, 0:260], AF.Identity, reads=(psb,), writes=(PRETB,))

            def stage_b(k, g=g, st=st):
                pc0, tk0 = segs[k]
                PRET, PRETB = PRET2[k % 2]
                p2, p2b = self.psum()
                for tap in range(5):
                    self.mm(p2[:, 0:256], DG5[:, tap, :], PRET[:, tap:tap + 256], tap == 0, tap == 4,
                            reads=(DG5B, PRETB), writes=(p2b,))
                if g < 2:
                    tm, tmb = self.tmp()
                    sq, sqb = SQH[k % 2]
                    self.act(tm[:, 0:256], p2[:, 0:256], AF.Silu, reads=(p2b,), writes=(tmb,))
                    self.act(sq, tm[:, 0:256], AF.Square, reads=(tmb,), writes=(sqb,))
                    st[k] = (tm, tmb)
                else:
                    VTT, VTTB = VTT2[k % 2]
                    self.act(VTT, p2[:, 0:256], AF.Silu, reads=(p2b,), writes=(VTTB,))

            def stage_c(k, g=g, st=st):
                pc0, tk0 = segs[k]
                if g < 2:
                    tm, tmb = st[k]
                    sq, sqb = SQH[k % 2]
                    p3, p3b = self.psum()
                    self.mm(p3[:, 0:256], self.ONESS, sq, True, True, reads=(sqb, self.ONESSB), writes=(p3b,))
                    self.rstd_from(p3[:, 0:256], p3b, 256, self.RSTD, self.RSTDB)
                    dst_, dstb_ = (QT, QTB) if g == 0 else (KF, KFB)
                    dve(lambda e: e.scalar_tensor_tensor(
                        dst_[:, tk0:tk0 + 256], tm[:, 0:256], QSC if g == 0 else 1.0, self.RSTD[:, 0:256], ALU.mult, ALU.mult),
                        (tmb, self.RSTDB), (dstb_,))
                else:
                    VTT, VTTB = VTT2[k % 2]
                    for cc in range(2):
                        c = tk0 // 128 + cc
                        p4, p4b = self.psum()
                        self.mm(p4[:, 0:128], VTT[:, 128 * cc:128 * (cc + 1)], self.IDENT, True, True,
                                reads=(VTTB, self.IDENTB), writes=(p4b,))
                        self.act(VTM[:, c, :], p4[:, 0:128], AF.Identity, reads=(p4b,), writes=(VTMB,))

            ns_ = len(segs)
            for t_ in range(ns_ + 2):
                if t_ < ns_:
                    stage_a(t_)
                if 0 <= t_ - 1 < ns_:
                    stage_b(t_ - 1)
                if 0 <= t_ - 2 < ns_:
                    stage_c(t_ - 2)

        def pre(d, c, si, ci):
            n = 8 * d + h
            ck = slice(128 * c, 128 * (c + 1))
            col = lambda Tn, off=0: Tn[:, c, off + n:off + n + 1]
            (E1, E1B), (E2, E2B), (EROW, EROWB), (M2I, M2IB) = (f32t["E1"][ci], f32t["E2"][ci],
                                                               f32t["EROW"][ci], f32t["M2I"][ci])
            cnt_ = {"P": 0, "PT": 0, "Z": 0}

            def nxc(nm):
                k_ = cnt_[nm]
                cnt_[nm] = k_ + 1
                return f32t[nm][2 * ci + k_ % 2]
            hr, hrb = self.psum()
            self.mm(hr[:, 0:128], SELR[:, d, :], G[:, ck], True, True, reads=(SELRB, GB), writes=(hrb,))
            yield
            br, brb = self.psum()
            self.mm(br[:, 0:128], SELR[:, 2 + d, :], G[:, ck], True, True, reads=(SELRB, GB), writes=(brb,))
            yield
            if d == 0:
                mDs, mTs, mTi = self.MLS, self.MUS, self.MU
            else:
                mDs, mTs, mTi = self.MUS, self.MLS, self.ML
            dve(lambda e: e.tensor_scalar(E1, hr[:, 0:128], col(COLS), 0.0, ALU.subtract, ALU.min), (hrb, COLSB), (E1B,))
            self.act(E1, E1, AF.Exp, reads=(E1B,), writes=(E1B,))
            dve(lambda e: e.tensor_tensor(E1, E1, mDs, ALU.mult), (E1B, self.MASKB), (E1B,))
            dve(lambda e: e.tensor_scalar(E2, hr[:, 0:128], col(COLS), 0.0, ALU.subtract, ALU.max), (hrb, COLSB), (E2B,))
            self.act(E2, E2, AF.Exp, reads=(E2B,), writes=(E2B,), scale=-1.0)
            dve(lambda e: e.tensor_tensor(M2I, E2, mTi, ALU.mult), (E2B, self.MASKB), (M2IB,))
            dve(lambda e: e.tensor_tensor(E2, E2, mTs, ALU.mult), (E2B, self.MASKB), (E2B,))
            self.act(EROW, hr[:, 0:128], AF.Exp, reads=(hrb,), writes=(EROWB,), scale=-1.0)
            KB, KBB = b16t["KB"][si]
            self.act(KB, KF[:, ck], AF.Identity, reads=(KFB,), writes=(KBB,))
            kk, kkb = self.psum()
            self.mm(kk[:, 0:128], KF[:, ck], KF[:, ck], True, True, reads=(KFB,), writes=(kkb,))
            yield
            Pm, PmB = nxc("P")
            PTm, PTmB = nxc("PT")
            Zm, ZmB = nxc("Z")
            dve(lambda e, Pm=Pm: e.scalar_tensor_tensor(Pm, kk[:, 0:128], col(NB), E1, ALU.mult, ALU.mult), (kkb, NBB, E1B), (PmB,))
            dve(lambda e: e.scalar_tensor_tensor(E2, kk[:, 0:128], -1.0, E2, ALU.mult, ALU.mult), (kkb, E2B), (E2B,))
            dve(lambda e, PTm=PTm: e.tensor_tensor(PTm, E2, br[:, 0:128], ALU.mult), (E2B, brb), (PTmB,))
            qk_, qkb_ = self.psum()
            self.mm(qk_[:, 0:128], KB, QT[:, ck], True, True, reads=(KBB, QTB), writes=(qkb_,))
            yield
            QKTM, QKTMB = b16t["QKTM"][si]
            dve(lambda e: e.tensor_tensor(QKTM, qk_[:, 0:128], M2I, ALU.mult), (qkb_, M2IB), (QKTMB,))
            dve(lambda e, Zm=Zm, PTm=PTm: e.tensor_tensor(Zm, self.IDF, PTm, ALU.add), (self.IDFB, PTmB), (ZmB,))
            for m in range(1, 6):
                Pn, PnB = nxc("P")
                pp, ppb = self.psum()
                self.mm(pp[:, 0:128], PTm, Pm, True, True, reads=(PTmB, PmB), writes=(ppb,))
                yield
                self.act(Pn, pp[:, 0:128], AF.Identity, reads=(ppb,), writes=(PnB,))
                if m < 5:
                    PTn, PTnB = nxc("PT")
                    pt, ptb = self.psum()
                    self.P.op("pe", lambda e, pt=pt, Pn=Pn: e.transpose(pt[:, 0:128], Pn, self.IDF),
                              reads=(PnB, self.IDFB), writes=(ptb,))
                    yield
                    self.act(PTn, pt[:, 0:128], AF.Identity, reads=(ptb,), writes=(PTnB,))
                Zn, ZnB = nxc("Z")
                pz, pzb = self.psum()
                self.mm(pz[:, 0:128], Pn, Zm, True, True, reads=(PnB, ZmB), writes=(pzb,))
                yield
                dve(lambda e, Zn=Zn, Zm=Zm, pz=pz: e.tensor_tensor(Zn, pz[:, 0:128], Zm, ALU.add), (pzb, ZmB), (ZnB,))
                Pm, PmB = Pn, PnB
                if m < 5:
                    PTm, PTmB = PTn, PTnB
                Zm, ZmB = Zn, ZnB
            BV, BVB = b16t["BV"][si]
            KD, KDB = b16t["KD"][si]
            QG, QGB = b16t["QG"][si]
            dve(lambda e: e.tensor_scalar(BV, VTM[:, c, :], col(COLS, 16), None, ALU.mult), (VTMB, COLSB), (BVB,))
            p4, p4b = self.psum()
            self.mm(p4[:, 0:128], KB, self.IDENT, True, True, reads=(KBB, self.IDENTB), writes=(p4b,))
            yield
            dve(lambda e: e.tensor_scalar(KD, p4[:, 0:128], col(KDF), None, ALU.mult), (p4b, KDFB), (KDB,))
            dve(lambda e: e.tensor_tensor(QG, QT[:, ck], EROW, ALU.mult), (QTB, EROWB), (QGB,))
            ZB, ZBB = b16t["ZB"][si]
            self.act(ZB, Zm, AF.Identity, reads=(ZmB,), writes=(ZBB,))
            yield

        def seq(d, c, si, first_touch):
            n = 8 * d + h
            ck = slice(128 * c, 128 * (c + 1))
            col = lambda Tn, off=0: Tn[:, c, off + n:off + n + 1]
            (KB, KBB), (QKTM, QKTMB), (BV, BVB), (KD, KDB), (QG, QGB), (ZB, ZBB) = (
                b16t["KB"][si], b16t["QKTM"][si], b16t["BV"][si], b16t["KD"][si], b16t["QG"][si], b16t["ZB"][si])
            Y, YB = nxt(b16t, "YB")
            VN, VNB = nxt(b16t, "VN")
            for a_ in ((0, 1) if d == 0 else (1, 0)):
                hs = slice(64 * a_, 64 * a_ + 64)
                tk = slice(128 * c + 64 * a_, 128 * c + 64 * a_ + 64)
                ks, ksb = self.psum()
                self.mm(ks[:, 0:128], KB, SBF, True, True, reads=(KBB, SBFB), writes=(ksb,))
                yield
                dve(lambda e, ks=ks: e.scalar_tensor_tensor(Y, ks[:, 0:128], col(NBE), BV, ALU.mult, ALU.add),
                    (ksb, NBEB, BVB), (YB,))
                vn, vnb = self.psum()
                self.mm(vn[:, 0:128], ZB, Y, True, True, reads=(ZBB, YB), writes=(vnb,))
                yield
                self.act(VN, vn[:, 0:128], AF.Identity, reads=(vnb,), writes=(VNB,))
                ot, otb = self.psum()
                self.mm(ot[:, 0:64], SBF, QG[:, hs], True, False, reads=(SBFB, QGB), writes=(otb,))
                yield
                self.mm(ot[:, 0:64], VN, QKTM[:, hs], False, True, reads=(VNB, QKTMB), writes=(otb,))
                yield
                if first_touch:
                    dve(lambda e, ot=ot, tk=tk: e.tensor_copy(O[:, tk], ot[:, 0:64]), (otb,), (OB,))
                else:
                    dve(lambda e, ot=ot, tk=tk: e.tensor_tensor(O[:, tk], O[:, tk], ot[:, 0:64], ALU.add), (otb, OB), (OB,))
                sn, snb = self.psum()
                self.mm(sn[:, 0:128], KD[hs, :], VN[hs, :], True, True, reads=(KDB, VNB), writes=(snb,))
                yield
                dve(lambda e, sn=sn, a_=a_: e.scalar_tensor_tensor(S, S, col(CD[a_]), sn[:, 0:128], ALU.mult, ALU.add),
                    (SB_, CDB[a_], snb), (SB_,))
                self.act(SBF, S, AF.Identity, reads=(SB_,), writes=(SBFB,))


        def interleave(gens):
            gens = [g for g in gens if g is not None]
            while gens:
                for g in list(gens):
                    try:
                        next(g)
                    except StopIteration:
                        gens.remove(g)

        def seq_many(d, items, first_touch):
            for (c_, si_) in items:
                yield from seq(d, c_, si_, first_touch)

        def run_scan(d, cs, first_touch):
            prev = []
            for t_ in range(0, len(cs), 3):
                cur = [(cs[k], k % 6) for k in range(t_, min(t_ + 3, len(cs)))]
                gens = [pre(d, c_, si_, idx) for idx, (c_, si_) in enumerate(cur)]
                if prev:
                    gens.append(seq_many(d, prev, first_touch))
                interleave(gens)
                prev = cur
            interleave([seq_many(d, prev, first_touch)])

        def zero_state():
            P.op("pool", lambda e: e.memset(S, 0.0), writes=(SB_,))
            P.op("pool", lambda e: e.memset(SBF, 0.0), writes=(SBFB,))

        for b_ in aliasQ:
            b_.last_w = WQKVB.last_w
            b_.readers = WQKVB.readers[-1:]
        for b_ in aliasO:
            b_.last_w = WOB.last_w
            b_.readers = WOB.readers[-1:]
        zero_state()
        run_scan(0, list(range(NCK)), True)
        if self.dbg_stop == ("A", h):
            raise StopBuild()
        self.exchange("st", S, SB_, self.st_in, self.STI, self.st_out, self.STO, SG, SGB)
        pm = lambda r: self.cst("pm", r, 1)
        dve(lambda e: e.tensor_scalar(S, SG[:, 0, :], pm(0), None, ALU.mult), SGB + (self.CONSTB,), (SB_,))
        dve(lambda e: e.scalar_tensor_tensor(S, SG[:, 1, :], pm(1), S, ALU.mult, ALU.add), SGB + (SB_, self.CONSTB), (SB_,))
        self.act(SBF, S, AF.Identity, reads=(SB_,), writes=(SBFB,))
        run_scan(1, list(range(NCK - 1, 1, -1)), False)
        if not last:
            zero_state()
            run_scan(1, [1, 0], False)
        P.dma("pool", "wo", lambda e: e.dma_start(out=WO, in_=W["dn_w_o"][j][128 * h:128 * (h + 1), :]),
              writes=(WOB,) + tuple(aliasO))
        if self.dbg_stop == ("B", h):
            raise StopBuild()
        for (t0, n, s) in self.tiles(include_ctx=not last):
            self.head_out(i, j, t0, n, s, WZ, WZB, WO, WOB, HNP, HNPB, O, OB, OG, OGB, pcol, SQ1, SQH)
        if self.dbg_stop == ("H", h):
            raise StopBuild()

    def head_out(self, i, j, t0, n, s, WIN, WINB, WO, WOB, HNP, HNPB, O, OB, OG, OGB, pcol, SQ1, SQH):
        sqbs = (SQH[0][1], SQH[1][1])
        self.act(SQ1[:, 0:n], O[:, t0:t0 + n], AF.Square, reads=(OB,), writes=sqbs)
        ps, psb = self.psum()
        self.mm(ps[:, 0:n], self.ONES128, SQ1[:, 0:n], True, True, reads=sqbs + (self.ONES128B,), writes=(psb,))
        self.rstd_from(ps[:, 0:n], psb, n, self.RSTD, self.RSTDB)
        zp, zpb = self.psum()
        for k in range(NCH):
            self.mm(zp[:, 0:n], WIN[:, k, :], HNP[:, k, pcol(t0):pcol(t0) + n], k == 0, k == NCH - 1,
                    reads=(WINB, HNPB), writes=(zpb,))
        zs, zsb = self.tmp()
        self.act(zs[:, 0:n], zp[:, 0:n], AF.Silu, reads=(zpb,), writes=(zsb,))
        t1, t1b = self.tmp()
        self.dve(lambda e: e.scalar_tensor_tensor(t1[:, 0:n], O[:, t0:t0 + n], self.cst(f"ng{j}"), self.RSTD[:, 0:n],
                                                  ALU.mult, ALU.mult), (OB, self.RSTDB, self.CONSTB), (t1b,))
        self.dve(lambda e: e.tensor_tensor(OG[:, 0:n], t1[:, 0:n], zs[:, 0:n], ALU.mult), (t1b, zsb), (OGB,))
        for fc in range(NCH):
            py, pyb = self.psum()
            self.mm(py[:, 0:n], WO[:, 128 * fc:128 * (fc + 1)], OG[:, 0:n], True, True, reads=(WOB, OGB), writes=(pyb,))
            self.dve(lambda e, fc=fc, py=py: e.scalar_tensor_tensor(
                self.X[:, fc, t0:t0 + n], py[:, 0:n], self.MOD[:, i, 2, fc, s:s + 1], self.X[:, fc, t0:t0 + n],
                ALU.mult, ALU.add), (pyb, self.MODB, self.XB), (self.XB,))


def needed_weights(nlayers):
    if nlayers == 0:
        return ()
    if nlayers == 1:
        return ("w_mod", "conv_w_pw1", "conv_w_pw2", "mlp_w1", "mlp_w2")
    return ("w_mod", "conv_w_pw1", "conv_w_pw2", "dn_w_in", "dn_w_o", "mlp_w1", "mlp_w2")


def make_inputs(inp, nlayers=DEPTH):
    maps = []
    shared = {k: np.ascontiguousarray(np.asarray(inp[k], np.float32)) for k in needed_weights(nlayers)}
    cols = None
    for core in range(NCORES):
        b, hf = core // 2, core % 2
        cp = pack_consts(inp, b, hf)
        cols = cp.cols
        xc = np.asarray(inp["ctx"][b], np.float32)
        xl = np.asarray(inp["x"][b, NLAT * hf:NLAT * (hf + 1)], np.float32)
        if hf == 1:
            xc = xc[::-1]
            xl = xl[::-1]
        xt = np.concatenate([xc, xl], axis=0)
        m = dict(shared)
        if nlayers >= 2:
            m["wab"] = make_wab(inp, hf)
        m["xT"] = np.ascontiguousarray(xt.T)
        m["consts"] = cp.array()
        maps.append(m)
    return maps, cols, maps[0]["consts"].shape[1]


def run(inp, nlayers=DEPTH):
    maps, cols, ncols = make_inputs(inp, nlayers)
    bld = Builder(nlayers, cols, ncols)
    nc = bld.build()
    res = run_bass_kernel_spmd(nc, maps, core_ids=list(range(NCORES)))
    out = np.zeros((4, SEQ, D), np.float32)
    for core in range(NCORES):
        b, hf = core // 2, core % 2
        o = res.results[core]["outT"].T
        out[b, NLAT * hf:NLAT * (hf + 1)] = o[::-1] if hf == 1 else o
    return out


def kernel(**inputs):
    inp = {k: np.asarray(v) for k, v in inputs.items()}
    return run(inp, DEPTH)
```
